# Optimizing a Trainium2 kernel written in Bass

```python
import math
import jax, jax.numpy as jnp
from jax import lax
import numpy as np

D_MODEL = 2048
BATCH = 2
SEQ = 4096
DEPTH = 4

CTX_LEN = 256
GRID_W = 64
D_MIX = D_MODEL
S5_WIDTH = D_MIX // 2
S5_GROUP = 16
S5_GROUPS = S5_WIDTH // S5_GROUP
S5_STATE = 64
DN_WIDTH = D_MIX - S5_WIDTH
DN_HEAD_DIM = 128
DN_HEADS = DN_WIDTH // DN_HEAD_DIM
DN_CONV = 5
DN_CHUNK = 64
N_IN = S5_WIDTH + 4 * DN_WIDTH + 4 * DN_HEADS
N_EXPERTS = 16
EXPERT_FF = D_MODEL // 2
EC_CAPACITY_FACTOR = 2
DEEPNORM_ALPHA = (2 * DEPTH) ** 0.25
DEEPNORM_BETA = (8 * DEPTH) ** -0.25
LN_EPS = 1e-5
RMS_EPS = 1e-6

kernel_name = "hybrid_s5_gdn_ec_dit_trunk"


def layer_norm(x, g, b):
    xf = x.astype(jnp.float32)
    mu = jnp.mean(xf, axis=-1, keepdims=True)
    var = jnp.mean(jnp.square(xf - mu), axis=-1, keepdims=True)
    return ((xf - mu) * lax.rsqrt(var + LN_EPS) * g + b).astype(x.dtype)


def raster_to_columns(t):
    b, l, ch = t.shape
    rows = l // GRID_W
    return t.reshape(b, rows, GRID_W, ch).transpose(0, 2, 1, 3).reshape(b, l, ch)


def columns_to_raster(t):
    b, l, ch = t.shape
    rows = l // GRID_W
    return t.reshape(b, GRID_W, rows, ch).transpose(0, 2, 1, 3).reshape(b, l, ch)


def _linear_recurrence_op(e1, e2):
    a1, b1 = e1
    a2, b2 = e2
    return a2 * a1, a2 * b1 + b2


def s5_scan(u, h0, lam_dt, lam_bar, b_bar, c_mat, reverse):
    n = u.shape[1]
    bu = jnp.einsum('gpc,bngc->bngp', b_bar, u.astype(jnp.complex64))
    a = jnp.broadcast_to(lam_bar[None, None], (1, n) + lam_bar.shape)
    _, h = lax.associative_scan(_linear_recurrence_op, (a, bu), axis=1, reverse=reverse)
    steps = jnp.arange(1, n + 1, dtype=jnp.float32)
    if reverse:
        steps = steps[::-1]
    carry = jnp.exp(lam_dt[None] * steps[:, None, None])
    h = h + carry[None] * h0[:, None]
    y = jnp.einsum('gcp,bngp->bngc', c_mat, h).real
    h_final = h[:, 0] if reverse else h[:, -1]
    return y, h_final


def s5_mixer(u_ctx, u_lat, lam_re, lam_im, log_step, b_re, b_im, c_re, c_im, d_skip, glu_w, glu_b):
    f32 = jnp.float32
    bsz = u_lat.shape[0]
    uc = u_ctx.astype(f32)
    ul = u_lat.astype(f32)
    grp = lambda t: t.reshape(t.shape[0], t.shape[1], S5_GROUPS, S5_GROUP)
    ys_ctx, ys_lat = [], []
    for d in range(2):
        lam = lax.complex(lam_re[d].astype(f32), lam_im[d].astype(f32))
        lam_dt = lam * jnp.exp(log_step[d].astype(f32))[:, None]
        lam_bar = jnp.exp(lam_dt)
        b_bar = ((lam_bar - 1.0) / lam)[..., None] * lax.complex(b_re[d].astype(f32), b_im[d].astype(f32))
        c_mat = lax.complex(c_re[d].astype(f32), c_im[d].astype(f32))
        h0 = jnp.zeros((bsz, S5_GROUPS, S5_STATE), jnp.complex64)
        y_c, h_c = s5_scan(grp(uc), h0, lam_dt, lam_bar, b_bar, c_mat, reverse=(d == 1))
        y_l, _ = s5_scan(grp(ul), h_c, lam_dt, lam_bar, b_bar, c_mat, reverse=(d == 1))
        ys_ctx.append(y_c)
        ys_lat.append(y_l)

    def finish(y_f, y_b, u):
        y = (y_f + y_b).reshape(u.shape) + d_skip * u
        g = jax.nn.gelu(y)
        return g * jax.nn.sigmoid(g @ glu_w + glu_b)

    return (finish(ys_ctx[0], ys_ctx[1], uc).astype(u_ctx.dtype),
            finish(ys_lat[0], ys_lat[1], ul).astype(u_lat.dtype))


def short_conv(t, w):
    ch = t.shape[-1]
    y = lax.conv_general_dilated(t, w[:, None, :].astype(t.dtype), window_strides=(1,),
                                 padding=[(DN_CONV // 2, DN_CONV // 2)],
                                 dimension_numbers=('NWC', 'WIO', 'NWC'), feature_group_count=ch)
    return jax.nn.silu(y)


def l2norm(t):
    return t * lax.rsqrt(jnp.sum(t * t, axis=-1, keepdims=True) + RMS_EPS)


def gated_delta_rule(q, k, v, g, beta, s0):
    b, n, h, dk = q.shape
    dv = v.shape[-1]
    nc = n // DN_CHUNK
    cs = DN_CHUNK
    chunks = lambda t: t.reshape((b, nc, cs, h) + t.shape[3:]).swapaxes(2, 3)
    qc, kc, vc, gc, bc = chunks(q), chunks(k), chunks(v), chunks(g), chunks(beta)
    gcum = jnp.cumsum(gc, axis=-1)
    pos = jnp.arange(cs)
    incl = pos[:, None] >= pos[None, :]
    strict = pos[:, None] > pos[None, :]
    decay = jnp.exp(jnp.where(incl, gcum[..., :, None] - gcum[..., None, :], -jnp.inf))
    kb = kc * bc[..., None]
    a_mat = jnp.where(strict, jnp.einsum('bnhid,bnhjd->bnhij', kb, kc) * decay, 0.0)
    eye = jnp.eye(cs, dtype=q.dtype)
    rhs = jnp.concatenate([vc * bc[..., None], kb * jnp.exp(gcum)[..., None]], axis=-1)
    sol = lax.linalg.triangular_solve(eye + a_mat, rhs, left_side=True, lower=True)
    u_c, w_c = sol[..., :dv], sol[..., dv:]
    qk = jnp.where(incl, jnp.einsum('bnhid,bnhjd->bnhij', qc, kc) * decay, 0.0)
    q_dec = qc * jnp.exp(gcum)[..., None]
    k_dec = kc * jnp.exp(gcum[..., -1:] - gcum)[..., None]
    g_last = jnp.exp(gcum[..., -1])

    def step(s, xs):
        u_i, w_i, q_i, k_i, qk_i, gl_i = xs
        v_new = u_i - jnp.einsum('bhcd,bhde->bhce', w_i, s)
        o = jnp.einsum('bhcd,bhde->bhce', q_i, s) + jnp.einsum('bhij,bhje->bhie', qk_i, v_new)
        s = s * gl_i[..., None, None] + jnp.einsum('bhcd,bhce->bhde', k_i, v_new)
        return s, o

    xs = tuple(t.swapaxes(0, 1) for t in (u_c, w_c, q_dec, k_dec, qk, g_last))
    s_fin, o = lax.scan(step, s0, xs)
    o = o.transpose(1, 0, 3, 2, 4).reshape(b, n, h, dv)
    return o, s_fin


def deltanet_mixer(p_ctx, p_lat, conv_w, a_log, dt_bias, norm_w):
    f32 = jnp.float32

    def prep(p):
        b, n, _ = p.shape
        p = p.astype(f32)
        qkv = short_conv(p[..., :3 * DN_WIDTH], conv_w)
        q, k, v = [t.reshape(b, n, DN_HEADS, DN_HEAD_DIM) for t in jnp.split(qkv, 3, axis=-1)]
        q = l2norm(q) * DN_HEAD_DIM ** -0.5
        k = l2norm(k)
        z = p[..., 3 * DN_WIDTH:4 * DN_WIDTH].reshape(b, n, DN_HEADS, DN_HEAD_DIM)
        ab = p[..., 4 * DN_WIDTH:].reshape(b, n, 2, 2, DN_HEADS)
        return q, k, v, z, ab

    streams = [prep(p_ctx), prep(p_lat)]
    bsz = p_lat.shape[0]
    outs = [[], []]
    for d in range(2):
        s = jnp.zeros((bsz, DN_HEADS, DN_HEAD_DIM, DN_HEAD_DIM), f32)
        for i, (q, k, v, z, ab) in enumerate(streams):
            g = -jnp.exp(a_log[d].astype(f32)) * jax.nn.softplus(ab[:, :, d, 0] + dt_bias[d].astype(f32))
            beta = jax.nn.sigmoid(ab[:, :, d, 1])
            if d == 1:
                q, k, v, g, beta = [jnp.flip(t, axis=1) for t in (q, k, v, g, beta)]
            o, s = gated_delta_rule(q, k, v, g, beta, s)
            if d == 1:
                o = jnp.flip(o, axis=1)
            outs[i].append(o)

    def finish(o, z):
        o = o * lax.rsqrt(jnp.mean(o * o, axis=-1, keepdims=True) + RMS_EPS) * norm_w
        o = o * jax.nn.silu(z)
        return o.reshape(o.shape[0], o.shape[1], DN_WIDTH)

    return (finish(outs[0][0] + outs[0][1], streams[0][3]).astype(p_ctx.dtype),
            finish(outs[1][0] + outs[1][1], streams[1][3]).astype(p_lat.dtype))


def expert_choice_ffn(h, router_w, w1, w3, w2):
    b, n, d = h.shape
    cap = EC_CAPACITY_FACTOR * n // N_EXPERTS
    aff = jax.nn.softmax((h @ router_w).astype(jnp.float32), axis=-1)
    gate, idx = lax.top_k(aff.swapaxes(1, 2), cap)
    sel = jax.vmap(lambda hb, ib: hb[ib])(h, idx)
    hid = jax.nn.silu(jnp.einsum('becd,edf->becf', sel, w1)) * jnp.einsum('becd,edf->becf', sel, w3)
    y = jnp.einsum('becf,efd->becd', hid, w2) * gate[..., None].astype(h.dtype)
    return jax.vmap(lambda yb, ib: jnp.zeros((n, d), yb.dtype).at[ib.reshape(-1)].add(yb.reshape(-1, d)))(y, idx)


def setup_inputs(seed: int = 0) -> dict:
    key = jax.random.key(seed)
    ks = iter(jax.random.split(key, 40))
    f32 = jnp.float32
    nrm = lambda shape, scale: jax.random.normal(next(ks), shape, f32) * scale
    L, G, P = DEPTH, S5_GROUPS, S5_STATE
    x = nrm((BATCH, SEQ, D_MODEL), 1.0)
    c = nrm((BATCH, D_MODEL), 1.0)
    ctx = nrm((BATCH, CTX_LEN, D_MODEL), 1.0)
    c_ctx = nrm((D_MODEL,), 1.0)
    ada_w = nrm((L, D_MODEL, 6 * D_MODEL), 0.5 * D_MODEL ** -0.5)
    ada_b = nrm((L, 6 * D_MODEL), 0.02)
    w_in = nrm((L, D_MODEL, N_IN), D_MODEL ** -0.5)
    w_out = nrm((L, D_MIX, D_MODEL), D_MIX ** -0.5 * DEEPNORM_BETA)
    s5_lam_re = -0.5 + nrm((L, 2, G, P), 0.01)
    s5_lam_im = math.pi * jnp.arange(P, dtype=f32) + nrm((L, 2, G, P), 0.01)
    s5_log_step = jax.random.uniform(next(ks), (L, 2, G), f32, math.log(1e-3), math.log(1e-1))
    s5_b_re = nrm((L, 2, G, P, S5_GROUP), (2 * S5_GROUP) ** -0.5)
    s5_b_im = nrm((L, 2, G, P, S5_GROUP), (2 * S5_GROUP) ** -0.5)
    s5_c_re = nrm((L, 2, G, S5_GROUP, P), (2 * P) ** -0.5)
    s5_c_im = nrm((L, 2, G, S5_GROUP, P), (2 * P) ** -0.5)
    s5_d = nrm((L, S5_WIDTH), 1.0)
    s5_glu_w = nrm((L, S5_WIDTH, S5_WIDTH), S5_WIDTH ** -0.5)
    s5_glu_b = nrm((L, S5_WIDTH), 0.02)
    dn_conv_w = nrm((L, DN_CONV, 3 * DN_WIDTH), DN_CONV ** -0.5)
    dn_a_log = jnp.log(jax.random.uniform(next(ks), (L, 2, DN_HEADS), f32, 1.0, 16.0))
    dt = jnp.exp(jax.random.uniform(next(ks), (L, 2, DN_HEADS), f32, math.log(1e-3), math.log(1e-1)))
    dn_dt_bias = dt + jnp.log(-jnp.expm1(-dt))
    dn_norm_w = 1.0 + nrm((L, DN_HEAD_DIM), 0.02)
    ln1_g = 1.0 + nrm((L, D_MODEL), 0.02)
    ln1_b = nrm((L, D_MODEL), 0.02)
    ln2_g = 1.0 + nrm((L, D_MODEL), 0.02)
    ln2_b = nrm((L, D_MODEL), 0.02)
    router_w = nrm((L, D_MODEL, N_EXPERTS), D_MODEL ** -0.5)
    exp_w1 = nrm((L, N_EXPERTS, D_MODEL, EXPERT_FF), D_MODEL ** -0.5)
    exp_w3 = nrm((L, N_EXPERTS, D_MODEL, EXPERT_FF), D_MODEL ** -0.5)
    exp_w2 = nrm((L, N_EXPERTS, EXPERT_FF, D_MODEL), EXPERT_FF ** -0.5 * DEEPNORM_BETA)
    return {"x": x, "c": c, "ctx": ctx, "c_ctx": c_ctx, "ada_w": ada_w, "ada_b": ada_b,
            "w_in": w_in, "w_out": w_out, "s5_lam_re": s5_lam_re, "s5_lam_im": s5_lam_im,
            "s5_log_step": s5_log_step, "s5_b_re": s5_b_re, "s5_b_im": s5_b_im,
            "s5_c_re": s5_c_re, "s5_c_im": s5_c_im, "s5_d": s5_d, "s5_glu_w": s5_glu_w,
            "s5_glu_b": s5_glu_b, "dn_conv_w": dn_conv_w, "dn_a_log": dn_a_log,
            "dn_dt_bias": dn_dt_bias, "dn_norm_w": dn_norm_w, "ln1_g": ln1_g, "ln1_b": ln1_b,
            "ln2_g": ln2_g, "ln2_b": ln2_b, "router_w": router_w, "exp_w1": exp_w1,
            "exp_w3": exp_w3, "exp_w2": exp_w2}


def reference(x, c, ctx, c_ctx, ada_w, ada_b, w_in, w_out, s5_lam_re, s5_lam_im, s5_log_step,
              s5_b_re, s5_b_im, s5_c_re, s5_c_im, s5_d, s5_glu_w, s5_glu_b, dn_conv_w, dn_a_log,
              dn_dt_bias, dn_norm_w, ln1_g, ln1_b, ln2_g, ln2_b, router_w, exp_w1, exp_w3, exp_w2):
    n_ctx = ctx.shape[1]
    ctx_h = ctx
    for l in range(DEPTH):
        last = l == DEPTH - 1
        mod = jax.nn.silu(c) @ ada_w[l] + ada_b[l]
        mod_c = jax.nn.silu(c_ctx) @ ada_w[l] + ada_b[l]
        sh1, sc1, g1, sh2, sc2, g2 = jnp.split(mod[:, None, :], 6, axis=-1)
        csh1, csc1, cg1, csh2, csc2, cg2 = jnp.split(mod_c, 6)

        hin = jnp.concatenate([ctx_h * (1 + csc1) + csh1, x * (1 + sc1) + sh1], axis=1)
        proj = hin @ w_in[l]
        p_ctx, p_lat = proj[:, :n_ctx], proj[:, n_ctx:]
        s5_c, s5_l = s5_mixer(p_ctx[..., :S5_WIDTH], p_lat[..., :S5_WIDTH], s5_lam_re[l], s5_lam_im[l],
                              s5_log_step[l], s5_b_re[l], s5_b_im[l], s5_c_re[l], s5_c_im[l],
                              s5_d[l], s5_glu_w[l], s5_glu_b[l])
        dn_c, dn_l = deltanet_mixer(p_ctx[..., S5_WIDTH:], raster_to_columns(p_lat[..., S5_WIDTH:]),
                                    dn_conv_w[l], dn_a_log[l], dn_dt_bias[l], dn_norm_w[l])
        dn_l = columns_to_raster(dn_l)
        mix_lat = jnp.concatenate([s5_l, dn_l], axis=-1) @ w_out[l]
        x = layer_norm(DEEPNORM_ALPHA * x + g1 * mix_lat, ln1_g[l], ln1_b[l])
        if not last:
            mix_ctx = jnp.concatenate([s5_c, dn_c], axis=-1) @ w_out[l]
            ctx_h = layer_norm(DEEPNORM_ALPHA * ctx_h + cg1 * mix_ctx, ln1_g[l], ln1_b[l])

        ff_lat = expert_choice_ffn(x * (1 + sc2) + sh2, router_w[l], exp_w1[l], exp_w3[l], exp_w2[l])
        x = layer_norm(DEEPNORM_ALPHA * x + g2 * ff_lat, ln2_g[l], ln2_b[l])
        if not last:
            ff_ctx = expert_choice_ffn(ctx_h * (1 + csc2) + csh2, router_w[l], exp_w1[l], exp_w3[l], exp_w2[l])
            ctx_h = layer_norm(DEEPNORM_ALPHA * ctx_h + cg2 * ff_ctx, ln2_g[l], ln2_b[l])
    return x
```

```python
import numpy as np
from contextlib import ExitStack, contextmanager
import concourse.bass as bass
import concourse.mybir as mybir
from concourse.bass_utils import run_bass_kernel_spmd

F32 = mybir.dt.float32
BF16 = mybir.dt.bfloat16
ALU = mybir.AluOpType
AF = mybir.ActivationFunctionType
AX = mybir.AxisListType

D = 2048
NTOK = 4352
NT = 34
NCTX = 256
NLAT = 4096
DEPTH = 4
ALPHA = (2 * DEPTH) ** 0.25
NIN = 5152
R = 4
UC = 1024 // R
NGP = 32 // R
NH = 8 // R
NEL = 16 // R
DC = D // R
ABC = 4 * NH
NINL = 5 * UC + ABC
RG = [[0, 1, 2, 3], [4, 5, 6, 7]]
CONST_NAMES = ["ident", "J", "ones", "tri_f", "tri_b", "ms_f", "ms_b", "mi_f", "mi_b", "blk_a", "blk_b", "blk_d", "lt"]


def make_consts():
    i = np.arange(128)
    s, j = np.meshgrid(i, i, indexing="ij")
    same = (s // 64) == (j // 64)
    c = {}
    c["ident"] = (s == j)
    c["J"] = (s + j == 127)
    c["ones"] = np.ones((128, 128), bool)
    c["tri_f"] = same & (s <= j)
    c["tri_b"] = same & (s >= j)
    c["ms_f"] = same & (s < j)
    c["ms_b"] = same & (s > j)
    c["mi_f"] = same & (s <= j)
    c["mi_b"] = same & (s >= j)
    c["blk_a"] = (s < 64)
    c["blk_b"] = (s >= 64)
    c["blk_d"] = same
    c["lt"] = (j < s)
    return np.concatenate([c[n].astype(np.float32) for n in CONST_NAMES], axis=1)


class T:
    __slots__ = ("h", "lw", "rd", "name")

    def __init__(self, h, name=""):
        self.h = h
        self.lw = None
        self.rd = []
        self.name = name

    def __getitem__(self, k):
        return self.h[k]


class Prog:
    NDMA = 8

    def __init__(self, nc, es):
        self.nc = nc
        self.stack = [es]
        self.eng = {"pe": nc.tensor, "act": nc.scalar, "dve": nc.vector, "pool": nc.gpsimd, "sp": nc.sync}
        self.sem = {}
        self.cnt = {}
        for e in ("pe", "act", "dve", "pool"):
            self.sem[e] = es.enter_context(nc.semaphore("s_" + e))
            self.cnt[e] = 0
        self.dq = {}
        for q in ("sp", "act", "pool"):
            sems = []
            for i in range(self.NDMA):
                k = "d_%s_%d" % (q, i)
                self.sem[k] = es.enter_context(nc.semaphore(k))
                self.cnt[k] = 0
                sems.append(k)
            self.dq[q] = [sems, 0]
        self.known = {e: {} for e in self.eng}
        self.ninst = 0
        self.uid = 0

    @contextmanager
    def scope(self):
        es = ExitStack()
        self.stack.append(es)
        try:
            yield
        finally:
            self.barrier()
            self.stack.pop()
            es.close()

    def _nm(self, name):
        self.uid += 1
        return "%s_%d" % (name, self.uid)

    def sb(self, name, shape, dt=F32):
        return T(self.stack[-1].enter_context(self.nc.sbuf_tensor(self._nm(name), list(shape), dt)), name)

    def ps(self, name, shape, dt=F32):
        return T(self.stack[-1].enter_context(self.nc.psum_tensor(self._nm(name), list(shape), dt)), name)

    def dram(self, name, shape, dt=F32, kind="Internal"):
        return T(self.nc.dram_tensor(name, list(shape), dt, kind=kind), name)

    def _wait(self, e, dep):
        if dep is None:
            return
        k, v = dep
        if v <= 0 or self.known[e].get(k, 0) >= v:
            return
        self.eng[e].wait_ge(self.sem[k], v)
        self.known[e][k] = v

    def _deps(self, e, reads, writes):
        for t in reads:
            if t.lw is not None and not (e == "pe" and t.lw[0] == "pe"):
                self._wait(e, t.lw)
        for t in writes:
            if t.lw is not None and not (e == "pe" and t.lw[0] == "pe"):
                self._wait(e, t.lw)
            for d in t.rd:
                if not (e == "pe" and d[0] == "pe"):
                    self._wait(e, d)

    def _mark(self, key, val, reads, writes):
        for t in reads:
            t.rd.append((key, val))
            if len(t.rd) > 32:
                m = {}
                for k, v in t.rd:
                    if m.get(k, 0) < v:
                        m[k] = v
                t.rd = list(m.items())
        for t in writes:
            t.lw = (key, val)
            t.rd = []

    def op(self, e, fn, reads=(), writes=()):
        self._deps(e, reads, writes)
        ins = fn(self.eng[e])
        self.cnt[e] += 1
        ins.then_inc(self.sem[e], 1)
        self._mark(e, self.cnt[e], reads, writes)
        self.ninst += 1
        return ins

    def dma(self, q, out, in_, reads=(), writes=(), **kw):
        sems, i = self.dq[q]
        k = sems[i % self.NDMA]
        self.dq[q][1] = i + 1
        self._wait(q, (k, self.cnt[k]))
        self._deps(q, reads, writes)
        ins = self.eng[q].dma_start(out=out, in_=in_, **kw)
        self.cnt[k] += 16
        ins.then_inc(self.sem[k], 16)
        self._mark(k, self.cnt[k], reads, writes)
        self.ninst += 1
        return ins

    def coll(self, kind, op, rg, in_ap, out_ap, reads, writes):
        q = "pool"
        k = "cc"
        if k not in self.sem:
            self.sem[k] = self.stack[0].enter_context(self.nc.semaphore("s_cc"))
            self.cnt[k] = 0
        self._wait(q, (k, self.cnt[k]))
        self._deps(q, reads, writes)
        ins = self.nc.gpsimd.collective_compute(kind, op, replica_groups=rg, ins=[in_ap], outs=[out_ap])
        self.cnt[k] += 1
        ins.then_inc(self.sem[k], 1)
        self._mark(k, self.cnt[k], reads, writes)
        self.ninst += 1
        return ins

    def barrier(self):
        for e in self.eng:
            for k in self.sem:
                self._wait(e, (k, self.cnt[k]))

    def mm(self, out_ap, lhsT_ap, rhs_ap, start, stop, reads, writes):
        return self.op("pe", lambda e: e.matmul(out_ap, lhsT=lhsT_ap, rhs=rhs_ap, start=start, stop=stop),
                       reads=reads, writes=writes)

    def act(self, out_ap, in_ap, func, reads, writes, **kw):
        return self.op("act", lambda e: e.activation(out=out_ap, in_=in_ap, func=func, **kw), reads=reads, writes=writes)

    def tt(self, e, out_ap, a_ap, b_ap, op, reads, writes):
        return self.op(e, lambda g: g.tensor_tensor(out=out_ap, in0=a_ap, in1=b_ap, op=op), reads=reads, writes=writes)

    def ts(self, e, out_ap, a_ap, s1, s2, op0, op1, reads, writes, accum=None):
        if op1 is None:
            return self.op(e, lambda g: g.tensor_scalar(out=out_ap, in0=a_ap, scalar1=s1, scalar2=None, op0=op0),
                           reads=reads, writes=writes)
        if accum is not None:
            return self.op(e, lambda g: g.tensor_scalar(out=out_ap, in0=a_ap, scalar1=s1, scalar2=s2, op0=op0, op1=op1,
                                                        accum_out=accum), reads=reads, writes=writes)
        return self.op(e, lambda g: g.tensor_scalar(out=out_ap, in0=a_ap, scalar1=s1, scalar2=s2, op0=op0, op1=op1),
                       reads=reads, writes=writes)

    def stt(self, e, out_ap, a_ap, s, b_ap, op0, op1, reads, writes):
        return self.op(e, lambda g: g.scalar_tensor_tensor(out=out_ap, in0=a_ap, scalar=s, in1=b_ap, op0=op0, op1=op1),
                       reads=reads, writes=writes)


RT = [1, 0] + [35 - j for j in range(2, NT)]
RTINV = {t: j for j, t in enumerate(RT)}


class Net:
    def __init__(self, nlayers=DEPTH, dump=None, upto=None, L=DEPTH, EW=NEL):
        self.L = L
        self.EW = EW
        self.nlayers = nlayers
        self.dump = dump or []
        self.upto = upto

    def build(self):
        nc = bass.Bass("TRN2", target_bir_lowering=False)
        self.nc = nc
        with ExitStack() as es:
            P = Prog(nc, es)
            self.P = P
            self.declare()
            self.body()
            P.barrier()
        return nc

    def declare(self):
        P = self.P
        L = self.L
        ext = lambda n, s: P.dram(n, s, F32, kind="ExternalInput")
        self.xin = ext("xin", [NTOK, D])
        self.cT = ext("cT", [128, 32])
        self.consts = ext("consts", [128, 128 * len(CONST_NAMES)])
        self.iotar = ext("iotar", [128, 544])
        self.iotac = ext("iotac", [128, 8])
        self.ada_w = ext("ada_w", [L, D, 6 * DC])
        self.ada_b = ext("ada_b", [L, 6 * DC])
        self.w_in = ext("w_in", [L, D, NINL])
        self.w_out = ext("w_out", [L, D, DC])
        self.lam_re = ext("s5_lam_re", [L, 2, 2 * NGP, 64])
        self.lam_im = ext("s5_lam_im", [L, 2, 2 * NGP, 64])
        self.log_step = ext("s5_log_step", [L, 2, 2 * NGP])
        self.b_re = ext("s5_b_re", [L, 2, 2 * NGP, 64, 16])
        self.b_im = ext("s5_b_im", [L, 2, 2 * NGP, 64, 16])
        self.c_re = ext("s5_c_re", [L, 2, 2 * NGP, 16, 64])
        self.c_im = ext("s5_c_im", [L, 2, 2 * NGP, 16, 64])
        self.s5_d = ext("s5_d", [L, UC])
        self.glu_w = ext("s5_glu_w", [L, 1024, UC])
        self.glu_b = ext("s5_glu_b", [L, UC])
        self.conv_w = ext("dn_conv_w", [L, 5, 3 * UC])
        self.a_log = ext("dn_a_log", [L, 2 * NH])
        self.dt_bias = ext("dn_dt_bias", [L, 2 * NH])
        self.norm_w = ext("dn_norm_w", [L, 128])
        self.ln1_g = ext("ln1_g", [L, D]); self.ln1_b = ext("ln1_b", [L, D])
        self.ln2_g = ext("ln2_g", [L, D]); self.ln2_b = ext("ln2_b", [L, D])
        self.router_w = ext("router_w", [L, D, 16])
        self.w1 = ext("exp_w1", [L, self.EW, D, 1024])
        self.w3 = ext("exp_w3", [L, self.EW, D, 1024])
        self.w2 = ext("exp_w2", [L, self.EW, 1024, D])
        self.yout = P.dram("yout", [NLAT, D], F32, kind="ExternalOutput")
        self.X = P.dram("X", [NTOK, D])
        self.MODL = P.dram("MODL", [L * 2, 6 * DC]); self.MOD = P.dram("MODF", [R * L * 2, 6 * DC])
        self.PT = P.dram("PT", [4 * UC, NTOK])
        self.UR = P.dram("UR", [UC, NTOK])
        self.Z = P.dram("Z", [NTOK, UC])
        self.AB = P.dram("AB", [NTOK, ABC])
        self.GT = P.dram("GT", [UC, NTOK]); self.GTF = P.dram("GTF", [NGP, R * 32, NTOK])
        self.S5TF = P.dram("S5TF", [UC // 64, R * 64, NTOK], BF16); self.DNOF = P.dram("DNOF", [R * NTOK, UC])
        self.MIXL = P.dram("MIXL", [NTOK, DC]); self.MIXF = P.dram("MIXF", [R * NTOK, DC]); self.FFP = P.dram("FFP", [NTOK, D])
        self.H2 = P.dram("H2", [NTOK, D], BF16); self.AFF = P.dram("AFF", [NTOK, 16]); self.AFFT = P.dram("AFFT", [16, NTOK])
        self.RANKD = P.dram("RANKD", [NTOK, 16]); self.RANKT = P.dram("RANKT", [16, NTOK])
        self.YG = P.dram("YG", [NEL, 544, D], BF16); self.FF = P.dram("FF", [NTOK, D])
        self.S5T = P.dram("S5T", [UC, NTOK], BF16)
        self.QT = P.dram("QT", [UC, NTOK]); self.KT = P.dram("KT", [UC, NTOK])
        self.KTOK = P.dram("KTOK", [NTOK, UC]); self.VTOK = P.dram("VTOK", [NTOK, UC])
        self.GB = P.dram("GB", [NTOK, 4 * NH]); self.OD = P.dram("OD", [2, NTOK, UC]); self.DNO = P.dram("DNO", [NTOK, UC])
        self.dbg = {}
        for n, shp in self.dump:
            self.dbg[n] = P.dram("dbg_" + n, shp, F32, kind="ExternalOutput")

    def cst(self, name):
        i = CONST_NAMES.index(name)
        return self.C[:, i * 128:(i + 1) * 128]

    def body(self):
        P = self.P
        with P.scope():
            self.C = P.sb("C", [128, 128 * len(CONST_NAMES)])
            P.dma("sp", self.C[:], self.consts[:], reads=[self.consts], writes=[self.C])
            self.Cb = P.sb("Cb", [128, 256], BF16)
            P.op("dve", lambda e: e.tensor_copy(out=self.Cb[:], in_=self.C[:, 0:256]), reads=[self.C], writes=[self.Cb])
            self.iota = P.sb("iota", [128, 544])
            P.dma("sp", self.iota[:], self.iotar[:], reads=[self.iotar], writes=[self.iota])
            self.iotc = P.sb("iotc", [128, 8])
            P.dma("sp", self.iotc[:], self.iotac[:], reads=[self.iotac], writes=[self.iotc])
            self.copy_x()
            self.stage_mod()
            if self.upto == "mod":
                return
            for l in range(self.nlayers):
                self.stage_inproj(l)
                if self.upto == "inproj":
                    return
                self.stage_s5(l)
                if self.upto == "s5":
                    return
                self.stage_dn(l)
                if self.upto == "dn":
                    return
                self.stage_glu(l)
                self.stage_outproj(l)
                if self.upto == "outproj":
                    return
                self.stage_moe(l)
            self.write_out()

    def copy_x(self):
        P = self.P
        with P.scope():
            tb = [P.sb("cx", [128, D]) for _ in range(2)]
            for i in range(NT):
                t = tb[i % 2]
                P.dma("sp", t[:], self.xin[i * 128:(i + 1) * 128, :], reads=[self.xin], writes=[t])
                P.dma("pool", self.X[i * 128:(i + 1) * 128, :], t[:], reads=[t], writes=[self.X])

    def write_out(self):
        P = self.P
        with P.scope():
            tb = [P.sb("wo", [128, D]) for _ in range(2)]
            for i in range(2, NT):
                t = tb[i % 2]
                P.dma("sp", t[:], self.X[i * 128:(i + 1) * 128, :], reads=[self.X], writes=[t])
                P.dma("pool", self.yout[(i - 2) * 128:(i - 1) * 128, :], t[:], reads=[t], writes=[self.yout])

    def dump_dram(self, name, src_ap_fn, rows, cols, src):
        if name not in self.dbg:
            return
        P = self.P
        dst = self.dbg[name]
        with P.scope():
            tb = [P.sb("dd", [128, cols]) for _ in range(2)]
            for r0 in range(0, rows, 128):
                n = min(128, rows - r0)
                t = tb[(r0 // 128) % 2]
                P.dma("sp", t[0:n, :], src_ap_fn(r0, n), reads=[src], writes=[t])
                P.dma("sp", dst[r0:r0 + n, :], t[0:n, :], reads=[t], writes=[dst])

    def stage_mod(self):
        P = self.P
        W = 6 * DC
        with P.scope():
            ct = P.sb("ct", [128, 32])
            P.dma("sp", ct[:], self.cT[:], reads=[self.cT], writes=[ct])
            sc = P.sb("sc", [128, 32])
            P.act(sc[:], ct[:], AF.Silu, [ct], [sc])
            wb = [P.sb("adw", [128, 16, 512]) for _ in range(2)]
            pm = [P.ps("pmod", [2, 512]) for _ in range(2)]
            row = P.sb("modrow", [2, W])
            bia = P.sb("modb", [2, W])
            for l in range(self.nlayers):
                P.dma("pool", bia[:], self.ada_b[l:l + 1, :].partition_broadcast(2), reads=[self.ada_b], writes=[bia])
                for nb in range(W // 512):
                    w = wb[nb % 2]
                    p = pm[nb % 2]
                    src = self.ada_w[l].rearrange("(k p) n -> p k n", p=128)[:, :, nb * 512:(nb + 1) * 512]
                    P.dma("sp" if nb % 2 == 0 else "pool", w[:], src, reads=[self.ada_w], writes=[w])
                    for kc in range(16):
                        P.mm(p[:], sc[:, kc:32:16], w[:, kc, :], kc == 0, kc == 15, [sc, w], [p])
                    P.tt("dve", row[:, nb * 512:(nb + 1) * 512], p[:], bia[:, nb * 512:(nb + 1) * 512], ALU.add,
                         [p, bia], [row])
                for sg in (1, 4):
                    P.ts("dve", row[:, sg * DC:(sg + 1) * DC], row[:, sg * DC:(sg + 1) * DC], 1.0, None, ALU.add, None, [row], [row])
                P.dma("sp", self.MODL[l * 2:(l + 1) * 2, :], row[:], reads=[row], writes=[self.MODL])
            P.coll("AllGather", ALU.bypass, RG, self.MODL[:, :], self.MOD[:, :], [self.MODL], [self.MOD])

    def mod_load(self, q, dst, l, row, seg):
        v = self.MOD[:, :].rearrange("(r l two) (s j) -> l two s r j", r=R, l=self.L, two=2, s=6)[l][row][seg]
        self.P.dma(q, dst[:, :].rearrange("p (r j) -> p r j", r=R), v.partition_broadcast(128), reads=[self.MOD], writes=[dst])

    def bcast_load(self, q, dst, src_row_ap, src_t):
        self.P.dma(q, dst[:], src_row_ap.partition_broadcast(128), reads=[src_t], writes=[dst])

    def stage_inproj(self, l):
        P = self.P
        fwd_tiles = list(range(NT))
        passes = [(fwd_tiles[:12], False, 0), (fwd_tiles[12:24], False, 12 * 128), (fwd_tiles[24:], False, 24 * 128),
                  (RT[:12], True, 0), (RT[12:24], True, 12 * 128), (RT[24:], True, 24 * 128)]
        with P.scope():
            modt = {}
            for who, row in (("lat", 0), ("ctx", 1)):
                scp = P.sb("scp", [128, D]); sh = P.sb("sh", [128, D])
                self.mod_load("sp", scp, l, row, 1)
                self.mod_load("pool", sh, l, row, 0)
                modt[who] = (scp, sh)
            hinT = P.sb("hinT", [128, 16, 12 * 128], BF16)
            xt = [P.sb("xt", [128, D]) for _ in range(2)]
            hb = [P.sb("hb", [128, D], BF16) for _ in range(2)]
            ptr = [P.ps("ptr", [128, 512]) for _ in range(2)]
            pacc = [P.ps("pacc", [128, 512]) for _ in range(3)]
            wst = [P.sb("wst", [128, 16, 128]) for _ in range(2)]
            wbf = [P.sb("wbf", [128, 16, 128], BF16) for _ in range(2)]
            wst2 = P.sb("wst2", [128, 16, 256])
            wbf2 = P.sb("wbf2", [128, 16, 256], BF16)
            ot = [P.sb("ot", [128, 12 * 128]) for _ in range(2)]
            ot2 = [P.sb("ot2", [128, 512]) for _ in range(2)]
            win = self.w_in[l].rearrange("(k p) n -> p k n", p=128)
            for tiles, rev, col0 in passes:
                for ti, t in enumerate(tiles):
                    x = xt[ti % 2]; h = hb[ti % 2]
                    scp, sh = modt["ctx" if t < 2 else "lat"]
                    P.dma("sp" if ti % 2 == 0 else "pool", x[:], self.X[t * 128:(t + 1) * 128, :], reads=[self.X], writes=[x])
                    P.tt("dve", x[:], x[:], scp[:], ALU.mult, [x, scp], [x])
                    P.tt("pool", h[:], x[:], sh[:], ALU.add, [x, sh], [h])
                    idm = self.Cb[:, 128:256] if rev else self.Cb[:, 0:128]
                    for kg in range(4):
                        pt = ptr[kg % 2]
                        for kk in range(4):
                            kc = kg * 4 + kk
                            P.mm(pt[:, kk * 128:(kk + 1) * 128], h[:, kc * 128:(kc + 1) * 128], idm, True, True, [h, self.Cb], [pt])
                        P.act(hinT[:, kg * 4:(kg + 1) * 4, ti * 128:(ti + 1) * 128],
                              pt[:].rearrange("p (k t) -> p k t", k=4), AF.Copy, [pt], [hinT])
                ntk = len(tiles) * 128
                noc = (UC // 128) if rev else (4 * UC // 128)
                dst = self.UR if rev else self.PT
                for oc in range(noc):
                    ws = wst[oc % 2]; wb = wbf[oc % 2]; o = ot[oc % 2]
                    P.dma("sp" if oc % 2 == 0 else "pool", ws[:], win[:, :, oc * 128:(oc + 1) * 128], reads=[self.w_in], writes=[ws])
                    P.op("pool", lambda e, wb=wb, ws=ws: e.tensor_copy(out=wb[:], in_=ws[:]), reads=[ws], writes=[wb])
                    for tb in range(0, ntk, 512):
                        n = min(512, ntk - tb)
                        pa = pacc[(tb // 512) % 3]
                        for kc in range(16):
                            P.mm(pa[:, 0:n], wb[:, kc, :], hinT[:, kc, tb:tb + n], kc == 0, kc == 15, [wb, hinT], [pa])
                        P.act(o[:, tb:tb + n], pa[:, 0:n], AF.Copy, [pa], [o])
                    P.dma("sp", dst[oc * 128:(oc + 1) * 128, col0:col0 + ntk], o[:, 0:ntk], reads=[o], writes=[dst])
                if rev:
                    continue
                nzb = UC // 256
                for zb in range(nzb + 1):
                    c0 = 4 * UC + zb * 256
                    ncol = 256 if zb < nzb else ABC
                    P.dma("sp", wst2[:, :, 0:ncol], win[:, :, c0:c0 + ncol], reads=[self.w_in], writes=[wst2])
                    P.op("pool", lambda e, ncol=ncol: e.tensor_copy(out=wbf2[:, :, 0:ncol], in_=wst2[:, :, 0:ncol]), reads=[wst2], writes=[wbf2])
                    for ti, t in enumerate(tiles):
                        pa = pacc[ti % 3]; o2 = ot2[ti % 2]
                        for kc in range(16):
                            P.mm(pa[:, 0:ncol], hinT[:, kc, ti * 128:(ti + 1) * 128], wbf2[:, kc, 0:ncol], kc == 0, kc == 15, [wbf2, hinT], [pa])
                        P.act(o2[:, 0:ncol], pa[:, 0:ncol], AF.Copy, [pa], [o2])
                        if zb < nzb:
                            P.dma("pool", self.Z[t * 128:(t + 1) * 128, zb * 256:(zb + 1) * 256], o2[:, 0:256], reads=[o2], writes=[self.Z])
                        else:
                            P.dma("pool", self.AB[t * 128:(t + 1) * 128, :], o2[:, 0:ABC], reads=[o2], writes=[self.AB])


def prep_inputs(inputs, b, r=0, L=DEPTH, EW=None):
    f = lambda a: np.ascontiguousarray(np.asarray(a, dtype=np.float32))
    g = lambda k: np.asarray(inputs[k])[:L]
    m = {}
    m["xin"] = f(np.concatenate([inputs["ctx"][b], inputs["x"][b]], axis=0))
    cT = np.concatenate([np.asarray(inputs["c"][b]).reshape(16, 128).T, np.asarray(inputs["c_ctx"]).reshape(16, 128).T], axis=1)
    m["cT"] = f(cT)
    m["consts"] = make_consts()
    m["iotar"] = f(np.tile(np.arange(544, dtype=np.float32)[None, :], (128, 1)))
    m["iotac"] = f(np.arange(128, dtype=np.float32)[:, None] + 128.0 * np.arange(8, dtype=np.float32)[None, :])
    for k in ["dn_norm_w", "ln1_g", "ln1_b", "ln2_g", "ln2_b"]:
        m[k] = f(g(k))
    aw = g("ada_w"); ab_ = g("ada_b")
    m["ada_w"] = f(np.concatenate([aw[:, :, sg * D + r * DC:sg * D + (r + 1) * DC] for sg in range(6)], axis=2))
    m["ada_b"] = f(np.concatenate([ab_[:, sg * D + r * DC:sg * D + (r + 1) * DC] for sg in range(6)], axis=1))
    cu = slice(r * UC, (r + 1) * UC)
    heads = list(range(r * NH, (r + 1) * NH))
    abcols = [5120 + d * 16 + k * 8 + h for d in range(2) for k in range(2) for h in heads]
    w_in = g("w_in")
    m["w_in"] = f(np.concatenate([w_in[:, :, j * 1024 + r * UC:j * 1024 + (r + 1) * UC] for j in range(5)] + [w_in[:, :, abcols]], axis=2))
    m["w_out"] = f(g("w_out")[:, :, r * DC:(r + 1) * DC])
    gs = slice(r * 2 * NGP, (r + 1) * 2 * NGP)
    for k in ["s5_lam_re", "s5_lam_im", "s5_log_step", "s5_b_re", "s5_b_im", "s5_c_re", "s5_c_im"]:
        m[k] = f(g(k)[:, :, gs])
    m["s5_d"] = f(g("s5_d")[:, cu])
    m["s5_glu_w"] = f(g("s5_glu_w")[:, :, cu])
    m["s5_glu_b"] = f(g("s5_glu_b")[:, cu])
    cw = g("dn_conv_w")
    m["dn_conv_w"] = f(np.concatenate([cw[:, :, j * 1024 + r * UC:j * 1024 + (r + 1) * UC] for j in range(3)], axis=2))
    m["dn_a_log"] = f(g("dn_a_log")[:, :, heads].reshape(L, 2 * NH))
    m["dn_dt_bias"] = f(g("dn_dt_bias")[:, :, heads].reshape(L, 2 * NH))
    eo = list(range(r * NEL, (r + 1) * NEL)) + [e for e in range(16) if not (r * NEL <= e < (r + 1) * NEL)]
    m["router_w"] = f(g("router_w")[:, :, eo])
    ne = NEL if EW is None else EW
    for k in ["exp_w1", "exp_w3", "exp_w2"]:
        m[k] = f(g(k)[:, r * NEL:r * NEL + ne])
    return m


def kernel(**inputs):
    net = Net()
    nc = net.build()
    in_maps = [prep_inputs(inputs, c // R, c % R) for c in range(2 * R)]
    res = run_bass_kernel_spmd(nc, in_maps, core_ids=list(range(2 * R)))
    return np.stack([np.asarray(res.results[b * R]["yout"], dtype=np.float32) for b in range(2)], axis=0)


MAGIC = 12582912.0
TWO_PI = 6.283185307179586


def _s5_prep(self, l):
    P = self.P
    ident = self.cst("ident")
    pr = {}
    pt = P.ps("s5pp", [128, 512])
    G = NGP
    pr["rho"] = [P.sb("s5rho", [128, G]) for _ in range(2)]; pr["f"] = [P.sb("s5f", [128, G]) for _ in range(2)]
    pr["BrT"] = [P.sb("s5BrT", [32, G, 128], BF16) for _ in range(2)]; pr["BiT"] = [P.sb("s5BiT", [32, G, 128], BF16) for _ in range(2)]
    pr["CTr"] = [P.sb("s5CTr", [128, G, 32], BF16) for _ in range(2)]; pr["CTi"] = [P.sb("s5CTi", [128, G, 32], BF16) for _ in range(2)]
    pr["dsk"] = P.sb("s5dsk", [32, G])

    def one_dir(d):
        rho = pr["rho"][d]; f = pr["f"][d]; BrT = pr["BrT"][d]; BiT = pr["BiT"][d]; CTr = pr["CTr"][d]; CTi = pr["CTi"][d]
        A = P.sb("s5A", [G, 3, 128])
        P.dma("sp", A[:, 0, :], self.lam_re[l][d].rearrange("(gp two) p -> gp (two p)", two=2), reads=[self.lam_re], writes=[A])
        P.dma("sp", A[:, 1, :], self.lam_im[l][d].rearrange("(gp two) p -> gp (two p)", two=2), reads=[self.lam_im], writes=[A])
        ls = P.sb("s5ls", [G, 2])
        P.dma("sp", ls[:], self.log_step[l][d:d + 1, :].rearrange("o (gp two) -> (o gp) two", two=2), reads=[self.log_step], writes=[ls])
        P.op("dve", lambda e, A=A, ls=ls: e.tensor_copy(out=A[:, 2, :].rearrange("g (t p) -> g t p", t=2),
                                                        in_=ls[:, :].unsqueeze(2).to_broadcast([G, 2, 64])), reads=[ls], writes=[A])
        q = P.sb("s5q", [128, 3, G])
        for k in range(3):
            P.mm(pt[:, k * G:(k + 1) * G], A[:, k, :], ident[0:G, 0:G], True, True, [A, self.C], [pt])
        P.act(q[:].rearrange("p k g -> p (k g)"), pt[:, 0:3 * G], AF.Copy, [pt], [q])
        lr = q[:, 0, :]; li = q[:, 1, :]
        dl = P.sb("s5dl", [128, G])
        P.act(dl[:], q[:, 2, :], AF.Exp, [q], [dl])
        P.tt("dve", rho[:], lr, dl[:], ALU.mult, [q, dl], [rho])
        P.act(rho[:], rho[:], AF.Exp, [rho], [rho])
        P.tt("dve", f[:], li, dl[:], ALU.mult, [q, dl], [f])
        P.ts("dve", f[:], f[:], 1.0 / TWO_PI, None, ALU.mult, None, [f], [f])
        w = P.sb("s5w", [128, 6, G])
        for k, off in ((0, 0.0), (1, 0.25)):
            P.ts("dve", w[:, 2, :], f[:], off, None, ALU.add, None, [f], [w])
            P.ts("dve", w[:, 3, :], w[:, 2, :], MAGIC, MAGIC, ALU.add, ALU.subtract, [w], [w])
            P.tt("dve", w[:, 2, :], w[:, 2, :], w[:, 3, :], ALU.subtract, [w], [w])
            P.act(w[:, k, :], w[:, 2, :], AF.Sin, [w], [w], scale=TWO_PI)
        sn = w[:, 0, :]; cs = w[:, 1, :]
        nr = P.sb("s5nr", [128, G]); ni = P.sb("s5ni", [128, G]); den = P.sb("s5den", [128, G])
        cr = P.sb("s5cr", [128, G]); ci = P.sb("s5ci", [128, G]); tmp = P.sb("s5tmp", [128, G])
        P.tt("dve", nr[:], rho[:], cs, ALU.mult, [rho, w], [nr])
        P.ts("dve", nr[:], nr[:], -1.0, None, ALU.add, None, [nr], [nr])
        P.tt("dve", ni[:], rho[:], sn, ALU.mult, [rho, w], [ni])
        P.tt("dve", den[:], lr, lr, ALU.mult, [q], [den])
        P.tt("dve", tmp[:], li, li, ALU.mult, [q], [tmp])
        P.tt("dve", den[:], den[:], tmp[:], ALU.add, [den, tmp], [den])
        P.op("dve", lambda e, den=den: e.reciprocal(out=den[:], in_=den[:]), reads=[den], writes=[den])
        P.tt("dve", cr[:], nr[:], lr, ALU.mult, [nr, q], [cr])
        P.tt("dve", tmp[:], ni[:], li, ALU.mult, [ni, q], [tmp])
        P.tt("dve", cr[:], cr[:], tmp[:], ALU.add, [cr, tmp], [cr])
        P.tt("dve", cr[:], cr[:], den[:], ALU.mult, [cr, den], [cr])
        P.tt("dve", ci[:], ni[:], lr, ALU.mult, [ni, q], [ci])
        P.tt("dve", tmp[:], nr[:], li, ALU.mult, [nr, q], [tmp])
        P.tt("dve", ci[:], ci[:], tmp[:], ALU.subtract, [ci, tmp], [ci])
        P.tt("dve", ci[:], ci[:], den[:], ALU.mult, [ci, den], [ci])
        Bl = P.sb("s5Bl", [128, 2, G, 16])
        P.dma("sp", Bl[:, 0], self.b_re[l][d].rearrange("(gp two) p c -> (two p) gp c", two=2), reads=[self.b_re], writes=[Bl])
        P.dma("pool", Bl[:, 1], self.b_im[l][d].rearrange("(gp two) p c -> (two p) gp c", two=2), reads=[self.b_im], writes=[Bl])
        S = P.sb("s5S", [128, 2, G, 32])
        P.op("pool", lambda e, S=S: e.memset(S[:], 0.0), writes=[S])
        t1 = P.sb("s5t1", [128, G, 16]); t2 = P.sb("s5t2", [128, G, 16])
        crb = cr[:, :].unsqueeze(2).to_broadcast([128, G, 16]); cib = ci[:, :].unsqueeze(2).to_broadcast([128, G, 16])
        for k, (a0, a1, op) in enumerate(((0, 1, ALU.subtract), (1, 0, ALU.add))):
            P.tt("dve", t1[:], Bl[:, a0], crb, ALU.mult, [Bl, cr], [t1])
            P.tt("dve", t2[:], Bl[:, a1], cib, ALU.mult, [Bl, ci], [t2])
            for half in range(2):
                ps_ = slice(half * 64, half * 64 + 64)
                P.tt("dve", S[ps_, k, :, half * 16:half * 16 + 16], t1[ps_], t2[ps_], op, [t1, t2], [S])
        for k, dst in ((0, BrT), (1, BiT)):
            for g4 in range(G // 4):
                for gg in range(4):
                    gp = g4 * 4 + gg
                    P.mm(pt[0:32, gg * 128:(gg + 1) * 128], S[:, k, gp, :], ident, True, True, [S, self.C], [pt])
                P.act(dst[:, g4 * 4:(g4 + 1) * 4, :], pt[0:32, :].rearrange("p (g s) -> p g s", g=4), AF.Copy, [pt], [dst])
        Cx = P.sb("s5Cx", [64, 2, G, 128])
        P.op("pool", lambda e, Cx=Cx: e.memset(Cx[:], 0.0), writes=[Cx])
        for k, src in ((0, self.c_re), (1, self.c_im)):
            v = src[l][d].rearrange("(gp two) c p -> two c gp p", two=2)
            P.dma("sp", Cx[0:16, k, :, 0:64], v[0], reads=[src], writes=[Cx])
            P.dma("pool", Cx[32:48, k, :, 64:128], v[1], reads=[src], writes=[Cx])
        for k, dst, scl in ((0, CTr, 1.0), (1, CTi, -1.0)):
            for g8 in range(G // 8):
                for gg in range(8):
                    gp = g8 * 8 + gg
                    P.mm(pt[:, gg * 64:(gg + 1) * 64], Cx[:, k, gp, :], ident[0:64, 0:64], True, True, [Cx, self.C], [pt])
                P.act(dst[:, g8 * 8:(g8 + 1) * 8, :].rearrange("p g (t c) -> p g t c", t=2),
                      pt[:, :].rearrange("p (g t c) -> p g t c", g=8, t=2)[:, :, :, 0:16], AF.Copy, [pt], [dst], scale=scl)
    for d in range(2):
        with P.scope():
            one_dir(d)
    dA = P.sb("s5dA", [G, 32])
    P.dma("sp", dA[:], self.s5_d[l:l + 1, :].rearrange("o (gp j) -> (o gp) j", j=32), reads=[self.s5_d], writes=[dA])
    P.mm(pt[0:32, 0:G], dA[:], ident[0:G, 0:G], True, True, [dA, self.C], [pt])
    dsk = pr["dsk"]
    P.act(dsk[:], pt[0:32, 0:G], AF.Copy, [pt], [dsk])
    return pr


def _stage_s5(self, l):
    P = self.P
    J = self.cst("J")
    with P.scope():
        pr = _s5_prep(self, l)
        H = [[P.sb("s5H", [128, NTOK], BF16) for _ in range(2)] for _ in range(2)]
        uf = [P.sb("s5uf", [32, NTOK]) for _ in range(2)]
        ub = [P.sb("s5ub", [32, NTOK], BF16) for _ in range(2)]
        cosT = P.sb("s5cos", [128, 513]); sinT = P.sb("s5sin", [128, 513]); rhoT = P.sb("s5rhoT", [128, 512])
        ph = P.sb("s5ph", [128, 513]); rr = P.sb("s5rr", [128, 513])
        px = [P.ps("s5px", [128, 512]) for _ in range(4)]
        pf = P.ps("s5pf", [32, 512]); pb = P.ps("s5pb", [128, 128]); pj = P.ps("s5pj", [32, 512])
        tm = [P.sb("s5tm", [128, 512]) for _ in range(8)]
        xt_ = [P.sb("s5xt", [128, 512]) for _ in range(2)]
        gg_ = [[P.sb("s5g", [128, 512]) for _ in range(2)] for _ in range(2)]
        ini = [P.sb("s5ini", [128, 4]) for _ in range(2)]
        ybt = P.sb("s5ybt", [128, 32]); ybs = P.sb("s5ybs", [32, 512]); yy = P.sb("s5yy", [32, 512]); ww = P.sb("s5ww", [32, 512])
        GTt = P.sb("s5GT", [32, NTOK])
        srcs = [self.PT, self.UR]
        for gp in range(NGP):
            for d in range(2):
                P.dma("sp" if d == 0 else "pool", uf[d][:], srcs[d][gp * 32:(gp + 1) * 32, :], reads=[srcs[d]], writes=[uf[d]])
                P.act(ub[d][:], uf[d][:], AF.Copy, [uf[d]], [ub[d]])
            for d in range(2):
                f = pr["f"][d]; rho = pr["rho"][d]
                P.ts("dve", ph[:], self.iota[:, 0:513], f[:, gp:gp + 1], None, ALU.mult, None, [self.iota, f], [ph])
                for dst, off in ((sinT, 0.0), (cosT, 0.25)):
                    if off:
                        P.ts("dve", ph[:], ph[:], off, None, ALU.add, None, [ph], [ph])
                    P.ts("dve", rr[:], ph[:], MAGIC, MAGIC, ALU.add, ALU.subtract, [ph], [rr])
                    P.tt("dve", rr[:], ph[:], rr[:], ALU.subtract, [ph, rr], [rr])
                    P.act(dst[:], rr[:], AF.Sin, [rr], [dst], scale=TWO_PI)
                P.ts("dve", rhoT[:], self.iota[:, 0:512], 0.0, rho[:, gp:gp + 1], ALU.mult, ALU.add, [self.iota, rho], [rhoT])
                nch = 9
                for k in range(nch):
                    c0 = k * 512
                    n = min(512, NTOK - c0)
                    xr = px[(2 * k) % 4]; xi = px[(2 * k + 1) % 4]
                    P.mm(xr[:, 0:n], pr["BrT"][d][:, gp, :], ub[d][:, c0:c0 + n], True, True, [pr["BrT"][d], ub[d]], [xr])
                    P.mm(xi[:, 0:n], pr["BiT"][d][:, gp, :], ub[d][:, c0:c0 + n], True, True, [pr["BiT"][d], ub[d]], [xi])
                    c = cosT[:, 0:n]; s = sinT[:, 0:n]
                    P.tt("dve", tm[0][:, 0:n], xr[:, 0:n], c, ALU.mult, [xr, cosT], [tm[0]])
                    P.tt("dve", tm[1][:, 0:n], xi[:, 0:n], s, ALU.mult, [xi, sinT], [tm[1]])
                    P.tt("pool", xt_[0][:, 0:n], tm[0][:, 0:n], tm[1][:, 0:n], ALU.add, [tm[0], tm[1]], [xt_[0]])
                    P.tt("dve", tm[2][:, 0:n], xi[:, 0:n], c, ALU.mult, [xi, cosT], [tm[2]])
                    P.tt("dve", tm[3][:, 0:n], xr[:, 0:n], s, ALU.mult, [xr, sinT], [tm[3]])
                    P.tt("pool", xt_[1][:, 0:n], tm[2][:, 0:n], tm[3][:, 0:n], ALU.subtract, [tm[2], tm[3]], [xt_[1]])
                    g = gg_[k % 2]
                    icur = ini[k % 2]; inxt = ini[(k + 1) % 2]
                    for ri in range(2):
                        init = 0.0 if k == 0 else icur[:, ri:ri + 1]
                        P.op("dve", lambda e, g=g, ri=ri, init=init, n=n: e.tensor_tensor_scan(
                            out=g[ri][:, 0:n], data0=rhoT[:, 0:n], data1=xt_[ri][:, 0:n], initial=init,
                            op0=ALU.mult, op1=ALU.add), reads=[rhoT, xt_[ri]] + ([icur] if k else []), writes=[g[ri]])
                    if k + 1 < nch:
                        grl = g[0][:, n - 1:n]; gil = g[1][:, n - 1:n]; cT = cosT[:, n:n + 1]; sT = sinT[:, n:n + 1]
                        P.ts("dve", inxt[:, 2:3], gil, sT, None, ALU.mult, None, [g[1], sinT], [inxt])
                        P.stt("dve", inxt[:, 0:1], grl, cT, inxt[:, 2:3], ALU.mult, ALU.subtract, [g[0], cosT, inxt], [inxt])
                        P.ts("dve", inxt[:, 3:4], gil, cT, None, ALU.mult, None, [g[1], cosT], [inxt])
                        P.stt("dve", inxt[:, 1:2], grl, sT, inxt[:, 3:4], ALU.mult, ALU.add, [g[0], sinT, inxt], [inxt])
                    P.tt("pool", tm[4][:, 0:n], g[0][:, 0:n], c, ALU.mult, [g[0], cosT], [tm[4]])
                    P.tt("pool", tm[5][:, 0:n], g[1][:, 0:n], s, ALU.mult, [g[1], sinT], [tm[5]])
                    P.tt("pool", H[d][0][:, c0:c0 + n], tm[4][:, 0:n], tm[5][:, 0:n], ALU.subtract, [tm[4], tm[5]], [H[d][0]])
                    P.tt("pool", tm[6][:, 0:n], g[0][:, 0:n], s, ALU.mult, [g[0], sinT], [tm[6]])
                    P.tt("pool", tm[7][:, 0:n], g[1][:, 0:n], c, ALU.mult, [g[1], cosT], [tm[7]])
                    P.tt("pool", H[d][1][:, c0:c0 + n], tm[6][:, 0:n], tm[7][:, 0:n], ALU.add, [tm[6], tm[7]], [H[d][1]])
            for nb in range(9):
                c0 = nb * 512
                n = min(512, NTOK - c0)
                P.mm(pf[:, 0:n], pr["CTr"][0][:, gp, :], H[0][0][:, c0:c0 + n], True, False, [pr["CTr"][0], H[0][0]], [pf])
                P.mm(pf[:, 0:n], pr["CTi"][0][:, gp, :], H[0][1][:, c0:c0 + n], False, True, [pr["CTi"][0], H[0][1]], [pf])
                for sbk in range(n // 128):
                    tau = nb * 4 + sbk
                    j = RTINV[tau]
                    P.mm(pb[:, 0:32], H[1][0][:, j * 128:(j + 1) * 128], pr["CTr"][1][:, gp, :], True, False, [pr["CTr"][1], H[1][0]], [pb])
                    P.mm(pb[:, 0:32], H[1][1][:, j * 128:(j + 1) * 128], pr["CTi"][1][:, gp, :], False, True, [pr["CTi"][1], H[1][1]], [pb])
                    P.act(ybt[:], pb[:, 0:32], AF.Copy, [pb], [ybt])
                    P.mm(pj[:, sbk * 128:(sbk + 1) * 128], ybt[:], J, True, True, [ybt, self.C], [pj])
                P.act(ybs[:, 0:n], pj[:, 0:n], AF.Copy, [pj], [ybs])
                P.tt("dve", yy[:, 0:n], pf[:, 0:n], ybs[:, 0:n], ALU.add, [pf, ybs], [yy])
                P.stt("dve", yy[:, 0:n], uf[0][:, c0:c0 + n], pr["dsk"][:, gp:gp + 1], yy[:, 0:n], ALU.mult, ALU.add, [uf[0], pr["dsk"], yy], [yy])
                P.act(ww[:, 0:n], yy[:, 0:n], AF.Square, [yy], [ww])
                P.ts("dve", ww[:, 0:n], ww[:, 0:n], 0.044715, 1.0, ALU.mult, ALU.add, [ww], [ww])
                P.tt("dve", ww[:, 0:n], ww[:, 0:n], yy[:, 0:n], ALU.mult, [ww, yy], [ww])
                P.act(ww[:, 0:n], ww[:, 0:n], AF.Sigmoid, [ww], [ww], scale=1.5957691216)
                P.tt("dve", GTt[:, c0:c0 + n], ww[:, 0:n], yy[:, 0:n], ALU.mult, [ww, yy], [GTt])
            P.dma("sp", self.GT[gp * 32:(gp + 1) * 32, :], GTt[:], reads=[GTt], writes=[self.GT])
            P.coll("AllGather", ALU.bypass, RG, self.GT[gp * 32:(gp + 1) * 32, :], self.GTF[gp], [self.GT], [self.GTF])


Net.stage_s5 = _stage_s5


def dn_tile_rows(dram_t, i, ncols_slice=None):
    if i < 2:
        return [(slice(0, 128), dram_t[i * 128:(i + 1) * 128, :])]
    c0 = 2 * (i - 2)
    v = dram_t[NCTX:NTOK, :].rearrange("(r c) n -> c r n", c=64)
    return [(slice(0, 64), v[c0]), (slice(64, 128), v[c0 + 1])]


def _stage_dn_prep(self, l):
    P = self.P
    ident = self.cst("ident"); ones = self.cst("ones")
    with P.scope():
        pt = P.ps("dpt", [128, 512]); pn = [P.ps("dpn", [128, 512]) for _ in range(2)]
        NCH = 3 * UC // 128
        HC = UC // 128
        cwl = P.sb("cwl", [5, 3 * UC])
        P.dma("sp", cwl[:], self.conv_w[l], reads=[self.conv_w], writes=[cwl])
        cwT = P.sb("cwT", [128, NCH, 5])
        for c in range(NCH):
            P.mm(pt[:, c * 8:c * 8 + 5], cwl[0:5, c * 128:(c + 1) * 128], ident[0:5, 0:5], True, True, [cwl, self.C], [pt])
        P.act(cwT[:], pt[:, 0:8 * NCH].rearrange("p (c j) -> p c j", j=8)[:, :, 0:5], AF.Copy, [pt], [cwT])
        raw = P.sb("draw", [128, NTOK]); pd = P.sb("dpd", [128, NTOK + 8]); acc = P.sb("dacc", [128, NTOK]); cs = P.sb("dcs", [128, NTOK])
        sq = P.sb("dsq", [128, 512]); rs = P.sb("drs", [128, 512]); tk = [P.sb("dtk", [128, 512]) for _ in range(2)]
        P.op("pool", lambda e: e.memset(pd[:], 0.0), writes=[pd])
        for c in range(NCH):
            kind = c // HC
            h = c % HC
            P.dma("sp", raw[:], self.PT[UC + c * 128:UC + (c + 1) * 128, :], reads=[self.PT], writes=[raw])
            P.act(pd[:, 2:258], raw[:, 0:256], AF.Copy, [raw], [pd])
            P.op("pool", lambda e: e.tensor_copy(out=pd[:, 262:262 + NLAT].rearrange("p (c r) -> p c r", r=64),
                                                 in_=raw[:, 256:NTOK].rearrange("p (r c) -> p c r", c=64)), reads=[raw], writes=[pd])
            for base, o0, n in ((0, 0, 256), (260, 256, NLAT)):
                P.ts("dve", acc[:, o0:o0 + n], pd[:, base:base + n], cwT[:, c, 0:1], None, ALU.mult, None, [pd, cwT], [acc])
                for j in range(1, 5):
                    P.stt("dve", acc[:, o0:o0 + n], pd[:, base + j:base + j + n], cwT[:, c, j:j + 1], acc[:, o0:o0 + n],
                          ALU.mult, ALU.add, [pd, cwT, acc], [acc])
            P.act(cs[:], acc[:], AF.Silu, [acc], [cs])
            if kind < 2:
                for nb in range(9):
                    c0 = nb * 512; n = min(512, NTOK - c0)
                    p_ = pn[nb % 2]
                    P.act(sq[:, 0:n], cs[:, c0:c0 + n], AF.Square, [cs], [sq])
                    P.mm(p_[:, 0:n], ones, sq[:, 0:n], True, True, [self.C, sq], [p_])
                    P.act(rs[:, 0:n], p_[:, 0:n], AF.Sqrt, [p_], [rs], bias=1e-6)
                    P.op("dve", lambda e, n=n: e.reciprocal(out=rs[:, 0:n], in_=rs[:, 0:n]), reads=[rs], writes=[rs])
                    if kind == 0:
                        P.stt("dve", cs[:, c0:c0 + n], cs[:, c0:c0 + n], 128.0 ** -0.5, rs[:, 0:n], ALU.mult, ALU.mult, [cs, rs], [cs])
                    else:
                        P.tt("dve", cs[:, c0:c0 + n], cs[:, c0:c0 + n], rs[:, 0:n], ALU.mult, [cs, rs], [cs])
                P.dma("sp", (self.QT if kind == 0 else self.KT)[h * 128:(h + 1) * 128, :], cs[:], reads=[cs],
                      writes=[self.QT if kind == 0 else self.KT])
            if kind >= 1:
                dst = self.KTOK if kind == 1 else self.VTOK
                for i4 in range(0, NT, 4):
                    nn = min(4, NT - i4)
                    for ii in range(nn):
                        i = i4 + ii
                        P.mm(pt[:, ii * 128:(ii + 1) * 128], cs[:, i * 128:(i + 1) * 128], ident, True, True, [cs, self.C], [pt])
                    t_ = tk[(i4 // 4) % 2]
                    P.act(t_[:, 0:nn * 128], pt[:, 0:nn * 128], AF.Copy, [pt], [t_])
                    for ii in range(nn):
                        i = i4 + ii
                        P.dma("pool", dst[i * 128:(i + 1) * 128, h * 128:(h + 1) * 128], t_[:, ii * 128:(ii + 1) * 128], reads=[t_], writes=[dst])
        dtb = P.sb("ddtb", [128, 2 * NH]); nea = P.sb("dnea", [128, 2 * NH])
        self.bcast_load("sp", dtb, self.dt_bias[l:l + 1, :], self.dt_bias)
        self.bcast_load("sp", nea, self.a_log[l:l + 1, :], self.a_log)
        P.act(nea[:], nea[:], AF.Exp, [nea], [nea])
        P.ts("dve", nea[:], nea[:], -1.0, None, ALU.mult, None, [nea], [nea])
        abt = [P.sb("dabt", [128, ABC]) for _ in range(2)]; gbt = [P.sb("dgbt", [128, ABC]) for _ in range(2)]
        for i in range(NT):
            a_ = abt[i % 2]; g_ = gbt[i % 2]
            for ps_, ap in dn_tile_rows(self.AB, i):
                P.dma("sp", a_[ps_, :], ap, reads=[self.AB], writes=[a_])
            av = a_[:, :].rearrange("p (d k h) -> p d k h", d=2, k=2)
            G2 = 2 * NH
            gv = g_[:, 0:G2].rearrange("p (d h) -> p d h", d=2)
            P.tt("dve", gv, av[:, :, 0, :], dtb[:, :].rearrange("p (d h) -> p d h", d=2), ALU.add, [a_, dtb], [g_])
            P.act(g_[:, 0:G2], g_[:, 0:G2], AF.Exp, [g_], [g_])
            P.act(g_[:, 0:G2], g_[:, 0:G2], AF.Ln, [g_], [g_], bias=1.0)
            P.tt("dve", g_[:, 0:G2], g_[:, 0:G2], nea[:], ALU.mult, [g_, nea], [g_])
            P.act(g_[:, G2:2 * G2].rearrange("p (d h) -> p d h", d=2), av[:, :, 1, :], AF.Sigmoid, [a_], [g_])
            P.dma("pool", self.GB[i * 128:(i + 1) * 128, :], g_[:], reads=[g_], writes=[self.GB])


def _stage_dn_main(self, l):
    P = self.P
    ident = self.cst("ident")
    with P.scope():
        pA = P.ps("dpA", [128, 1024]); pB = P.ps("dpB", [128, 1024]); pC = P.ps("dpC", [128, 1024]); pD = P.ps("dpD", [128, 1024])
        S = P.sb("dS", [128, NH, 128])
        qT = P.sb("dqT", [128, NH, 128]); kT = P.sb("dkT", [128, NH, 128]); ktok = P.sb("dktok", [128, NH, 128]); vtok = P.sb("dvtok", [128, NH, 128])
        gb = P.sb("dgb", [128, 4 * NH]); sm = P.sb("dsm", [128, 6, NH])
        Gall = P.sb("dGall", [128, NH, 128]); Dm = P.sb("dDm", [128, NH, 128]); QKm = P.sb("dQKm", [128, NH, 128])
        MT = [P.sb("dMT", [128, NH, 128]) for _ in range(2)]; MA = [P.sb("dMA", [128, NH, 128]) for _ in range(2)]
        r = P.sb("dr", [128, NH, 256]); WT = P.sb("dWT", [128, NH, 128]); kdec = P.sb("dkdec", [128, NH, 128])
        vnew = P.sb("dvnew", [128, NH, 128]); o1 = P.sb("do1", [128, NH, 128]); O = P.sb("dO", [128, NH, 128])
        v3 = lambda t: t[:, 0:NH * 128].rearrange("p (h n) -> p h n", h=NH)
        for d in range(2):
            tri = self.cst("tri_f" if d == 0 else "tri_b"); ms = self.cst("ms_f" if d == 0 else "ms_b"); mi = self.cst("mi_f" if d == 0 else "mi_b")
            order = list(range(NT)) if d == 0 else [1, 0] + list(range(NT - 1, 1, -1))
            halves = (slice(0, 64), slice(64, 128)) if d == 0 else (slice(64, 128), slice(0, 64))
            P.op("pool", lambda e: e.memset(S[:], 0.0), writes=[S])
            for i in order:
                cols = slice(i * 128, (i + 1) * 128)
                P.dma("sp", qT[:], self.QT[:, cols].rearrange("(h p) n -> p h n", p=128), reads=[self.QT], writes=[qT])
                P.dma("pool", kT[:], self.KT[:, cols].rearrange("(h p) n -> p h n", p=128), reads=[self.KT], writes=[kT])
                P.dma("sp", ktok[:].rearrange("p h n -> p (h n)"), self.KTOK[cols, :], reads=[self.KTOK], writes=[ktok])
                P.dma("pool", vtok[:].rearrange("p h n -> p (h n)"), self.VTOK[cols, :], reads=[self.VTOK], writes=[vtok])
                P.dma("sp", gb[:], self.GB[cols, :], reads=[self.GB], writes=[gb])
                g = gb[:, d * NH:d * NH + NH]; beta = gb[:, 2 * NH + d * NH:2 * NH + d * NH + NH]
                P.mm(pA[:, 0:NH], tri, g, True, True, [self.C, gb], [pA])
                P.mm(pA[:, NH:2 * NH], self.cst("blk_d"), g, True, True, [self.C, gb], [pA])
                P.mm(pA[:, 2 * NH:3 * NH], self.cst("blk_a"), g, True, True, [self.C, gb], [pA])
                P.mm(pA[:, 3 * NH:4 * NH], self.cst("blk_b"), g, True, True, [self.C, gb], [pA])
                P.act(sm[:, 0:4, :].rearrange("p a h -> p (a h)"), pA[:, 0:4 * NH], AF.Copy, [pA], [sm])
                P.act(sm[:, 4, :], sm[:, 0, :], AF.Exp, [sm], [sm])
                P.tt("dve", sm[:, 5, :], sm[:, 1, :], sm[:, 0, :], ALU.subtract, [sm], [sm])
                P.act(sm[:, 5, :], sm[:, 5, :], AF.Exp, [sm], [sm])
                P.act(sm[:, 2:4, :], sm[:, 2:4, :], AF.Exp, [sm], [sm])
                gc = sm[:, 0, :]; egc = sm[:, 4, :]; ekd = sm[:, 5, :]
                bc = lambda ap, n=128: ap.unsqueeze(2).to_broadcast([128, NH, n])
                P.op("dve", lambda e: e.tensor_copy(out=Gall[:], in_=bc(g)), reads=[gb], writes=[Gall])
                for h in range(NH):
                    P.mm(pB[:, h * 128:(h + 1) * 128], Gall[:, h, :], tri, True, True, [Gall, self.C], [pB])
                P.tt("dve", Dm[:], v3(pB), bc(gc), ALU.subtract, [pB, sm], [Dm])
                P.ts("dve", Dm[:], Dm[:], 0.0, None, ALU.min, None, [Dm], [Dm])
                P.act(Dm[:], Dm[:], AF.Exp, [Dm], [Dm])
                for h in range(NH):
                    P.mm(pC[:, h * 128:(h + 1) * 128], kT[:, h, :], kT[:, h, :], True, True, [kT], [pC])
                    P.mm(pD[:, h * 128:(h + 1) * 128], kT[:, h, :], qT[:, h, :], True, True, [kT, qT], [pD])
                msb = ms.unsqueeze(1).to_broadcast([128, NH, 128]); mib = mi.unsqueeze(1).to_broadcast([128, NH, 128])
                P.tt("dve", MT[0][:], v3(pC), Dm[:], ALU.mult, [pC, Dm], [MT[0]])
                P.tt("pool", MT[0][:], MT[0][:], msb, ALU.mult, [MT[0], self.C], [MT[0]])
                P.tt("dve", MT[0][:], MT[0][:], bc(beta), ALU.mult, [MT[0], gb], [MT[0]])
                P.tt("dve", QKm[:], v3(pD), Dm[:], ALU.mult, [pD, Dm], [QKm])
                P.tt("pool", QKm[:], QKm[:], mib, ALU.mult, [QKm, self.C], [QKm])
                for h in range(NH):
                    P.mm(pC[:, h * 128:(h + 1) * 128], MT[0][:, h, :], ident, True, True, [MT[0], self.C], [pC])
                P.act(MA[0][:], v3(pC), AF.Copy, [pC], [MA[0]])
                P.op("pool", lambda e: e.tensor_copy(out=r[:, :, 0:128], in_=vtok[:]), reads=[vtok], writes=[r])
                P.tt("dve", r[:, :, 128:256], ktok[:], bc(egc), ALU.mult, [ktok, sm], [r])
                P.tt("pool", kdec[:], ktok[:], bc(ekd), ALU.mult, [ktok, sm], [kdec])
                cur = 0
                for k in range(6):
                    mt = MT[cur]; ma = MA[cur]
                    for h in range(NH):
                        P.mm(pA[:, h * 256:(h + 1) * 256], mt[:, h, :], r[:, h, :], True, True, [mt, r], [pA])
                    P.tt("dve", r[:], r[:], pA[:, 0:NH * 256].rearrange("p (h n) -> p h n", h=NH),
                         ALU.subtract if k == 0 else ALU.add, [r, pA], [r])
                    if k < 5:
                        nx = 1 - cur
                        for h in range(NH):
                            P.mm(pC[:, h * 128:(h + 1) * 128], ma[:, h, :], mt[:, h, :], True, True, [ma, mt], [pC])
                            P.mm(pD[:, h * 128:(h + 1) * 128], mt[:, h, :], ma[:, h, :], True, True, [ma, mt], [pD])
                        P.act(MT[nx][:], v3(pC), AF.Copy, [pC], [MT[nx]])
                        P.act(MA[nx][:], v3(pD), AF.Copy, [pD], [MA[nx]])
                        cur = nx
                P.tt("dve", r[:], r[:], bc(beta, 256), ALU.mult, [r, gb], [r])
                for h in range(NH):
                    P.mm(pC[:, h * 128:(h + 1) * 128], r[:, h, 128:256], ident, True, True, [r, self.C], [pC])
                P.act(WT[:], v3(pC), AF.Copy, [pC], [WT])
                for hi, rows in enumerate(halves):
                    egl = sm[:, 2 if rows.start == 0 else 3, :]
                    for h in range(NH):
                        P.mm(pA[rows, h * 128:(h + 1) * 128], WT[:, h, rows], S[:, h, :], True, True, [WT, S], [pA])
                        P.mm(pB[rows, h * 128:(h + 1) * 128], qT[:, h, rows], S[:, h, :], True, True, [qT, S], [pB])
                    P.tt("dve", vnew[rows], r[rows, :, 0:128], v3(pA)[rows], ALU.subtract, [r, pA], [vnew])
                    P.tt("dve", o1[rows], v3(pB)[rows], egc[rows].unsqueeze(2).to_broadcast([64, NH, 128]), ALU.mult, [pB, sm], [o1])
                    for h in range(NH):
                        P.mm(pC[rows, h * 128:(h + 1) * 128], QKm[rows, h, rows], vnew[rows, h, :], True, True, [QKm, vnew], [pC])
                        P.mm(pD[:, h * 128:(h + 1) * 128], kdec[rows, h, :], vnew[rows, h, :], True, True, [kdec, vnew], [pD])
                    P.tt("dve", O[rows], o1[rows], v3(pC)[rows], ALU.add, [o1, pC], [O])
                    P.tt("pool", S[:], S[:], bc(egl), ALU.mult, [S, sm], [S])
                    P.tt("dve", S[:], S[:], v3(pD), ALU.add, [S, pD], [S])
                P.dma("sp", self.OD[d][cols, :], O[:].rearrange("p h n -> p (h n)"), reads=[O], writes=[self.OD])


def _stage_dn_fin(self, l):
    P = self.P
    with P.scope():
        nw = P.sb("dnw", [128, 128])
        self.bcast_load("sp", nw, self.norm_w[l:l + 1, :], self.norm_w)
        oa = [P.sb("foa", [128, NH, 128]) for _ in range(2)]; ob = [P.sb("fob", [128, NH, 128]) for _ in range(2)]
        zt = [P.sb("fzt", [128, NH, 128]) for _ in range(2)]; sq = P.sb("fsq", [128, NH, 128]); ss = P.sb("fss", [128, NH])
        for i in range(NT):
            a = oa[i % 2]; b = ob[i % 2]; z = zt[i % 2]
            cols = slice(i * 128, (i + 1) * 128)
            P.dma("sp", a[:].rearrange("p h n -> p (h n)"), self.OD[0][cols, :], reads=[self.OD], writes=[a])
            P.dma("pool", b[:].rearrange("p h n -> p (h n)"), self.OD[1][cols, :], reads=[self.OD], writes=[b])
            for ps_, ap in dn_tile_rows(self.Z, i):
                P.dma("sp", z[ps_].rearrange("p h n -> p (h n)"), ap, reads=[self.Z], writes=[z])
            P.tt("dve", a[:], a[:], b[:], ALU.add, [a, b], [a])
            P.tt("pool", sq[:], a[:], a[:], ALU.mult, [a], [sq])
            P.op("dve", lambda e: e.tensor_reduce(out=ss[:], in_=sq[:], axis=AX.X, op=ALU.add), reads=[sq], writes=[ss])
            P.act(ss[:], ss[:], AF.Sqrt, [ss], [ss], scale=1.0 / 128.0, bias=1e-6)
            P.op("dve", lambda e: e.reciprocal(out=ss[:], in_=ss[:]), reads=[ss], writes=[ss])
            P.tt("dve", a[:], a[:], ss[:, :].unsqueeze(2).to_broadcast([128, NH, 128]), ALU.mult, [a, ss], [a])
            P.tt("pool", a[:], a[:], nw[:, :].unsqueeze(1).to_broadcast([128, NH, 128]), ALU.mult, [a, nw], [a])
            P.act(b[:], z[:], AF.Silu, [z], [b])
            P.tt("dve", a[:], a[:], b[:], ALU.mult, [a, b], [a])
            for ps_, ap in dn_tile_rows(self.DNO, i):
                P.dma("pool", ap, a[ps_].rearrange("p h n -> p (h n)"), reads=[a], writes=[self.DNO])
        for c in range(5):
            rc = 1024 if c < 4 else 256
            P.coll("AllGather", ALU.bypass, RG, self.DNO[c * 1024:c * 1024 + rc, :], self.DNOF[R * c * 1024:R * c * 1024 + R * rc, :],
                   [self.DNO], [self.DNOF])


def _stage_dn(self, l):
    _stage_dn_prep(self, l)
    _stage_dn_main(self, l)
    _stage_dn_fin(self, l)


Net.stage_dn = _stage_dn


def _ln_tile(self, t, lng, lnb, tmp, st):
    P = self.P
    P.op("dve", lambda e: e.tensor_reduce(out=st[:, 0:1], in_=t[:], axis=AX.X, op=ALU.add), reads=[t], writes=[st])
    P.ts("dve", st[:, 1:2], st[:, 0:1], -1.0 / D, None, ALU.mult, None, [st], [st])
    P.ts("dve", t[:], t[:], st[:, 1:2], None, ALU.add, None, [t, st], [t])
    P.tt("pool", tmp[:], t[:], t[:], ALU.mult, [t], [tmp])
    P.op("dve", lambda e: e.tensor_reduce(out=st[:, 2:3], in_=tmp[:], axis=AX.X, op=ALU.add), reads=[tmp], writes=[st])
    P.act(st[:, 3:4], st[:, 2:3], AF.Sqrt, [st], [st], scale=1.0 / D, bias=1e-5)
    P.op("dve", lambda e: e.reciprocal(out=st[:, 3:4], in_=st[:, 3:4]), reads=[st], writes=[st])
    P.stt("dve", t[:], t[:], st[:, 3:4], lng[:], ALU.mult, ALU.mult, [t, st, lng], [t])
    P.tt("pool", t[:], t[:], lnb[:], ALU.add, [t, lnb], [t])


def _stage_glu(self, l):
    P = self.P
    ident = self.cst("ident")
    with P.scope():
        ps = [P.ps("gps", [128, 512]) for _ in range(3)]
        gw = P.sb("ggw", [128, 8, UC], BF16); stg = P.sb("gstg", [128, NTOK])
        for kc in range(8):
            P.dma("sp", stg[:, 0:UC], self.glu_w[l][kc * 128:(kc + 1) * 128, :], reads=[self.glu_w], writes=[stg])
            P.act(gw[:, kc, :], stg[:, 0:UC], AF.Copy, [stg], [gw])
        NOC = UC // 128
        gbl = P.sb("ggbl", [NOC, 128]); glub = P.sb("gglub", [128, NOC])
        P.dma("sp", gbl[:], self.glu_b[l:l + 1, :].rearrange("o (k p) -> (o k) p", p=128), reads=[self.glu_b], writes=[gbl])
        P.mm(ps[0][:, 0:NOC], gbl[:], ident[0:NOC, 0:NOC], True, True, [gbl, self.C], [ps[0]])
        P.act(glub[:], ps[0][:, 0:NOC], AF.Copy, [ps[0]], [glub])
        Gb = P.sb("gGb", [128, 8, NTOK], BF16)
        for kc in range(8):
            rr_ = kc // (UC // 128); lb = kc % (UC // 128)
            for j in range(4):
                P.dma("sp" if j % 2 == 0 else "pool", stg[32 * j:32 * (j + 1), :], self.GTF[lb * 4 + j][rr_ * 32:(rr_ + 1) * 32, :],
                      reads=[self.GTF], writes=[stg])
            P.act(Gb[:, kc, :], stg[:], AF.Copy, [stg], [Gb])
        ob = [P.sb("gob", [128, NTOK], BF16) for _ in range(2)]; sg = [P.sb("gsg", [128, 512]) for _ in range(2)]
        for oc in range(NOC):
            o = ob[oc % 2]
            P.dma("sp", stg[:], self.GT[oc * 128:(oc + 1) * 128, :], reads=[self.GT], writes=[stg])
            for nb in range(9):
                c0 = nb * 512; n = min(512, NTOK - c0)
                p_ = ps[nb % 3]; s_ = sg[nb % 2]
                for kc in range(8):
                    P.mm(p_[:, 0:n], gw[:, kc, oc * 128:(oc + 1) * 128], Gb[:, kc, c0:c0 + n], kc == 0, kc == 7, [gw, Gb], [p_])
                P.act(s_[:, 0:n], p_[:, 0:n], AF.Sigmoid, [p_, glub], [s_], bias=glub[:, oc:oc + 1])
                P.tt("dve", o[:, c0:c0 + n], s_[:, 0:n], stg[:, c0:c0 + n], ALU.mult, [s_, stg], [o])
            P.dma("pool", self.S5T[oc * 128:(oc + 1) * 128, :], o[:], reads=[o], writes=[self.S5T])
            for hh in range(2):
                P.coll("AllGather", ALU.bypass, RG, self.S5T[oc * 128 + hh * 64:oc * 128 + (hh + 1) * 64, :], self.S5TF[oc * 2 + hh],
                       [self.S5T], [self.S5TF])


def _stage_outproj(self, l):
    P = self.P
    with P.scope():
        wo = P.sb("owo", [128, 16, DC], BF16); stg = P.sb("ostg", [128, DC])
        for kc in range(16):
            P.dma("sp" if kc % 2 == 0 else "pool", stg[:], self.w_out[l][kc * 128:(kc + 1) * 128, :], reads=[self.w_out], writes=[stg])
            P.act(wo[:, kc, :], stg[:], AF.Copy, [stg], [wo])
        s5t = [P.sb("os5t", [128, 8, 128], BF16) for _ in range(2)]
        dnt = [P.sb("odnt", [128, 1024]) for _ in range(2)]; dnb = P.sb("odnb", [128, 1024], BF16); dnT = P.sb("odnT", [128, 8, 128], BF16)
        pt = P.ps("opt", [128, 1024]); pa = [P.ps("opa", [128, 512]) for _ in range(3)]
        t = [P.sb("ot_", [128, DC]) for _ in range(2)]
        for i in range(NT):
            cols = slice(i * 128, (i + 1) * 128)
            s5 = s5t[i % 2]; dn = dnt[i % 2]; t_ = t[i % 2]
            for kc in range(8):
                rr_ = kc // (UC // 128); lb = kc % (UC // 128)
                for hh in range(2):
                    P.dma("sp" if hh == 0 else "pool", s5[hh * 64:(hh + 1) * 64, kc, :], self.S5TF[lb * 2 + hh][rr_ * 64:(rr_ + 1) * 64, cols],
                          reads=[self.S5TF], writes=[s5])
            c_ = i // 8; ii = i % 8; rc = 1024 if c_ < 4 else 256
            dnf = self.DNOF[R * c_ * 1024:R * c_ * 1024 + R * rc, :].rearrange("(r t) c -> t r c", r=R)
            P.dma("pool", dn[:, :].rearrange("p (r c) -> p r c", r=R), dnf[ii * 128:(ii + 1) * 128], reads=[self.DNOF], writes=[dn])
            P.act(dnb[:], dn[:], AF.Copy, [dn], [dnb])
            for kc in range(8):
                P.mm(pt[:, kc * 128:(kc + 1) * 128], dnb[:, kc * 128:(kc + 1) * 128], self.Cb[:, 0:128], True, True, [dnb, self.Cb], [pt])
            P.act(dnT[:].rearrange("p k n -> p (k n)"), pt[:], AF.Copy, [pt], [dnT])
            for cb in range(DC // 512):
                p_ = pa[(i + cb) % 3]
                cs_ = slice(cb * 512, (cb + 1) * 512)
                for kc in range(8):
                    P.mm(p_[:], s5[:, kc, :], wo[:, kc, cs_], kc == 0, False, [s5, wo], [p_])
                for kc in range(8):
                    P.mm(p_[:], dnT[:, kc, :], wo[:, 8 + kc, cs_], False, kc == 7, [dnT, wo], [p_])
                P.act(t_[:, cs_], p_[:], AF.Copy, [p_], [t_])
            P.dma("sp", self.MIXL[cols, :], t_[:], reads=[t_], writes=[self.MIXL])
            if i % 4 == 3 or i == NT - 1:
                c_ = i // 4; rc = 512 if c_ < 8 else 256
                P.coll("AllGather", ALU.bypass, RG, self.MIXL[c_ * 512:c_ * 512 + rc, :], self.MIXF[R * c_ * 512:R * c_ * 512 + R * rc, :],
                       [self.MIXL], [self.MIXF])
    _stage_resln(self, l, 1)


def _stage_resln(self, l, which):
    P = self.P
    with P.scope():
        gseg = 2 if which == 1 else 5
        g_ = {}
        for who, row in (("lat", 0), ("ctx", 1)):
            g_[who] = P.sb("lg", [128, D])
            self.mod_load("sp", g_[who], l, row, gseg)
        lng = P.sb("llng", [128, D]); lnb = P.sb("llnb", [128, D])
        self.bcast_load("sp", lng, (self.ln1_g if which == 1 else self.ln2_g)[l:l + 1, :], self.ln1_g if which == 1 else self.ln2_g)
        self.bcast_load("pool", lnb, (self.ln1_b if which == 1 else self.ln2_b)[l:l + 1, :], self.ln1_b if which == 1 else self.ln2_b)
        tt_ = [P.sb("lt", [128, D]) for _ in range(2)]; xx = [P.sb("lx", [128, D]) for _ in range(2)]
        tmp = P.sb("ltmp", [128, D]); st = P.sb("lst", [128, 4])
        for i in range(NT):
            cols = slice(i * 128, (i + 1) * 128)
            t = tt_[i % 2]; x = xx[i % 2]
            if which == 1:
                c_ = i // 4; ii = i % 4; rc = 512 if c_ < 8 else 256
                mixf = self.MIXF[R * c_ * 512:R * c_ * 512 + R * rc, :].rearrange("(r t) c -> t r c", r=R)
                P.dma("sp", t[:, :].rearrange("p (r c) -> p r c", r=R), mixf[ii * 128:(ii + 1) * 128], reads=[self.MIXF], writes=[t])
            else:
                P.dma("sp", t[:], self.FF[cols, :], reads=[self.FF], writes=[t])
            P.dma("pool", x[:], self.X[cols, :], reads=[self.X], writes=[x])
            P.tt("dve", t[:], t[:], g_["ctx" if i < 2 else "lat"][:], ALU.mult, [t, g_["ctx"], g_["lat"]], [t])
            P.stt("dve", t[:], x[:], ALPHA, t[:], ALU.mult, ALU.add, [x, t], [t])
            _ln_tile(self, t, lng, lnb, tmp, st)
            P.dma("sp", self.X[cols, :], t[:], reads=[t], writes=[self.X])
    self.dump_dram("X%d" % which, lambda r0, n: self.X[r0:r0 + n, :], NTOK, D, self.X)


Net.stage_glu = _stage_glu
Net.stage_outproj = _stage_outproj


def _stage_moe(self, l):
    P = self.P
    ident = self.cst("ident"); lt = self.cst("lt")
    NE = 16
    with P.scope():
        modt = {}
        for who, row in (("lat", 0), ("ctx", 1)):
            scp = P.sb("mscp", [128, D]); sh = P.sb("msh", [128, D])
            self.mod_load("sp", scp, l, row, 4)
            self.mod_load("pool", sh, l, row, 3)
            modt[who] = (scp, sh)
        rw = P.sb("mrw", [128, 16, 16])
        P.dma("sp", rw[:], self.router_w[l].rearrange("(k p) e -> p k e", p=128), reads=[self.router_w], writes=[rw])
        x = [P.sb("mx", [128, D]) for _ in range(2)]; hb = [P.sb("mhb", [128, D], BF16) for _ in range(2)]
        hT = P.sb("mhT", [128, 16, 128]); pt = [P.ps("mpt", [128, 512]) for _ in range(2)]; pl = P.ps("mpl", [128, 512])
        af = P.sb("maf", [128, 16]); st = P.sb("mst", [128, 4]); aft = P.sb("maft", [16, 128])
        for i in range(NT):
            cols = slice(i * 128, (i + 1) * 128)
            x_ = x[i % 2]; h_ = hb[i % 2]
            scp, sh = modt["ctx" if i < 2 else "lat"]
            P.dma("sp", x_[:], self.X[cols, :], reads=[self.X], writes=[x_])
            P.tt("dve", x_[:], x_[:], scp[:], ALU.mult, [x_, scp], [x_])
            P.tt("pool", x_[:], x_[:], sh[:], ALU.add, [x_, sh], [x_])
            P.act(h_[:], x_[:], AF.Copy, [x_], [h_])
            P.dma("pool", self.H2[cols, :], h_[:], reads=[h_], writes=[self.H2])
            for kg in range(4):
                p_ = pt[kg % 2]
                for kk in range(4):
                    kc = kg * 4 + kk
                    P.mm(p_[:, kk * 128:(kk + 1) * 128], x_[:, kc * 128:(kc + 1) * 128], ident, True, True, [x_, self.C], [p_])
                P.act(hT[:, kg * 4:(kg + 1) * 4, :].rearrange("p k n -> p (k n)"), p_[:], AF.Copy, [p_], [hT])
            for kc in range(16):
                P.mm(pl[:, 0:16], hT[:, kc, :], rw[:, kc, :], kc == 0, kc == 15, [hT, rw], [pl])
            P.op("dve", lambda e: e.tensor_reduce(out=st[:, 0:1], in_=pl[:, 0:16], axis=AX.X, op=ALU.max), reads=[pl], writes=[st])
            P.ts("dve", st[:, 1:2], st[:, 0:1], -1.0, None, ALU.mult, None, [st], [st])
            P.act(af[:], pl[:, 0:16], AF.Exp, [pl, st], [af], bias=st[:, 1:2])
            P.op("dve", lambda e: e.tensor_reduce(out=st[:, 2:3], in_=af[:], axis=AX.X, op=ALU.add), reads=[af], writes=[st])
            P.op("dve", lambda e: e.reciprocal(out=st[:, 3:4], in_=st[:, 2:3]), reads=[st], writes=[st])
            P.ts("dve", af[:], af[:], st[:, 3:4], None, ALU.mult, None, [af, st], [af])
            P.dma("sp", self.AFF[cols, :], af[:], reads=[af], writes=[self.AFF])
            P.mm(pl[0:16, 128:256], af[:], ident, True, True, [af, self.C], [pl])
            P.act(aft[:], pl[0:16, 128:256], AF.Copy, [pl], [aft])
            P.dma("sp", self.AFFT[:, cols], aft[:], reads=[aft], writes=[self.AFFT])
    with P.scope():
        affall = P.sb("raff", [128, NT, 16]); rank = P.sb("rrank", [128, NT, 16])
        P.dma("sp", affall[:], self.AFF[:, :].rearrange("(t p) e -> p t e", p=128), reads=[self.AFF], writes=[affall])
        arow = [P.sb("rarow", [128, NLAT]) for _ in range(2)]; junk = P.sb("rjunk", [128, NLAT]); j2 = P.sb("rj2", [128, 128])
        c4 = [P.sb("rc4", [128, 4]) for _ in range(2)]
        pl = P.ps("rpl", [128, 512]); rkt = P.sb("rrkt", [16, 128])
        P.op("pool", lambda g: g.memset(rank[:], 0.0), writes=[rank])
        for tiles, col0 in (([0, 1], 0), (list(range(2, NT)), NCTX)):
            n = len(tiles) * 128
            for e in range(NEL):
                ar = arow[e % 2]
                P.dma("sp" if e % 2 == 0 else "pool", ar[:, 0:n], self.AFFT[e:e + 1, col0:col0 + n].partition_broadcast(128),
                      reads=[self.AFFT], writes=[ar])
                for tl, t in enumerate(tiles):
                    sc = affall[:, t, e:e + 1]
                    c = c4[tl % 2]
                    P.op("pool", lambda g, c=c: g.memset(c[:], 0.0), writes=[c])
                    b0 = tl * 128
                    if tl > 0:
                        P.ts("dve", junk[:, 0:b0], ar[:, 0:b0], sc, 0.0, ALU.is_ge, ALU.add, [ar, affall], [junk, c], accum=c[:, 0:1])
                    if b0 + 128 < n:
                        P.ts("dve", junk[:, b0 + 128:n], ar[:, b0 + 128:n], sc, 0.0, ALU.is_gt, ALU.add, [ar, affall], [junk, c], accum=c[:, 1:2])
                    P.ts("dve", junk[:, b0:b0 + 128], ar[:, b0:b0 + 128], sc, 0.0, ALU.is_gt, ALU.add, [ar, affall], [junk, c], accum=c[:, 2:3])
                    P.stt("dve", j2[:], ar[:, b0:b0 + 128], sc, lt, ALU.is_equal, ALU.mult, [ar, affall, self.C], [j2])
                    P.op("dve", lambda g, c=c: g.tensor_reduce(out=c[:, 3:4], in_=j2[:], axis=AX.X, op=ALU.add), reads=[j2], writes=[c])
                    P.op("dve", lambda g, c=c, t=t, e=e: g.tensor_reduce(out=rank[:, t, e:e + 1], in_=c[:], axis=AX.X, op=ALU.add), reads=[c], writes=[rank])
        P.dma("sp", self.RANKD[:, :].rearrange("(t p) e -> p t e", p=128), rank[:], reads=[rank], writes=[self.RANKD])
        for t in range(NT):
            P.mm(pl[0:16, 0:128], rank[:, t, :], ident, True, True, [rank, self.C], [pl])
            P.act(rkt[:], pl[0:16, 0:128], AF.Copy, [pl], [rkt])
            P.dma("sp", self.RANKT[:, t * 128:(t + 1) * 128], rkt[:], reads=[rkt], writes=[self.RANKT])
    with P.scope():
        rk = P.sb("erk", [128, NT, 16])
        P.dma("sp", rk[:], self.RANKD[:, :].rearrange("(t p) e -> p t e", p=128), reads=[self.RANKD], writes=[rk])
        perm = P.sb("eperm", [128, 4, 512], BF16); permc = P.sb("epermc", [128, 2, 32], BF16)
        w1b = P.sb("ew1", [128, 16, 1024], BF16); w3b = P.sb("ew3", [128, 16, 1024], BF16); w2b = P.sb("ew2", [128, 8, D], BF16)
        stg = [P.sb("estg", [128, D]) for _ in range(2)]
        HsT = P.sb("eHsT", [128, 16, 544], BF16); hidT = P.sb("ehid", [128, 8, 544], BF16)
        h2t = [P.sb("eh2t", [128, 1024], BF16) for _ in range(3)]
        rrow = P.sb("errow", [128, NLAT]); arow = P.sb("earow", [128, NLAT]); g2_ = P.sb("eg2", [128, 2])
        gate = P.sb("egate", [128, 8]); sa = P.sb("esa", [128, 512]); yg = [P.sb("eyg", [128, D], BF16) for _ in range(2)]
        pg = [P.ps("epg", [128, 512]) for _ in range(8)]
        for e in range(NEL):
            ld = 0
            for src, dst, nk, w in ((self.w1, w1b, 16, 1024), (self.w3, w3b, 16, 1024), (self.w2, w2b, 8, D)):
                for kc in range(nk):
                    s_ = stg[ld % 2]
                    P.dma("sp" if ld % 2 == 0 else "pool", s_[:, 0:w], src[l][e][kc * 128:(kc + 1) * 128, :], reads=[src], writes=[s_])
                    if ld % 2 == 0:
                        P.act(dst[:, kc, :], s_[:, 0:w], AF.Copy, [s_], [dst])
                    else:
                        P.op("pool", lambda g, dst=dst, kc=kc, s_=s_, w=w: g.tensor_copy(out=dst[:, kc, :], in_=s_[:, 0:w]), reads=[s_], writes=[dst])
                    ld += 1
            for t in range(2):
                P.ts("dve", permc[:, t, :], self.iota[:, 0:32], rk[:, t, e:e + 1], None, ALU.is_equal, None, [self.iota, rk], [permc])
            P.dma("sp", rrow[:], self.RANKT[e:e + 1, NCTX:NTOK].partition_broadcast(128), reads=[self.RANKT], writes=[rrow])
            P.dma("pool", arow[:], self.AFFT[e:e + 1, NCTX:NTOK].partition_broadcast(128), reads=[self.AFFT], writes=[arow])
            for sbk in range(4):
                for hf in range(2):
                    hs = slice(hf * 2048, (hf + 1) * 2048)
                    P.stt("dve", stg[hf][:], rrow[:, hs], self.iotc[:, sbk:sbk + 1], arow[:, hs], ALU.is_equal, ALU.mult, [rrow, arow, self.iotc], [stg[hf]])
                    P.op("dve", lambda g, hf=hf: g.tensor_reduce(out=g2_[:, hf:hf + 1], in_=stg[hf][:], axis=AX.X, op=ALU.add), reads=[stg[hf]], writes=[g2_])
                P.tt("dve", gate[:, sbk:sbk + 1], g2_[:, 0:1], g2_[:, 1:2], ALU.add, [g2_], [gate])
            P.dma("sp", rrow[:, 0:NCTX], self.RANKT[e:e + 1, 0:NCTX].partition_broadcast(128), reads=[self.RANKT], writes=[rrow])
            P.dma("pool", arow[:, 0:NCTX], self.AFFT[e:e + 1, 0:NCTX].partition_broadcast(128), reads=[self.AFFT], writes=[arow])
            P.stt("dve", stg[0][:, 0:NCTX], rrow[:, 0:NCTX], self.iotc[:, 0:1], arow[:, 0:NCTX], ALU.is_equal, ALU.mult, [rrow, arow, self.iotc], [stg[0]])
            P.op("dve", lambda g: g.tensor_reduce(out=gate[:, 4:5], in_=stg[0][:, 0:NCTX], axis=AX.X, op=ALU.add), reads=[stg[0]], writes=[gate])
            for kg in range(2):
                for t in range(32):
                    h_ = h2t[t % 3]
                    P.dma("sp" if t % 2 == 0 else "pool", h_[:], self.H2[NCTX + t * 128:NCTX + (t + 1) * 128, kg * 1024:(kg + 1) * 1024],
                          reads=[self.H2], writes=[h_])
                    pm_ = perm[:, t % 4, :]
                    P.ts("dve" if t % 2 == 0 else "pool", pm_, self.iota[:, 0:512], rk[:, 2 + t, e:e + 1], None, ALU.is_equal, None,
                         [self.iota, rk], [perm])
                    for kk in range(8):
                        P.mm(pg[kk][:], h_[:, kk * 128:(kk + 1) * 128], pm_, t == 0, t == 31, [h_, perm], [pg[kk]])
                for kk in range(8):
                    P.act(HsT[:, kg * 8 + kk, 0:512], pg[kk][:], AF.Copy, [pg[kk]], [HsT])
                for t in range(2):
                    h_ = h2t[t % 3]
                    P.dma("sp", h_[:], self.H2[t * 128:(t + 1) * 128, kg * 1024:(kg + 1) * 1024], reads=[self.H2], writes=[h_])
                    for kk in range(8):
                        P.mm(pg[kk][:, 0:32], h_[:, kk * 128:(kk + 1) * 128], permc[:, t, :], t == 0, t == 1, [h_, permc], [pg[kk]])
                for kk in range(8):
                    P.act(HsT[:, kg * 8 + kk, 512:544], pg[kk][:, 0:32], AF.Copy, [pg[kk]], [HsT])
            q = 0
            for fc in range(8):
                for s0, n in ((0, 512), (512, 32)):
                    p1 = pg[(2 * q) % 8]; p3 = pg[(2 * q + 1) % 8]; q += 1
                    for kc in range(16):
                        P.mm(p1[:, 0:n], w1b[:, kc, fc * 128:(fc + 1) * 128], HsT[:, kc, s0:s0 + n], kc == 0, kc == 15, [w1b, HsT], [p1])
                    for kc in range(16):
                        P.mm(p3[:, 0:n], w3b[:, kc, fc * 128:(fc + 1) * 128], HsT[:, kc, s0:s0 + n], kc == 0, kc == 15, [w3b, HsT], [p3])
                    P.act(sa[:, 0:n], p1[:, 0:n], AF.Silu, [p1], [sa])
                    P.tt("dve", hidT[:, fc, s0:s0 + n], sa[:, 0:n], p3[:, 0:n], ALU.mult, [sa, p3], [hidT])
            q = 0
            for sbk in range(5):
                nr = 128 if sbk < 4 else 32
                s0 = sbk * 128
                y_ = yg[sbk % 2]
                for cb in range(4):
                    p_ = pg[q % 8]; q += 1
                    for fc in range(8):
                        P.mm(p_[0:nr, :], hidT[:, fc, s0:s0 + nr], w2b[:, fc, cb * 512:(cb + 1) * 512], fc == 0, fc == 7, [hidT, w2b], [p_])
                    P.ts("dve", y_[0:nr, cb * 512:(cb + 1) * 512], p_[0:nr, :], gate[0:nr, sbk:sbk + 1], None, ALU.mult, None, [p_, gate], [y_])
                P.dma("sp", self.YG[e][s0:s0 + nr, :], y_[0:nr, :], reads=[y_], writes=[self.YG])
    with P.scope():
        ygs = P.sb("cygs", [128, NEL, 4, 512], BF16); ygc = P.sb("cygc", [32, NEL, 512], BF16)
        rr = [P.sb("crr", [128, NEL, 128]) for _ in range(2)]; pT = [P.sb("cpT", [128, 4, NEL, 128], BF16) for _ in range(2)]
        pf = [P.ps("cpf", [128, 512]) for _ in range(2)]; fo = [P.sb("cfo", [128, 512]) for _ in range(2)]
        for cb in range(4):
            cs_ = slice(cb * 512, (cb + 1) * 512)
            for e in range(NEL):
                P.dma("sp" if e % 2 == 0 else "pool", ygs[:, e], self.YG[e][0:512, cs_].rearrange("(s p) n -> p s n", p=128), reads=[self.YG], writes=[ygs])
                P.dma("sp", ygc[:, e, :], self.YG[e][512:544, cs_], reads=[self.YG], writes=[ygc])
            for t in range(NT):
                cols = slice(t * 128, (t + 1) * 128)
                r_ = rr[t % 2]; p_ = pT[t % 2]; f_ = pf[t % 2]; o_ = fo[t % 2]
                P.dma("sp" if t % 2 == 0 else "pool", r_[:], self.RANKT[0:NEL, cols].partition_broadcast(128), reads=[self.RANKT], writes=[r_])
                if t >= 2:
                    for sbk in range(4):
                        P.ts("dve" if sbk % 2 == 0 else "pool", p_[:, sbk], r_[:], self.iotc[:, sbk:sbk + 1], None, ALU.is_equal, None, [r_, self.iotc], [p_])
                    k = 0
                    for e in range(NEL):
                        for sbk in range(4):
                            P.mm(f_[:], p_[:, sbk, e, :], ygs[:, e, sbk, :], k == 0, k == 4 * NEL - 1, [p_, ygs], [f_])
                            k += 1
                else:
                    P.ts("dve", p_[:, 0], r_[:], self.iotc[:, 0:1], None, ALU.is_equal, None, [r_, self.iotc], [p_])
                    for e in range(NEL):
                        P.mm(f_[:], p_[0:32, 0, e, :], ygc[0:32, e, :], e == 0, e == NEL - 1, [p_, ygc], [f_])
                P.act(o_[:], f_[:], AF.Copy, [f_], [o_])
                P.dma("sp", self.FFP[cols, cs_], o_[:], reads=[o_], writes=[self.FFP])
    for t in range(NT):
        P.coll("AllReduce", ALU.add, RG, self.FFP[t * 128:(t + 1) * 128, :], self.FF[t * 128:(t + 1) * 128, :], [self.FFP], [self.FF])
    self.dump_dram("FF", lambda r0, n: self.FF[r0:r0 + n, :], NTOK, D, self.FF)
    _stage_resln(self, l, 2)


Net.stage_moe = _stage_moe
```

```python
import numpy as np
from contextlib import ExitStack, contextmanager
import concourse.bass as bass
import concourse.mybir as mybir
from concourse.bass_utils import run_bass_kernel_spmd

F32 = mybir.dt.float32
BF16 = mybir.dt.bfloat16
ALU = mybir.AluOpType
AF = mybir.ActivationFunctionType
AX = mybir.AxisListType

D = 2048
NTOK = 4352
NT = 34
NCTX = 256
NLAT = 4096
DEPTH = 4
ALPHA = (2 * DEPTH) ** 0.25
NIN = 5152
R = 4
UC = 1024 // R
NGP = 32 // R
NH = 8 // R
NEL = 16 // R
DC = D // R
ABC = 4 * NH
NINL = 5 * UC + ABC
RG = [[0, 1, 2, 3], [4, 5, 6, 7]]
CONST_NAMES = ["ident", "J", "ones", "tri_f", "tri_b", "ms_f", "ms_b", "mi_f", "mi_b", "blk_a", "blk_b", "blk_d", "lt"]


def make_consts():
    i = np.arange(128)
    s, j = np.meshgrid(i, i, indexing="ij")
    same = (s // 64) == (j // 64)
    c = {}
    c["ident"] = (s == j)
    c["J"] = (s + j == 127)
    c["ones"] = np.ones((128, 128), bool)
    c["tri_f"] = same & (s <= j)
    c["tri_b"] = same & (s >= j)
    c["ms_f"] = same & (s < j)
    c["ms_b"] = same & (s > j)
    c["mi_f"] = same & (s <= j)
    c["mi_b"] = same & (s >= j)
    c["blk_a"] = (s < 64)
    c["blk_b"] = (s >= 64)
    c["blk_d"] = same
    c["lt"] = (j < s)
    return np.concatenate([c[n].astype(np.float32) for n in CONST_NAMES], axis=1)


class T:
    __slots__ = ("h", "lw", "rd", "name")

    def __init__(self, h, name=""):
        self.h = h
        self.lw = None
        self.rd = []
        self.name = name

    def __getitem__(self, k):
        return self.h[k]


class Prog:
    NDMA = 8
    SAME_ENG = True

    def __init__(self, nc, es):
        self.nc = nc
        self.stack = [es]
        self.eng = {"pe": nc.tensor, "act": nc.scalar, "dve": nc.vector, "pool": nc.gpsimd, "sp": nc.sync}
        self.sem = {}
        self.cnt = {}
        for e in ("pe", "act", "dve", "pool"):
            self.sem[e] = es.enter_context(nc.semaphore("s_" + e))
            self.cnt[e] = 0
        self.dq = {}
        for q in ("sp", "act", "pool"):
            sems = []
            for i in range(self.NDMA):
                k = "d_%s_%d" % (q, i)
                self.sem[k] = es.enter_context(nc.semaphore(k))
                self.cnt[k] = 0
                sems.append(k)
            self.dq[q] = [sems, 0]
        self.known = {e: {} for e in self.eng}
        self.ninst = 0
        self.uid = 0

    @contextmanager
    def scope(self):
        es = ExitStack()
        self.stack.append(es)
        try:
            yield
        finally:
            self.barrier()
            self.stack.pop()
            es.close()

    def _nm(self, name):
        self.uid += 1
        return "%s_%d" % (name, self.uid)

    def sb(self, name, shape, dt=F32):
        return T(self.stack[-1].enter_context(self.nc.sbuf_tensor(self._nm(name), list(shape), dt)), name)

    def ps(self, name, shape, dt=F32):
        return T(self.stack[-1].enter_context(self.nc.psum_tensor(self._nm(name), list(shape), dt)), name)

    def dram(self, name, shape, dt=F32, kind="Internal"):
        return T(self.nc.dram_tensor(name, list(shape), dt, kind=kind), name)

    def _wait(self, e, dep):
        if dep is None:
            return
        k, v = dep
        if v <= 0 or self.known[e].get(k, 0) >= v:
            return
        self.eng[e].wait_ge(self.sem[k], v)
        self.known[e][k] = v

    def _skip(self, e, key):
        return key == e and (e == "pe" or not self.SAME_ENG)

    def _deps(self, e, reads, writes):
        for t in reads:
            if t.lw is not None and not self._skip(e, t.lw[0]):
                self._wait(e, t.lw)
        for t in writes:
            if t.lw is not None and not self._skip(e, t.lw[0]):
                self._wait(e, t.lw)
            for d in t.rd:
                if not self._skip(e, d[0]):
                    self._wait(e, d)

    def _mark(self, key, val, reads, writes):
        for t in reads:
            t.rd.append((key, val))
            if len(t.rd) > 32:
                m = {}
                for k, v in t.rd:
                    if m.get(k, 0) < v:
                        m[k] = v
                t.rd = list(m.items())
        for t in writes:
            t.lw = (key, val)
            t.rd = []

    def op(self, e, fn, reads=(), writes=()):
        self._deps(e, reads, writes)
        ins = fn(self.eng[e])
        self.cnt[e] += 1
        ins.then_inc(self.sem[e], 1)
        self._mark(e, self.cnt[e], reads, writes)
        self.ninst += 1
        return ins

    def dma(self, q, out, in_, reads=(), writes=(), **kw):
        sems, i = self.dq[q]
        k = sems[i % self.NDMA]
        self.dq[q][1] = i + 1
        self._wait(q, (k, self.cnt[k]))
        self._deps(q, reads, writes)
        ins = self.eng[q].dma_start(out=out, in_=in_, **kw)
        self.cnt[k] += 16
        ins.then_inc(self.sem[k], 16)
        self._mark(k, self.cnt[k], reads, writes)
        self.ninst += 1
        return ins

    def coll(self, kind, op, rg, in_ap, out_ap, reads, writes):
        q = "pool"
        k = "cc"
        if k not in self.sem:
            self.sem[k] = self.stack[0].enter_context(self.nc.semaphore("s_cc"))
            self.cnt[k] = 0
        self._wait(q, (k, self.cnt[k]))
        self._deps(q, reads, writes)
        ins = self.nc.gpsimd.collective_compute(kind, op, replica_groups=rg, ins=[in_ap], outs=[out_ap])
        self.cnt[k] += 1
        ins.then_inc(self.sem[k], 1)
        self._mark(k, self.cnt[k], reads, writes)
        self.ninst += 1
        return ins

    def barrier(self):
        for e in self.eng:
            for k in self.sem:
                self._wait(e, (k, self.cnt[k]))

    def mm(self, out_ap, lhsT_ap, rhs_ap, start, stop, reads, writes):
        return self.op("pe", lambda e: e.matmul(out_ap, lhsT=lhsT_ap, rhs=rhs_ap, start=start, stop=stop),
                       reads=reads, writes=writes)

    def act(self, out_ap, in_ap, func, reads, writes, **kw):
        return self.op("act", lambda e: e.activation(out=out_ap, in_=in_ap, func=func, **kw), reads=reads, writes=writes)

    def tt(self, e, out_ap, a_ap, b_ap, op, reads, writes):
        return self.op(e, lambda g: g.tensor_tensor(out=out_ap, in0=a_ap, in1=b_ap, op=op), reads=reads, writes=writes)

    def ts(self, e, out_ap, a_ap, s1, s2, op0, op1, reads, writes, accum=None):
        if op1 is None:
            return self.op(e, lambda g: g.tensor_scalar(out=out_ap, in0=a_ap, scalar1=s1, scalar2=None, op0=op0),
                           reads=reads, writes=writes)
        if accum is not None:
            return self.op(e, lambda g: g.tensor_scalar(out=out_ap, in0=a_ap, scalar1=s1, scalar2=s2, op0=op0, op1=op1,
                                                        accum_out=accum), reads=reads, writes=writes)
        return self.op(e, lambda g: g.tensor_scalar(out=out_ap, in0=a_ap, scalar1=s1, scalar2=s2, op0=op0, op1=op1),
                       reads=reads, writes=writes)

    def stt(self, e, out_ap, a_ap, s, b_ap, op0, op1, reads, writes):
        return self.op(e, lambda g: g.scalar_tensor_tensor(out=out_ap, in0=a_ap, scalar=s, in1=b_ap, op0=op0, op1=op1),
                       reads=reads, writes=writes)


RT = [1, 0] + [35 - j for j in range(2, NT)]
RTINV = {t: j for j, t in enumerate(RT)}


class Net:
    def __init__(self, nlayers=DEPTH, dump=None, upto=None, L=DEPTH, EW=NEL):
        self.L = L
        self.EW = EW
        self.nlayers = nlayers
        self.dump = dump or []
        self.upto = upto

    def build(self):
        nc = bass.Bass("TRN2", target_bir_lowering=False)
        self.nc = nc
        with ExitStack() as es:
            P = Prog(nc, es)
            self.P = P
            self.declare()
            self.body()
            P.barrier()
        return nc

    def declare(self):
        P = self.P
        L = self.L
        ext = lambda n, s: P.dram(n, s, F32, kind="ExternalInput")
        self.xin = ext("xin", [NTOK, D])
        self.cT = ext("cT", [128, 32])
        self.consts = ext("consts", [128, 128 * len(CONST_NAMES)])
        self.iotar = ext("iotar", [128, 544])
        self.iotac = ext("iotac", [128, 8])
        self.ada_w = ext("ada_w", [L, D, 6 * DC])
        self.ada_b = ext("ada_b", [L, 6 * DC])
        self.w_in = ext("w_in", [L, D, NINL])
        self.w_out = ext("w_out", [L, D, DC])
        self.lam_re = ext("s5_lam_re", [L, 2, 2 * NGP, 64])
        self.lam_im = ext("s5_lam_im", [L, 2, 2 * NGP, 64])
        self.log_step = ext("s5_log_step", [L, 2, 2 * NGP])
        self.b_re = ext("s5_b_re", [L, 2, 2 * NGP, 64, 16])
        self.b_im = ext("s5_b_im", [L, 2, 2 * NGP, 64, 16])
        self.c_re = ext("s5_c_re", [L, 2, 2 * NGP, 16, 64])
        self.c_im = ext("s5_c_im", [L, 2, 2 * NGP, 16, 64])
        self.s5_d = ext("s5_d", [L, UC])
        self.glu_w = ext("s5_glu_w", [L, 1024, UC])
        self.glu_b = ext("s5_glu_b", [L, UC])
        self.conv_w = ext("dn_conv_w", [L, 5, 3 * UC])
        self.a_log = ext("dn_a_log", [L, 2 * NH])
        self.dt_bias = ext("dn_dt_bias", [L, 2 * NH])
        self.norm_w = ext("dn_norm_w", [L, 128])
        self.ln1_g = ext("ln1_g", [L, D]); self.ln1_b = ext("ln1_b", [L, D])
        self.ln2_g = ext("ln2_g", [L, D]); self.ln2_b = ext("ln2_b", [L, D])
        self.router_w = ext("router_w", [L, D, 16])
        self.w1 = ext("exp_w1", [L, self.EW, D, 1024])
        self.w3 = ext("exp_w3", [L, self.EW, D, 1024])
        self.w2 = ext("exp_w2", [L, self.EW, 1024, D])
        self.yout = P.dram("yout", [NLAT, D], F32, kind="ExternalOutput")
        self.X = P.dram("X", [NTOK, D])
        self.MODL = P.dram("MODL", [L * 2, 6 * DC]); self.MOD = P.dram("MODF", [R * L * 2, 6 * DC])
        self.PT = P.dram("PT", [4 * UC, NTOK])
        self.UR = P.dram("UR", [UC, NTOK])
        self.Z = P.dram("Z", [NTOK, UC])
        self.AB = P.dram("AB", [NTOK, ABC])
        self.GT = P.dram("GT", [UC, NTOK]); self.GTF = P.dram("GTF", [NGP, R * 32, NTOK])
        self.S5TF = P.dram("S5TF", [UC // 64, R * 64, NTOK], BF16); self.DNOF = P.dram("DNOF", [R * NTOK, UC])
        self.MIXL = P.dram("MIXL", [NTOK, DC]); self.MIXF = P.dram("MIXF", [R * NTOK, DC]); self.FFP = P.dram("FFP", [NTOK, D])
        self.H2 = P.dram("H2", [NTOK, D], BF16); self.AFF = P.dram("AFF", [NTOK, 16]); self.AFFT = P.dram("AFFT", [16, NTOK])
        self.RANKD = P.dram("RANKD", [NTOK, 16]); self.RANKT = P.dram("RANKT", [16, NTOK])
        self.YG = P.dram("YG", [NEL, 544, D], BF16); self.FF = P.dram("FF", [NTOK, D]); self.FFt = [T(self.FF.h, "FFt") for _ in range(NT)]
        self.S5T = P.dram("S5T", [UC, NTOK], BF16)
        self.QT = P.dram("QT", [UC, NTOK]); self.KT = P.dram("KT", [UC, NTOK])
        self.KTOK = P.dram("KTOK", [NTOK, UC]); self.VTOK = P.dram("VTOK", [NTOK, UC])
        self.GB = P.dram("GB", [NTOK, 4 * NH]); self.OD = P.dram("OD", [2, NTOK, UC]); self.ODt = [T(self.OD.h, "OD0"), T(self.OD.h, "OD1")]; self.DNO = P.dram("DNO", [NTOK, UC])
        self.dbg = {}
        for n, shp in self.dump:
            self.dbg[n] = P.dram("dbg_" + n, shp, F32, kind="ExternalOutput")

    def cst(self, name):
        i = CONST_NAMES.index(name)
        return self.C[:, i * 128:(i + 1) * 128]

    def body(self):
        P = self.P
        with P.scope():
            self.C = P.sb("C", [128, 128 * len(CONST_NAMES)])
            P.dma("sp", self.C[:], self.consts[:], reads=[self.consts], writes=[self.C])
            self.Cb = P.sb("Cb", [128, 256], BF16)
            P.op("dve", lambda e: e.tensor_copy(out=self.Cb[:], in_=self.C[:, 0:256]), reads=[self.C], writes=[self.Cb])
            self.iota = P.sb("iota", [128, 544])
            P.dma("sp", self.iota[:], self.iotar[:], reads=[self.iotar], writes=[self.iota])
            self.iotc = P.sb("iotc", [128, 8])
            P.dma("sp", self.iotc[:], self.iotac[:], reads=[self.iotac], writes=[self.iotc])
            self.copy_x()
            self.stage_mod()
            if self.upto == "mod":
                return
            for l in range(self.nlayers):
                self.stage_inproj(l)
                if self.upto == "inproj":
                    return
                self.stage_s5(l)
                if self.upto == "s5":
                    return
                self.stage_dn(l)
                if self.upto == "dn":
                    return
                self.stage_glu(l)
                self.stage_outproj(l)
                if self.upto == "outproj":
                    return
                self.stage_moe(l)
            self.write_out()

    def copy_x(self):
        P = self.P
        with P.scope():
            tb = [P.sb("cx", [128, D]) for _ in range(2)]
            for i in range(NT):
                t = tb[i % 2]
                P.dma("sp", t[:], self.xin[i * 128:(i + 1) * 128, :], reads=[self.xin], writes=[t])
                P.dma("pool", self.X[i * 128:(i + 1) * 128, :], t[:], reads=[t], writes=[self.X])

    def write_out(self):
        P = self.P
        with P.scope():
            tb = [P.sb("wo", [128, D]) for _ in range(2)]
            for i in range(2, NT):
                t = tb[i % 2]
                P.dma("sp", t[:], self.X[i * 128:(i + 1) * 128, :], reads=[self.X], writes=[t])
                P.dma("pool", self.yout[(i - 2) * 128:(i - 1) * 128, :], t[:], reads=[t], writes=[self.yout])

    def dump_dram(self, name, src_ap_fn, rows, cols, src):
        if name not in self.dbg:
            return
        P = self.P
        dst = self.dbg[name]
        with P.scope():
            tb = [P.sb("dd", [128, cols]) for _ in range(2)]
            for r0 in range(0, rows, 128):
                n = min(128, rows - r0)
                t = tb[(r0 // 128) % 2]
                P.dma("sp", t[0:n, :], src_ap_fn(r0, n), reads=[src], writes=[t])
                P.dma("sp", dst[r0:r0 + n, :], t[0:n, :], reads=[t], writes=[dst])

    def stage_mod(self):
        P = self.P
        W = 6 * DC
        with P.scope():
            ct = P.sb("ct", [128, 32])
            P.dma("sp", ct[:], self.cT[:], reads=[self.cT], writes=[ct])
            sc = P.sb("sc", [128, 32])
            P.act(sc[:], ct[:], AF.Silu, [ct], [sc])
            wb = [P.sb("adw", [128, 16, 512]) for _ in range(2)]
            pm = [P.ps("pmod", [2, 512]) for _ in range(2)]
            row = P.sb("modrow", [2, W])
            bia = P.sb("modb", [2, W])
            for l in range(self.nlayers):
                P.dma("pool", bia[:], self.ada_b[l:l + 1, :].partition_broadcast(2), reads=[self.ada_b], writes=[bia])
                for nb in range(W // 512):
                    w = wb[nb % 2]
                    p = pm[nb % 2]
                    src = self.ada_w[l].rearrange("(k p) n -> p k n", p=128)[:, :, nb * 512:(nb + 1) * 512]
                    P.dma("sp" if nb % 2 == 0 else "pool", w[:], src, reads=[self.ada_w], writes=[w])
                    for kc in range(16):
                        P.mm(p[:], sc[:, kc:32:16], w[:, kc, :], kc == 0, kc == 15, [sc, w], [p])
                    P.tt("dve", row[:, nb * 512:(nb + 1) * 512], p[:], bia[:, nb * 512:(nb + 1) * 512], ALU.add,
                         [p, bia], [row])
                for sg in (1, 4):
                    P.ts("dve", row[:, sg * DC:(sg + 1) * DC], row[:, sg * DC:(sg + 1) * DC], 1.0, None, ALU.add, None, [row], [row])
                P.dma("sp", self.MODL[l * 2:(l + 1) * 2, :], row[:], reads=[row], writes=[self.MODL])
            P.coll("AllGather", ALU.bypass, RG, self.MODL[:, :], self.MOD[:, :], [self.MODL], [self.MOD])

    def mod_load(self, q, dst, l, row, seg):
        v = self.MOD[:, :].rearrange("(r l two) (s j) -> l two s r j", r=R, l=self.L, two=2, s=6)[l][row][seg]
        self.P.dma(q, dst[:, :].rearrange("p (r j) -> p r j", r=R), v.partition_broadcast(128), reads=[self.MOD], writes=[dst])

    def bcast_load(self, q, dst, src_row_ap, src_t):
        self.P.dma(q, dst[:], src_row_ap.partition_broadcast(128), reads=[src_t], writes=[dst])

    def stage_inproj(self, l):
        P = self.P
        fwd_tiles = list(range(NT))
        passes = [(fwd_tiles[:12], False, 0), (fwd_tiles[12:24], False, 12 * 128), (fwd_tiles[24:], False, 24 * 128),
                  (RT[:12], True, 0), (RT[12:24], True, 12 * 128), (RT[24:], True, 24 * 128)]
        with P.scope():
            modt = {}
            for who, row in (("lat", 0), ("ctx", 1)):
                scp = P.sb("scp", [128, D]); sh = P.sb("sh", [128, D])
                self.mod_load("sp", scp, l, row, 1)
                self.mod_load("pool", sh, l, row, 0)
                modt[who] = (scp, sh)
            hinT = P.sb("hinT", [128, 16, 12 * 128], BF16)
            xt = [P.sb("xt", [128, D]) for _ in range(2)]
            hb = [P.sb("hb", [128, D], BF16) for _ in range(2)]
            ptr = [P.ps("ptr", [128, 512]) for _ in range(2)]
            pacc = [P.ps("pacc", [128, 512]) for _ in range(3)]
            wst = [P.sb("wst", [128, 16, 128]) for _ in range(2)]
            wbf = [P.sb("wbf", [128, 16, 128], BF16) for _ in range(2)]
            wst2 = P.sb("wst2", [128, 16, 256])
            wbf2 = P.sb("wbf2", [128, 16, 256], BF16)
            ot = [P.sb("ot", [128, 12 * 128]) for _ in range(2)]
            ot2 = [P.sb("ot2", [128, 512]) for _ in range(2)]
            win = self.w_in[l].rearrange("(k p) n -> p k n", p=128)
            for tiles, rev, col0 in passes:
                for ti, t in enumerate(tiles):
                    x = xt[ti % 2]; h = hb[ti % 2]
                    scp, sh = modt["ctx" if t < 2 else "lat"]
                    P.dma("sp" if ti % 2 == 0 else "pool", x[:], self.X[t * 128:(t + 1) * 128, :], reads=[self.X], writes=[x])
                    P.tt("dve", x[:], x[:], scp[:], ALU.mult, [x, scp], [x])
                    P.tt("pool", h[:], x[:], sh[:], ALU.add, [x, sh], [h])
                    idm = self.Cb[:, 128:256] if rev else self.Cb[:, 0:128]
                    for kg in range(4):
                        pt = ptr[kg % 2]
                        for kk in range(4):
                            kc = kg * 4 + kk
                            P.mm(pt[:, kk * 128:(kk + 1) * 128], h[:, kc * 128:(kc + 1) * 128], idm, True, True, [h, self.Cb], [pt])
                        P.act(hinT[:, kg * 4:(kg + 1) * 4, ti * 128:(ti + 1) * 128],
                              pt[:].rearrange("p (k t) -> p k t", k=4), AF.Copy, [pt], [hinT])
                ntk = len(tiles) * 128
                noc = (UC // 128) if rev else (4 * UC // 128)
                dst = self.UR if rev else self.PT
                for oc in range(noc):
                    ws = wst[oc % 2]; wb = wbf[oc % 2]; o = ot[oc % 2]
                    P.dma("sp" if oc % 2 == 0 else "pool", ws[:], win[:, :, oc * 128:(oc + 1) * 128], reads=[self.w_in], writes=[ws])
                    P.op("pool", lambda e, wb=wb, ws=ws: e.tensor_copy(out=wb[:], in_=ws[:]), reads=[ws], writes=[wb])
                    for tb in range(0, ntk, 512):
                        n = min(512, ntk - tb)
                        pa = pacc[(tb // 512) % 3]
                        for kc in range(16):
                            P.mm(pa[:, 0:n], wb[:, kc, :], hinT[:, kc, tb:tb + n], kc == 0, kc == 15, [wb, hinT], [pa])
                        P.act(o[:, tb:tb + n], pa[:, 0:n], AF.Copy, [pa], [o])
                    P.dma("sp", dst[oc * 128:(oc + 1) * 128, col0:col0 + ntk], o[:, 0:ntk], reads=[o], writes=[dst])
                if rev:
                    continue
                nzb = UC // 256
                for zb in range(nzb + 1):
                    c0 = 4 * UC + zb * 256
                    ncol = 256 if zb < nzb else ABC
                    P.dma("sp", wst2[:, :, 0:ncol], win[:, :, c0:c0 + ncol], reads=[self.w_in], writes=[wst2])
                    P.op("pool", lambda e, ncol=ncol: e.tensor_copy(out=wbf2[:, :, 0:ncol], in_=wst2[:, :, 0:ncol]), reads=[wst2], writes=[wbf2])
                    for ti, t in enumerate(tiles):
                        pa = pacc[ti % 3]; o2 = ot2[ti % 2]
                        for kc in range(16):
                            P.mm(pa[:, 0:ncol], hinT[:, kc, ti * 128:(ti + 1) * 128], wbf2[:, kc, 0:ncol], kc == 0, kc == 15, [wbf2, hinT], [pa])
                        P.act(o2[:, 0:ncol], pa[:, 0:ncol], AF.Copy, [pa], [o2])
                        if zb < nzb:
                            P.dma("pool", self.Z[t * 128:(t + 1) * 128, zb * 256:(zb + 1) * 256], o2[:, 0:256], reads=[o2], writes=[self.Z])
                        else:
                            P.dma("pool", self.AB[t * 128:(t + 1) * 128, :], o2[:, 0:ABC], reads=[o2], writes=[self.AB])


def prep_inputs(inputs, b, r=0, L=DEPTH, EW=None):
    f = lambda a: np.ascontiguousarray(np.asarray(a, dtype=np.float32))
    g = lambda k: np.asarray(inputs[k])[:L]
    m = {}
    m["xin"] = f(np.concatenate([inputs["ctx"][b], inputs["x"][b]], axis=0))
    cT = np.concatenate([np.asarray(inputs["c"][b]).reshape(16, 128).T, np.asarray(inputs["c_ctx"]).reshape(16, 128).T], axis=1)
    m["cT"] = f(cT)
    m["consts"] = make_consts()
    m["iotar"] = f(np.tile(np.arange(544, dtype=np.float32)[None, :], (128, 1)))
    m["iotac"] = f(np.arange(128, dtype=np.float32)[:, None] + 128.0 * np.arange(8, dtype=np.float32)[None, :])
    for k in ["dn_norm_w", "ln1_g", "ln1_b", "ln2_g", "ln2_b"]:
        m[k] = f(g(k))
    aw = g("ada_w"); ab_ = g("ada_b")
    m["ada_w"] = f(np.concatenate([aw[:, :, sg * D + r * DC:sg * D + (r + 1) * DC] for sg in range(6)], axis=2))
    m["ada_b"] = f(np.concatenate([ab_[:, sg * D + r * DC:sg * D + (r + 1) * DC] for sg in range(6)], axis=1))
    cu = slice(r * UC, (r + 1) * UC)
    heads = list(range(r * NH, (r + 1) * NH))
    abcols = [5120 + d * 16 + k * 8 + h for d in range(2) for k in range(2) for h in heads]
    w_in = g("w_in")
    m["w_in"] = f(np.concatenate([w_in[:, :, j * 1024 + r * UC:j * 1024 + (r + 1) * UC] for j in range(5)] + [w_in[:, :, abcols]], axis=2))
    m["w_out"] = f(g("w_out")[:, :, r * DC:(r + 1) * DC])
    gs = slice(r * 2 * NGP, (r + 1) * 2 * NGP)
    for k in ["s5_lam_re", "s5_lam_im", "s5_log_step", "s5_b_re", "s5_b_im", "s5_c_re", "s5_c_im"]:
        m[k] = f(g(k)[:, :, gs])
    m["s5_d"] = f(g("s5_d")[:, cu])
    m["s5_glu_w"] = f(g("s5_glu_w")[:, :, cu])
    m["s5_glu_b"] = f(g("s5_glu_b")[:, cu])
    cw = g("dn_conv_w")
    m["dn_conv_w"] = f(np.concatenate([cw[:, :, j * 1024 + r * UC:j * 1024 + (r + 1) * UC] for j in range(3)], axis=2))
    m["dn_a_log"] = f(g("dn_a_log")[:, :, heads].reshape(L, 2 * NH))
    m["dn_dt_bias"] = f(g("dn_dt_bias")[:, :, heads].reshape(L, 2 * NH))
    eo = list(range(r * NEL, (r + 1) * NEL)) + [e for e in range(16) if not (r * NEL <= e < (r + 1) * NEL)]
    m["router_w"] = f(g("router_w")[:, :, eo])
    ne = NEL if EW is None else EW
    for k in ["exp_w1", "exp_w3", "exp_w2"]:
        m[k] = f(g(k)[:, r * NEL:r * NEL + ne])
    return m


def kernel(**inputs):
    net = Net()
    nc = net.build()
    in_maps = [prep_inputs(inputs, c // R, c % R) for c in range(2 * R)]
    res = run_bass_kernel_spmd(nc, in_maps, core_ids=list(range(2 * R)))
    return np.stack([np.asarray(res.results[b * R]["yout"], dtype=np.float32) for b in range(2)], axis=0)


MAGIC = 12582912.0
TWO_PI = 6.283185307179586


def _s5_prep(self, l):
    P = self.P
    ident = self.cst("ident")
    pr = {}
    pt = P.ps("s5pp", [128, 512])
    G = NGP
    pr["rho"] = [P.sb("s5rho", [128, G]) for _ in range(2)]; pr["f"] = [P.sb("s5f", [128, G]) for _ in range(2)]
    pr["BrT"] = [P.sb("s5BrT", [32, G, 128], BF16) for _ in range(2)]; pr["BiT"] = [P.sb("s5BiT", [32, G, 128], BF16) for _ in range(2)]
    pr["CTr"] = [P.sb("s5CTr", [128, G, 32], BF16) for _ in range(2)]; pr["CTi"] = [P.sb("s5CTi", [128, G, 32], BF16) for _ in range(2)]
    pr["dsk"] = P.sb("s5dsk", [32, G])

    def one_dir(d):
        rho = pr["rho"][d]; f = pr["f"][d]; BrT = pr["BrT"][d]; BiT = pr["BiT"][d]; CTr = pr["CTr"][d]; CTi = pr["CTi"][d]
        A = P.sb("s5A", [G, 3, 128])
        P.dma("sp", A[:, 0, :], self.lam_re[l][d].rearrange("(gp two) p -> gp (two p)", two=2), reads=[self.lam_re], writes=[A])
        P.dma("sp", A[:, 1, :], self.lam_im[l][d].rearrange("(gp two) p -> gp (two p)", two=2), reads=[self.lam_im], writes=[A])
        ls = P.sb("s5ls", [G, 2])
        P.dma("sp", ls[:], self.log_step[l][d:d + 1, :].rearrange("o (gp two) -> (o gp) two", two=2), reads=[self.log_step], writes=[ls])
        P.op("dve", lambda e, A=A, ls=ls: e.tensor_copy(out=A[:, 2, :].rearrange("g (t p) -> g t p", t=2),
                                                        in_=ls[:, :].unsqueeze(2).to_broadcast([G, 2, 64])), reads=[ls], writes=[A])
        q = P.sb("s5q", [128, 3, G])
        for k in range(3):
            P.mm(pt[:, k * G:(k + 1) * G], A[:, k, :], ident[0:G, 0:G], True, True, [A, self.C], [pt])
        P.act(q[:].rearrange("p k g -> p (k g)"), pt[:, 0:3 * G], AF.Copy, [pt], [q])
        lr = q[:, 0, :]; li = q[:, 1, :]
        dl = P.sb("s5dl", [128, G])
        P.act(dl[:], q[:, 2, :], AF.Exp, [q], [dl])
        P.tt("dve", rho[:], lr, dl[:], ALU.mult, [q, dl], [rho])
        P.act(rho[:], rho[:], AF.Exp, [rho], [rho])
        P.tt("dve", f[:], li, dl[:], ALU.mult, [q, dl], [f])
        P.ts("dve", f[:], f[:], 1.0 / TWO_PI, None, ALU.mult, None, [f], [f])
        w = P.sb("s5w", [128, 6, G])
        for k, off in ((0, 0.0), (1, 0.25)):
            P.ts("dve", w[:, 2, :], f[:], off, None, ALU.add, None, [f], [w])
            P.ts("dve", w[:, 3, :], w[:, 2, :], MAGIC, MAGIC, ALU.add, ALU.subtract, [w], [w])
            P.tt("dve", w[:, 2, :], w[:, 2, :], w[:, 3, :], ALU.subtract, [w], [w])
            P.act(w[:, k, :], w[:, 2, :], AF.Sin, [w], [w], scale=TWO_PI)
        sn = w[:, 0, :]; cs = w[:, 1, :]
        nr = P.sb("s5nr", [128, G]); ni = P.sb("s5ni", [128, G]); den = P.sb("s5den", [128, G])
        cr = P.sb("s5cr", [128, G]); ci = P.sb("s5ci", [128, G]); tmp = P.sb("s5tmp", [128, G])
        P.tt("dve", nr[:], rho[:], cs, ALU.mult, [rho, w], [nr])
        P.ts("dve", nr[:], nr[:], -1.0, None, ALU.add, None, [nr], [nr])
        P.tt("dve", ni[:], rho[:], sn, ALU.mult, [rho, w], [ni])
        P.tt("dve", den[:], lr, lr, ALU.mult, [q], [den])
        P.tt("dve", tmp[:], li, li, ALU.mult, [q], [tmp])
        P.tt("dve", den[:], den[:], tmp[:], ALU.add, [den, tmp], [den])
        P.op("dve", lambda e, den=den: e.reciprocal(out=den[:], in_=den[:]), reads=[den], writes=[den])
        P.tt("dve", cr[:], nr[:], lr, ALU.mult, [nr, q], [cr])
        P.tt("dve", tmp[:], ni[:], li, ALU.mult, [ni, q], [tmp])
        P.tt("dve", cr[:], cr[:], tmp[:], ALU.add, [cr, tmp], [cr])
        P.tt("dve", cr[:], cr[:], den[:], ALU.mult, [cr, den], [cr])
        P.tt("dve", ci[:], ni[:], lr, ALU.mult, [ni, q], [ci])
        P.tt("dve", tmp[:], nr[:], li, ALU.mult, [nr, q], [tmp])
        P.tt("dve", ci[:], ci[:], tmp[:], ALU.subtract, [ci, tmp], [ci])
        P.tt("dve", ci[:], ci[:], den[:], ALU.mult, [ci, den], [ci])
        Bl = P.sb("s5Bl", [128, 2, G, 16])
        P.dma("sp", Bl[:, 0], self.b_re[l][d].rearrange("(gp two) p c -> (two p) gp c", two=2), reads=[self.b_re], writes=[Bl])
        P.dma("pool", Bl[:, 1], self.b_im[l][d].rearrange("(gp two) p c -> (two p) gp c", two=2), reads=[self.b_im], writes=[Bl])
        S = P.sb("s5S", [128, 2, G, 32])
        P.op("pool", lambda e, S=S: e.memset(S[:], 0.0), writes=[S])
        t1 = P.sb("s5t1", [128, G, 16]); t2 = P.sb("s5t2", [128, G, 16])
        crb = cr[:, :].unsqueeze(2).to_broadcast([128, G, 16]); cib = ci[:, :].unsqueeze(2).to_broadcast([128, G, 16])
        for k, (a0, a1, op) in enumerate(((0, 1, ALU.subtract), (1, 0, ALU.add))):
            P.tt("dve", t1[:], Bl[:, a0], crb, ALU.mult, [Bl, cr], [t1])
            P.tt("dve", t2[:], Bl[:, a1], cib, ALU.mult, [Bl, ci], [t2])
            for half in range(2):
                ps_ = slice(half * 64, half * 64 + 64)
                P.tt("dve", S[ps_, k, :, half * 16:half * 16 + 16], t1[ps_], t2[ps_], op, [t1, t2], [S])
        for k, dst in ((0, BrT), (1, BiT)):
            for g4 in range(G // 4):
                for gg in range(4):
                    gp = g4 * 4 + gg
                    P.mm(pt[0:32, gg * 128:(gg + 1) * 128], S[:, k, gp, :], ident, True, True, [S, self.C], [pt])
                P.act(dst[:, g4 * 4:(g4 + 1) * 4, :], pt[0:32, :].rearrange("p (g s) -> p g s", g=4), AF.Copy, [pt], [dst])
        Cx = P.sb("s5Cx", [64, 2, G, 128])
        P.op("pool", lambda e, Cx=Cx: e.memset(Cx[:], 0.0), writes=[Cx])
        for k, src in ((0, self.c_re), (1, self.c_im)):
            v = src[l][d].rearrange("(gp two) c p -> two c gp p", two=2)
            P.dma("sp", Cx[0:16, k, :, 0:64], v[0], reads=[src], writes=[Cx])
            P.dma("pool", Cx[32:48, k, :, 64:128], v[1], reads=[src], writes=[Cx])
        for k, dst, scl in ((0, CTr, 1.0), (1, CTi, -1.0)):
            for g8 in range(G // 8):
                for gg in range(8):
                    gp = g8 * 8 + gg
                    P.mm(pt[:, gg * 64:(gg + 1) * 64], Cx[:, k, gp, :], ident[0:64, 0:64], True, True, [Cx, self.C], [pt])
                P.act(dst[:, g8 * 8:(g8 + 1) * 8, :].rearrange("p g (t c) -> p g t c", t=2),
                      pt[:, :].rearrange("p (g t c) -> p g t c", g=8, t=2)[:, :, :, 0:16], AF.Copy, [pt], [dst], scale=scl)
    for d in range(2):
        with P.scope():
            one_dir(d)
    dA = P.sb("s5dA", [G, 32])
    P.dma("sp", dA[:], self.s5_d[l:l + 1, :].rearrange("o (gp j) -> (o gp) j", j=32), reads=[self.s5_d], writes=[dA])
    P.mm(pt[0:32, 0:G], dA[:], ident[0:G, 0:G], True, True, [dA, self.C], [pt])
    dsk = pr["dsk"]
    P.act(dsk[:], pt[0:32, 0:G], AF.Copy, [pt], [dsk])
    return pr


def _stage_s5(self, l):
    P = self.P
    J = self.cst("J")
    with P.scope():
        pr = _s5_prep(self, l)
        H = [[P.sb("s5H", [128, NTOK], BF16) for _ in range(2)] for _ in range(2)]
        uf = [P.sb("s5uf", [32, NTOK]) for _ in range(2)]
        ub = [P.sb("s5ub", [32, NTOK], BF16) for _ in range(2)]
        DB = []
        for d in range(2):
            b = {"cos": P.sb("s5cos", [128, 513]), "sin": P.sb("s5sin", [128, 513]), "rho": P.sb("s5rhoT", [128, 512]),
                 "ph": P.sb("s5ph", [128, 513]), "rr": P.sb("s5rr", [128, 513]),
                 "px": [P.ps("s5px", [128, 512]) for _ in range(2)],
                 "tm": [P.sb("s5tm", [128, 512]) for _ in range(4)], "xt": [P.sb("s5xt", [128, 512]) for _ in range(2)],
                 "g": [[P.sb("s5g", [128, 512]) for _ in range(2)] for _ in range(2)], "ini": [P.sb("s5ini", [128, 4]) for _ in range(2)]}
            DB.append(b)
        pf = P.ps("s5pf", [32, 512]); pb = P.ps("s5pb", [128, 128]); pj = P.ps("s5pj", [32, 512])
        ybt = P.sb("s5ybt", [128, 32]); ybs = P.sb("s5ybs", [32, 512]); yy = P.sb("s5yy", [32, 512]); ww = P.sb("s5ww", [32, 512])
        GTt = P.sb("s5GT", [32, NTOK])
        srcs = [self.PT, self.UR]

        def dir_gen(d, gp):
            b = DB[d]
            cosT, sinT, rhoT, ph, rr, px, tm, xt_, gg_, ini = b["cos"], b["sin"], b["rho"], b["ph"], b["rr"], b["px"], b["tm"], b["xt"], b["g"], b["ini"]
            f = pr["f"][d]; rho = pr["rho"][d]
            P.ts("dve", ph[:], self.iota[:, 0:513], f[:, gp:gp + 1], None, ALU.mult, None, [self.iota, f], [ph])
            yield
            for dst, off in ((sinT, 0.0), (cosT, 0.25)):
                if off:
                    P.ts("dve", ph[:], ph[:], off, None, ALU.add, None, [ph], [ph])
                    yield
                P.ts("dve", rr[:], ph[:], MAGIC, MAGIC, ALU.add, ALU.subtract, [ph], [rr])
                yield
                P.tt("dve", rr[:], ph[:], rr[:], ALU.subtract, [ph, rr], [rr])
                yield
                P.act(dst[:], rr[:], AF.Sin, [rr], [dst], scale=TWO_PI)
                yield
            P.ts("dve", rhoT[:], self.iota[:, 0:512], 0.0, rho[:, gp:gp + 1], ALU.mult, ALU.add, [self.iota, rho], [rhoT])
            nch = 9
            for k in range(nch):
                c0 = k * 512
                n = min(512, NTOK - c0)
                xr = px[0]; xi = px[1]
                P.mm(xr[:, 0:n], pr["BrT"][d][:, gp, :], ub[d][:, c0:c0 + n], True, True, [pr["BrT"][d], ub[d]], [xr])
                P.mm(xi[:, 0:n], pr["BiT"][d][:, gp, :], ub[d][:, c0:c0 + n], True, True, [pr["BiT"][d], ub[d]], [xi])
                yield
                c = cosT[:, 0:n]; s = sinT[:, 0:n]
                P.tt("dve", tm[0][:, 0:n], xr[:, 0:n], c, ALU.mult, [xr, cosT], [tm[0]])
                P.tt("dve", tm[1][:, 0:n], xi[:, 0:n], s, ALU.mult, [xi, sinT], [tm[1]])
                yield
                P.tt("pool", xt_[0][:, 0:n], tm[0][:, 0:n], tm[1][:, 0:n], ALU.add, [tm[0], tm[1]], [xt_[0]])
                P.tt("dve", tm[2][:, 0:n], xi[:, 0:n], c, ALU.mult, [xi, cosT], [tm[2]])
                P.tt("dve", tm[3][:, 0:n], xr[:, 0:n], s, ALU.mult, [xr, sinT], [tm[3]])
                yield
                P.tt("pool", xt_[1][:, 0:n], tm[2][:, 0:n], tm[3][:, 0:n], ALU.subtract, [tm[2], tm[3]], [xt_[1]])
                g = gg_[k % 2]
                icur = ini[k % 2]; inxt = ini[(k + 1) % 2]
                for ri in range(2):
                    init = 0.0 if k == 0 else icur[:, ri:ri + 1]
                    P.op("dve", lambda e, g=g, ri=ri, init=init, n=n: e.tensor_tensor_scan(
                        out=g[ri][:, 0:n], data0=rhoT[:, 0:n], data1=xt_[ri][:, 0:n], initial=init,
                        op0=ALU.mult, op1=ALU.add), reads=[rhoT, xt_[ri]] + ([icur] if k else []), writes=[g[ri]])
                    yield
                if k + 1 < nch:
                    grl = g[0][:, n - 1:n]; gil = g[1][:, n - 1:n]; cT = cosT[:, n:n + 1]; sT = sinT[:, n:n + 1]
                    P.ts("dve", inxt[:, 2:3], gil, sT, None, ALU.mult, None, [g[1], sinT], [inxt])
                    yield
                    P.stt("dve", inxt[:, 0:1], grl, cT, inxt[:, 2:3], ALU.mult, ALU.subtract, [g[0], cosT, inxt], [inxt])
                    yield
                    P.ts("dve", inxt[:, 3:4], gil, cT, None, ALU.mult, None, [g[1], cosT], [inxt])
                    yield
                    P.stt("dve", inxt[:, 1:2], grl, sT, inxt[:, 3:4], ALU.mult, ALU.add, [g[0], sinT, inxt], [inxt])
                    yield
                P.tt("pool", tm[0][:, 0:n], g[0][:, 0:n], c, ALU.mult, [g[0], cosT], [tm[0]])
                P.tt("dve", tm[1][:, 0:n], g[1][:, 0:n], s, ALU.mult, [g[1], sinT], [tm[1]])
                yield
                P.tt("pool", H[d][0][:, c0:c0 + n], tm[0][:, 0:n], tm[1][:, 0:n], ALU.subtract, [tm[0], tm[1]], [H[d][0]])
                P.tt("dve", tm[2][:, 0:n], g[0][:, 0:n], s, ALU.mult, [g[0], sinT], [tm[2]])
                yield
                P.tt("pool", tm[3][:, 0:n], g[1][:, 0:n], c, ALU.mult, [g[1], cosT], [tm[3]])
                yield
                P.tt("pool", H[d][1][:, c0:c0 + n], tm[2][:, 0:n], tm[3][:, 0:n], ALU.add, [tm[2], tm[3]], [H[d][1]])
                yield

        for gp in range(NGP):
            for d in range(2):
                P.dma("sp" if d == 0 else "pool", uf[d][:], srcs[d][gp * 32:(gp + 1) * 32, :], reads=[srcs[d]], writes=[uf[d]])
                P.act(ub[d][:], uf[d][:], AF.Copy, [uf[d]], [ub[d]])
            _interleave([dir_gen(0, gp), dir_gen(1, gp)])
            for nb in range(9):
                c0 = nb * 512
                n = min(512, NTOK - c0)
                P.mm(pf[:, 0:n], pr["CTr"][0][:, gp, :], H[0][0][:, c0:c0 + n], True, False, [pr["CTr"][0], H[0][0]], [pf])
                P.mm(pf[:, 0:n], pr["CTi"][0][:, gp, :], H[0][1][:, c0:c0 + n], False, True, [pr["CTi"][0], H[0][1]], [pf])
                for sbk in range(n // 128):
                    tau = nb * 4 + sbk
                    j = RTINV[tau]
                    P.mm(pb[:, 0:32], H[1][0][:, j * 128:(j + 1) * 128], pr["CTr"][1][:, gp, :], True, False, [pr["CTr"][1], H[1][0]], [pb])
                    P.mm(pb[:, 0:32], H[1][1][:, j * 128:(j + 1) * 128], pr["CTi"][1][:, gp, :], False, True, [pr["CTi"][1], H[1][1]], [pb])
                    P.act(ybt[:], pb[:, 0:32], AF.Copy, [pb], [ybt])
                    P.mm(pj[:, sbk * 128:(sbk + 1) * 128], ybt[:], J, True, True, [ybt, self.C], [pj])
                P.act(ybs[:, 0:n], pj[:, 0:n], AF.Copy, [pj], [ybs])
                P.tt("dve", yy[:, 0:n], pf[:, 0:n], ybs[:, 0:n], ALU.add, [pf, ybs], [yy])
                P.stt("dve", yy[:, 0:n], uf[0][:, c0:c0 + n], pr["dsk"][:, gp:gp + 1], yy[:, 0:n], ALU.mult, ALU.add, [uf[0], pr["dsk"], yy], [yy])
                P.act(ww[:, 0:n], yy[:, 0:n], AF.Square, [yy], [ww])
                P.ts("dve", ww[:, 0:n], ww[:, 0:n], 0.044715, 1.0, ALU.mult, ALU.add, [ww], [ww])
                P.tt("dve", ww[:, 0:n], ww[:, 0:n], yy[:, 0:n], ALU.mult, [ww, yy], [ww])
                P.act(ww[:, 0:n], ww[:, 0:n], AF.Sigmoid, [ww], [ww], scale=1.5957691216)
                P.tt("dve", GTt[:, c0:c0 + n], ww[:, 0:n], yy[:, 0:n], ALU.mult, [ww, yy], [GTt])
            P.dma("sp", self.GT[gp * 32:(gp + 1) * 32, :], GTt[:], reads=[GTt], writes=[self.GT])
            P.coll("AllGather", ALU.bypass, RG, self.GT[gp * 32:(gp + 1) * 32, :], self.GTF[gp], [self.GT], [self.GTF])


Net.stage_s5 = _stage_s5


def dn_tile_rows(dram_t, i, ncols_slice=None):
    if i < 2:
        return [(slice(0, 128), dram_t[i * 128:(i + 1) * 128, :])]
    c0 = 2 * (i - 2)
    v = dram_t[NCTX:NTOK, :].rearrange("(r c) n -> c r n", c=64)
    return [(slice(0, 64), v[c0]), (slice(64, 128), v[c0 + 1])]


def _stage_dn_prep(self, l):
    P = self.P
    ident = self.cst("ident"); ones = self.cst("ones")
    with P.scope():
        pt = P.ps("dpt", [128, 512]); pn = [P.ps("dpn", [128, 512]) for _ in range(2)]
        NCH = 3 * UC // 128
        HC = UC // 128
        cwl = P.sb("cwl", [5, 3 * UC])
        P.dma("sp", cwl[:], self.conv_w[l], reads=[self.conv_w], writes=[cwl])
        cwT = P.sb("cwT", [128, NCH, 5])
        for c in range(NCH):
            P.mm(pt[:, c * 8:c * 8 + 5], cwl[0:5, c * 128:(c + 1) * 128], ident[0:5, 0:5], True, True, [cwl, self.C], [pt])
        P.act(cwT[:], pt[:, 0:8 * NCH].rearrange("p (c j) -> p c j", j=8)[:, :, 0:5], AF.Copy, [pt], [cwT])
        raw = P.sb("draw", [128, NTOK]); pd = P.sb("dpd", [128, NTOK + 8]); acc = P.sb("dacc", [128, NTOK]); cs = P.sb("dcs", [128, NTOK])
        sq = P.sb("dsq", [128, 512]); rs = P.sb("drs", [128, 512]); tk = [P.sb("dtk", [128, 512]) for _ in range(2)]
        P.op("pool", lambda e: e.memset(pd[:], 0.0), writes=[pd])
        for c in range(NCH):
            kind = c // HC
            h = c % HC
            P.dma("sp", raw[:], self.PT[UC + c * 128:UC + (c + 1) * 128, :], reads=[self.PT], writes=[raw])
            P.act(pd[:, 2:258], raw[:, 0:256], AF.Copy, [raw], [pd])
            P.op("pool", lambda e: e.tensor_copy(out=pd[:, 262:262 + NLAT].rearrange("p (c r) -> p c r", r=64),
                                                 in_=raw[:, 256:NTOK].rearrange("p (r c) -> p c r", c=64)), reads=[raw], writes=[pd])
            for base, o0, n in ((0, 0, 256), (260, 256, NLAT)):
                P.ts("dve", acc[:, o0:o0 + n], pd[:, base:base + n], cwT[:, c, 0:1], None, ALU.mult, None, [pd, cwT], [acc])
                for j in range(1, 5):
                    P.stt("dve", acc[:, o0:o0 + n], pd[:, base + j:base + j + n], cwT[:, c, j:j + 1], acc[:, o0:o0 + n],
                          ALU.mult, ALU.add, [pd, cwT, acc], [acc])
            P.act(cs[:], acc[:], AF.Silu, [acc], [cs])
            if kind < 2:
                for nb in range(9):
                    c0 = nb * 512; n = min(512, NTOK - c0)
                    p_ = pn[nb % 2]
                    P.act(sq[:, 0:n], cs[:, c0:c0 + n], AF.Square, [cs], [sq])
                    P.mm(p_[:, 0:n], ones, sq[:, 0:n], True, True, [self.C, sq], [p_])
                    P.act(rs[:, 0:n], p_[:, 0:n], AF.Sqrt, [p_], [rs], bias=1e-6)
                    P.op("dve", lambda e, n=n: e.reciprocal(out=rs[:, 0:n], in_=rs[:, 0:n]), reads=[rs], writes=[rs])
                    if kind == 0:
                        P.stt("dve", cs[:, c0:c0 + n], cs[:, c0:c0 + n], 128.0 ** -0.5, rs[:, 0:n], ALU.mult, ALU.mult, [cs, rs], [cs])
                    else:
                        P.tt("dve", cs[:, c0:c0 + n], cs[:, c0:c0 + n], rs[:, 0:n], ALU.mult, [cs, rs], [cs])
                P.dma("sp", (self.QT if kind == 0 else self.KT)[h * 128:(h + 1) * 128, :], cs[:], reads=[cs],
                      writes=[self.QT if kind == 0 else self.KT])
            if kind >= 1:
                dst = self.KTOK if kind == 1 else self.VTOK
                for i4 in range(0, NT, 4):
                    nn = min(4, NT - i4)
                    for ii in range(nn):
                        i = i4 + ii
                        P.mm(pt[:, ii * 128:(ii + 1) * 128], cs[:, i * 128:(i + 1) * 128], ident, True, True, [cs, self.C], [pt])
                    t_ = tk[(i4 // 4) % 2]
                    P.act(t_[:, 0:nn * 128], pt[:, 0:nn * 128], AF.Copy, [pt], [t_])
                    for ii in range(nn):
                        i = i4 + ii
                        P.dma("pool", dst[i * 128:(i + 1) * 128, h * 128:(h + 1) * 128], t_[:, ii * 128:(ii + 1) * 128], reads=[t_], writes=[dst])
        dtb = P.sb("ddtb", [128, 2 * NH]); nea = P.sb("dnea", [128, 2 * NH])
        self.bcast_load("sp", dtb, self.dt_bias[l:l + 1, :], self.dt_bias)
        self.bcast_load("sp", nea, self.a_log[l:l + 1, :], self.a_log)
        P.act(nea[:], nea[:], AF.Exp, [nea], [nea])
        P.ts("dve", nea[:], nea[:], -1.0, None, ALU.mult, None, [nea], [nea])
        abt = [P.sb("dabt", [128, ABC]) for _ in range(2)]; gbt = [P.sb("dgbt", [128, ABC]) for _ in range(2)]
        for i in range(NT):
            a_ = abt[i % 2]; g_ = gbt[i % 2]
            for ps_, ap in dn_tile_rows(self.AB, i):
                P.dma("sp", a_[ps_, :], ap, reads=[self.AB], writes=[a_])
            av = a_[:, :].rearrange("p (d k h) -> p d k h", d=2, k=2)
            G2 = 2 * NH
            gv = g_[:, 0:G2].rearrange("p (d h) -> p d h", d=2)
            P.tt("dve", gv, av[:, :, 0, :], dtb[:, :].rearrange("p (d h) -> p d h", d=2), ALU.add, [a_, dtb], [g_])
            P.act(g_[:, 0:G2], g_[:, 0:G2], AF.Exp, [g_], [g_])
            P.act(g_[:, 0:G2], g_[:, 0:G2], AF.Ln, [g_], [g_], bias=1.0)
            P.tt("dve", g_[:, 0:G2], g_[:, 0:G2], nea[:], ALU.mult, [g_, nea], [g_])
            P.act(g_[:, G2:2 * G2].rearrange("p (d h) -> p d h", d=2), av[:, :, 1, :], AF.Sigmoid, [a_], [g_])
            P.dma("pool", self.GB[i * 128:(i + 1) * 128, :], g_[:], reads=[g_], writes=[self.GB])


def _interleave(gens):
    gens = list(gens)
    while gens:
        for g in list(gens):
            try:
                next(g)
            except StopIteration:
                gens.remove(g)


def _stage_dn_main(self, l):
    P = self.P
    ident = self.cst("ident")
    v3 = lambda t: t[:, 0:NH * 128].rearrange("p (h n) -> p h n", h=NH)
    bc = lambda ap, n=128: ap.unsqueeze(2).to_broadcast([128, NH, n])
    with P.scope():
        B = []
        for d in range(2):
            b = {}
            for nm in ("pA", "pB", "pC", "pD"):
                b[nm] = P.ps("d" + nm, [128, 512])
            for nm in ("S", "qT", "kT", "ktok", "vtok", "Gall", "Dm", "QKm", "WT", "kdec", "vnew", "o1", "O"):
                b[nm] = P.sb("d" + nm, [128, NH, 128])
            b["gb"] = P.sb("dgb", [128, 4 * NH]); b["sm"] = P.sb("dsm", [128, 6, NH])
            b["MT"] = [P.sb("dMT", [128, NH, 128]) for _ in range(2)]; b["MA"] = [P.sb("dMA", [128, NH, 128]) for _ in range(2)]
            b["r"] = P.sb("dr", [128, NH, 256])
            b["OD"] = self.ODt[d]
            B.append(b)

        def tile_gen(d, i, b):
            pA, pB, pC, pD = b["pA"], b["pB"], b["pC"], b["pD"]
            S, qT, kT, ktok, vtok, Gall, Dm, QKm = b["S"], b["qT"], b["kT"], b["ktok"], b["vtok"], b["Gall"], b["Dm"], b["QKm"]
            WT, kdec, vnew, o1, O, gb, sm, MT, MA, r = b["WT"], b["kdec"], b["vnew"], b["o1"], b["O"], b["gb"], b["sm"], b["MT"], b["MA"], b["r"]
            tri = self.cst("tri_f" if d == 0 else "tri_b"); ms = self.cst("ms_f" if d == 0 else "ms_b"); mi = self.cst("mi_f" if d == 0 else "mi_b")
            halves = (slice(0, 64), slice(64, 128)) if d == 0 else (slice(64, 128), slice(0, 64))
            q0, q1 = ("sp", "pool") if d == 0 else ("pool", "sp")
            cols = slice(i * 128, (i + 1) * 128)
            P.dma(q0, qT[:], self.QT[:, cols].rearrange("(h p) n -> p h n", p=128), reads=[self.QT], writes=[qT])
            P.dma(q1, kT[:], self.KT[:, cols].rearrange("(h p) n -> p h n", p=128), reads=[self.KT], writes=[kT])
            P.dma(q0, ktok[:].rearrange("p h n -> p (h n)"), self.KTOK[cols, :], reads=[self.KTOK], writes=[ktok])
            P.dma(q1, vtok[:].rearrange("p h n -> p (h n)"), self.VTOK[cols, :], reads=[self.VTOK], writes=[vtok])
            P.dma(q0, gb[:], self.GB[cols, :], reads=[self.GB], writes=[gb])
            yield
            g = gb[:, d * NH:d * NH + NH]; beta = gb[:, 2 * NH + d * NH:2 * NH + d * NH + NH]
            P.mm(pA[:, 0:NH], tri, g, True, True, [self.C, gb], [pA])
            P.mm(pA[:, NH:2 * NH], self.cst("blk_d"), g, True, True, [self.C, gb], [pA])
            P.mm(pA[:, 2 * NH:3 * NH], self.cst("blk_a"), g, True, True, [self.C, gb], [pA])
            P.mm(pA[:, 3 * NH:4 * NH], self.cst("blk_b"), g, True, True, [self.C, gb], [pA])
            P.op("dve", lambda e: e.tensor_copy(out=Gall[:], in_=bc(g)), reads=[gb], writes=[Gall])
            yield
            P.act(sm[:, 0:4, :].rearrange("p a h -> p (a h)"), pA[:, 0:4 * NH], AF.Copy, [pA], [sm])
            for h in range(NH):
                P.mm(pB[:, h * 128:(h + 1) * 128], Gall[:, h, :], tri, True, True, [Gall, self.C], [pB])
            for h in range(NH):
                P.mm(pC[:, h * 128:(h + 1) * 128], kT[:, h, :], kT[:, h, :], True, True, [kT], [pC])
                P.mm(pD[:, h * 128:(h + 1) * 128], kT[:, h, :], qT[:, h, :], True, True, [kT, qT], [pD])
            yield
            P.act(sm[:, 4, :], sm[:, 0, :], AF.Exp, [sm], [sm])
            P.tt("dve", sm[:, 5, :], sm[:, 1, :], sm[:, 0, :], ALU.subtract, [sm], [sm])
            yield
            P.act(sm[:, 5, :], sm[:, 5, :], AF.Exp, [sm], [sm])
            P.act(sm[:, 2:4, :], sm[:, 2:4, :], AF.Exp, [sm], [sm])
            gc = sm[:, 0, :]; egc = sm[:, 4, :]; ekd = sm[:, 5, :]
            P.tt("dve", Dm[:], v3(pB), bc(gc), ALU.subtract, [pB, sm], [Dm])
            yield
            P.ts("dve", Dm[:], Dm[:], 0.0, None, ALU.min, None, [Dm], [Dm])
            yield
            P.act(Dm[:], Dm[:], AF.Exp, [Dm], [Dm])
            yield
            msb = ms.unsqueeze(1).to_broadcast([128, NH, 128]); mib = mi.unsqueeze(1).to_broadcast([128, NH, 128])
            P.tt("dve", MT[0][:], v3(pC), Dm[:], ALU.mult, [pC, Dm], [MT[0]])
            P.tt("dve", QKm[:], v3(pD), Dm[:], ALU.mult, [pD, Dm], [QKm])
            yield
            P.tt("pool", MT[0][:], MT[0][:], msb, ALU.mult, [MT[0], self.C], [MT[0]])
            P.tt("pool", QKm[:], QKm[:], mib, ALU.mult, [QKm, self.C], [QKm])
            P.op("pool", lambda e: e.tensor_copy(out=r[:, :, 0:128], in_=vtok[:]), reads=[vtok], writes=[r])
            yield
            P.tt("dve", MT[0][:], MT[0][:], bc(beta), ALU.mult, [MT[0], gb], [MT[0]])
            P.tt("dve", r[:, :, 128:256], ktok[:], bc(egc), ALU.mult, [ktok, sm], [r])
            P.tt("pool", kdec[:], ktok[:], bc(ekd), ALU.mult, [ktok, sm], [kdec])
            yield
            for h in range(NH):
                P.mm(pC[:, h * 128:(h + 1) * 128], MT[0][:, h, :], ident, True, True, [MT[0], self.C], [pC])
            yield
            P.act(MA[0][:], v3(pC), AF.Copy, [pC], [MA[0]])
            yield
            cur = 0
            for k in range(6):
                mt = MT[cur]; ma = MA[cur]
                for h in range(NH):
                    P.mm(pA[:, h * 256:(h + 1) * 256], mt[:, h, :], r[:, h, :], True, True, [mt, r], [pA])
                if k < 5:
                    nx = 1 - cur
                    for h in range(NH):
                        P.mm(pC[:, h * 128:(h + 1) * 128], ma[:, h, :], mt[:, h, :], True, True, [ma, mt], [pC])
                        P.mm(pD[:, h * 128:(h + 1) * 128], mt[:, h, :], ma[:, h, :], True, True, [ma, mt], [pD])
                yield
                P.tt("dve", r[:], r[:], pA[:, 0:NH * 256].rearrange("p (h n) -> p h n", h=NH),
                     ALU.subtract if k == 0 else ALU.add, [r, pA], [r])
                if k < 5:
                    P.act(MT[nx][:], v3(pC), AF.Copy, [pC], [MT[nx]])
                    P.act(MA[nx][:], v3(pD), AF.Copy, [pD], [MA[nx]])
                    cur = nx
                yield
            P.tt("dve", r[:], r[:], bc(beta, 256), ALU.mult, [r, gb], [r])
            yield
            for h in range(NH):
                P.mm(pC[:, h * 128:(h + 1) * 128], r[:, h, 128:256], ident, True, True, [r, self.C], [pC])
            yield
            P.act(WT[:], v3(pC), AF.Copy, [pC], [WT])
            yield
            for hi, rows in enumerate(halves):
                egl = sm[:, 2 if rows.start == 0 else 3, :]
                for h in range(NH):
                    P.mm(pA[rows, h * 128:(h + 1) * 128], WT[:, h, rows], S[:, h, :], True, True, [WT, S], [pA])
                    P.mm(pB[rows, h * 128:(h + 1) * 128], qT[:, h, rows], S[:, h, :], True, True, [qT, S], [pB])
                yield
                P.tt("dve", vnew[rows], r[rows, :, 0:128], v3(pA)[rows], ALU.subtract, [r, pA], [vnew])
                P.tt("pool", S[:], S[:], bc(egl), ALU.mult, [S, sm], [S])
                yield
                P.tt("dve", o1[rows], v3(pB)[rows], egc[rows].unsqueeze(2).to_broadcast([64, NH, 128]), ALU.mult, [pB, sm], [o1])
                for h in range(NH):
                    P.mm(pC[rows, h * 128:(h + 1) * 128], QKm[rows, h, rows], vnew[rows, h, :], True, True, [QKm, vnew], [pC])
                    P.mm(pD[:, h * 128:(h + 1) * 128], kdec[rows, h, :], vnew[rows, h, :], True, True, [kdec, vnew], [pD])
                yield
                P.tt("dve", O[rows], o1[rows], v3(pC)[rows], ALU.add, [o1, pC], [O])
                P.tt("dve", S[:], S[:], v3(pD), ALU.add, [S, pD], [S])
                yield
            P.dma(q0, self.OD[d][cols, :], O[:].rearrange("p h n -> p (h n)"), reads=[O], writes=[b["OD"]])
            yield

        def chain(d):
            b = B[d]
            order = list(range(NT)) if d == 0 else [1, 0] + list(range(NT - 1, 1, -1))
            P.op("pool", lambda e: e.memset(b["S"][:], 0.0), writes=[b["S"]])
            for i in order:
                yield from tile_gen(d, i, b)

        _interleave([chain(0), chain(1)])


def _stage_dn_fin(self, l):
    P = self.P
    with P.scope():
        nw = P.sb("dnw", [128, 128])
        self.bcast_load("sp", nw, self.norm_w[l:l + 1, :], self.norm_w)
        oa = [P.sb("foa", [128, NH, 128]) for _ in range(2)]; ob = [P.sb("fob", [128, NH, 128]) for _ in range(2)]
        zt = [P.sb("fzt", [128, NH, 128]) for _ in range(2)]; sq = P.sb("fsq", [128, NH, 128]); ss = P.sb("fss", [128, NH])
        for i in range(NT):
            a = oa[i % 2]; b = ob[i % 2]; z = zt[i % 2]
            cols = slice(i * 128, (i + 1) * 128)
            P.dma("sp", a[:].rearrange("p h n -> p (h n)"), self.OD[0][cols, :], reads=[self.ODt[0]], writes=[a])
            P.dma("pool", b[:].rearrange("p h n -> p (h n)"), self.OD[1][cols, :], reads=[self.ODt[1]], writes=[b])
            for ps_, ap in dn_tile_rows(self.Z, i):
                P.dma("sp", z[ps_].rearrange("p h n -> p (h n)"), ap, reads=[self.Z], writes=[z])
            P.tt("dve", a[:], a[:], b[:], ALU.add, [a, b], [a])
            P.tt("pool", sq[:], a[:], a[:], ALU.mult, [a], [sq])
            P.op("dve", lambda e: e.tensor_reduce(out=ss[:], in_=sq[:], axis=AX.X, op=ALU.add), reads=[sq], writes=[ss])
            P.act(ss[:], ss[:], AF.Sqrt, [ss], [ss], scale=1.0 / 128.0, bias=1e-6)
            P.op("dve", lambda e: e.reciprocal(out=ss[:], in_=ss[:]), reads=[ss], writes=[ss])
            P.tt("dve", a[:], a[:], ss[:, :].unsqueeze(2).to_broadcast([128, NH, 128]), ALU.mult, [a, ss], [a])
            P.tt("pool", a[:], a[:], nw[:, :].unsqueeze(1).to_broadcast([128, NH, 128]), ALU.mult, [a, nw], [a])
            P.act(b[:], z[:], AF.Silu, [z], [b])
            P.tt("dve", a[:], a[:], b[:], ALU.mult, [a, b], [a])
            for ps_, ap in dn_tile_rows(self.DNO, i):
                P.dma("pool", ap, a[ps_].rearrange("p h n -> p (h n)"), reads=[a], writes=[self.DNO])
        for c in range(5):
            rc = 1024 if c < 4 else 256
            P.coll("AllGather", ALU.bypass, RG, self.DNO[c * 1024:c * 1024 + rc, :], self.DNOF[R * c * 1024:R * c * 1024 + R * rc, :],
                   [self.DNO], [self.DNOF])


def _stage_dn(self, l):
    _stage_dn_prep(self, l)
    _stage_dn_main(self, l)
    _stage_dn_fin(self, l)


Net.stage_dn = _stage_dn


def _ln_tile(self, t, lng, lnb, tmp, st):
    P = self.P
    P.op("dve", lambda e: e.tensor_reduce(out=st[:, 0:1], in_=t[:], axis=AX.X, op=ALU.add), reads=[t], writes=[st])
    P.ts("dve", st[:, 1:2], st[:, 0:1], -1.0 / D, None, ALU.mult, None, [st], [st])
    P.ts("dve", t[:], t[:], st[:, 1:2], None, ALU.add, None, [t, st], [t])
    P.tt("pool", tmp[:], t[:], t[:], ALU.mult, [t], [tmp])
    P.op("dve", lambda e: e.tensor_reduce(out=st[:, 2:3], in_=tmp[:], axis=AX.X, op=ALU.add), reads=[tmp], writes=[st])
    P.act(st[:, 3:4], st[:, 2:3], AF.Sqrt, [st], [st], scale=1.0 / D, bias=1e-5)
    P.op("dve", lambda e: e.reciprocal(out=st[:, 3:4], in_=st[:, 3:4]), reads=[st], writes=[st])
    P.stt("dve", t[:], t[:], st[:, 3:4], lng[:], ALU.mult, ALU.mult, [t, st, lng], [t])
    P.tt("pool", t[:], t[:], lnb[:], ALU.add, [t, lnb], [t])


def _stage_glu(self, l):
    P = self.P
    ident = self.cst("ident")
    with P.scope():
        ps = [P.ps("gps", [128, 512]) for _ in range(3)]
        gw = P.sb("ggw", [128, 8, UC], BF16); stg = P.sb("gstg", [128, NTOK])
        for kc in range(8):
            P.dma("sp", stg[:, 0:UC], self.glu_w[l][kc * 128:(kc + 1) * 128, :], reads=[self.glu_w], writes=[stg])
            P.act(gw[:, kc, :], stg[:, 0:UC], AF.Copy, [stg], [gw])
        NOC = UC // 128
        gbl = P.sb("ggbl", [NOC, 128]); glub = P.sb("gglub", [128, NOC])
        P.dma("sp", gbl[:], self.glu_b[l:l + 1, :].rearrange("o (k p) -> (o k) p", p=128), reads=[self.glu_b], writes=[gbl])
        P.mm(ps[0][:, 0:NOC], gbl[:], ident[0:NOC, 0:NOC], True, True, [gbl, self.C], [ps[0]])
        P.act(glub[:], ps[0][:, 0:NOC], AF.Copy, [ps[0]], [glub])
        Gb = P.sb("gGb", [128, 8, NTOK], BF16)
        for kc in range(8):
            rr_ = kc // (UC // 128); lb = kc % (UC // 128)
            for j in range(4):
                P.dma("sp" if j % 2 == 0 else "pool", stg[32 * j:32 * (j + 1), :], self.GTF[lb * 4 + j][rr_ * 32:(rr_ + 1) * 32, :],
                      reads=[self.GTF], writes=[stg])
            P.act(Gb[:, kc, :], stg[:], AF.Copy, [stg], [Gb])
        ob = [P.sb("gob", [128, NTOK], BF16) for _ in range(2)]; sg = [P.sb("gsg", [128, 512]) for _ in range(2)]
        for oc in range(NOC):
            o = ob[oc % 2]
            P.dma("sp", stg[:], self.GT[oc * 128:(oc + 1) * 128, :], reads=[self.GT], writes=[stg])
            for nb in range(9):
                c0 = nb * 512; n = min(512, NTOK - c0)
                p_ = ps[nb % 3]; s_ = sg[nb % 2]
                for kc in range(8):
                    P.mm(p_[:, 0:n], gw[:, kc, oc * 128:(oc + 1) * 128], Gb[:, kc, c0:c0 + n], kc == 0, kc == 7, [gw, Gb], [p_])
                P.act(s_[:, 0:n], p_[:, 0:n], AF.Sigmoid, [p_, glub], [s_], bias=glub[:, oc:oc + 1])
                P.tt("dve", o[:, c0:c0 + n], s_[:, 0:n], stg[:, c0:c0 + n], ALU.mult, [s_, stg], [o])
            P.dma("pool", self.S5T[oc * 128:(oc + 1) * 128, :], o[:], reads=[o], writes=[self.S5T])
            for hh in range(2):
                P.coll("AllGather", ALU.bypass, RG, self.S5T[oc * 128 + hh * 64:oc * 128 + (hh + 1) * 64, :], self.S5TF[oc * 2 + hh],
                       [self.S5T], [self.S5TF])


def _stage_outproj(self, l):
    P = self.P
    with P.scope():
        wo = P.sb("owo", [128, 16, DC], BF16); stg = P.sb("ostg", [128, DC])
        for kc in range(16):
            P.dma("sp" if kc % 2 == 0 else "pool", stg[:], self.w_out[l][kc * 128:(kc + 1) * 128, :], reads=[self.w_out], writes=[stg])
            P.act(wo[:, kc, :], stg[:], AF.Copy, [stg], [wo])
        s5t = [P.sb("os5t", [128, 8, 128], BF16) for _ in range(2)]
        dnt = [P.sb("odnt", [128, 1024]) for _ in range(2)]; dnb = P.sb("odnb", [128, 1024], BF16); dnT = P.sb("odnT", [128, 8, 128], BF16)
        pt = P.ps("opt", [128, 1024]); pa = [P.ps("opa", [128, 512]) for _ in range(3)]
        t = [P.sb("ot_", [128, DC]) for _ in range(2)]
        for i in range(NT):
            cols = slice(i * 128, (i + 1) * 128)
            s5 = s5t[i % 2]; dn = dnt[i % 2]; t_ = t[i % 2]
            for kc in range(8):
                rr_ = kc // (UC // 128); lb = kc % (UC // 128)
                for hh in range(2):
                    P.dma("sp" if hh == 0 else "pool", s5[hh * 64:(hh + 1) * 64, kc, :], self.S5TF[lb * 2 + hh][rr_ * 64:(rr_ + 1) * 64, cols],
                          reads=[self.S5TF], writes=[s5])
            c_ = i // 8; ii = i % 8; rc = 1024 if c_ < 4 else 256
            dnf = self.DNOF[R * c_ * 1024:R * c_ * 1024 + R * rc, :].rearrange("(r t) c -> t r c", r=R)
            P.dma("pool", dn[:, :].rearrange("p (r c) -> p r c", r=R), dnf[ii * 128:(ii + 1) * 128], reads=[self.DNOF], writes=[dn])
            P.act(dnb[:], dn[:], AF.Copy, [dn], [dnb])
            for kc in range(8):
                P.mm(pt[:, kc * 128:(kc + 1) * 128], dnb[:, kc * 128:(kc + 1) * 128], self.Cb[:, 0:128], True, True, [dnb, self.Cb], [pt])
            P.act(dnT[:].rearrange("p k n -> p (k n)"), pt[:], AF.Copy, [pt], [dnT])
            for cb in range(DC // 512):
                p_ = pa[(i + cb) % 3]
                cs_ = slice(cb * 512, (cb + 1) * 512)
                for kc in range(8):
                    P.mm(p_[:], s5[:, kc, :], wo[:, kc, cs_], kc == 0, False, [s5, wo], [p_])
                for kc in range(8):
                    P.mm(p_[:], dnT[:, kc, :], wo[:, 8 + kc, cs_], False, kc == 7, [dnT, wo], [p_])
                P.act(t_[:, cs_], p_[:], AF.Copy, [p_], [t_])
            P.dma("sp", self.MIXL[cols, :], t_[:], reads=[t_], writes=[self.MIXL])
            if i % 4 == 3 or i == NT - 1:
                c_ = i // 4; rc = 512 if c_ < 8 else 256
                P.coll("AllGather", ALU.bypass, RG, self.MIXL[c_ * 512:c_ * 512 + rc, :], self.MIXF[R * c_ * 512:R * c_ * 512 + R * rc, :],
                       [self.MIXL], [self.MIXF])
    _stage_resln(self, l, 1)


def _stage_resln(self, l, which):
    P = self.P
    with P.scope():
        gseg = 2 if which == 1 else 5
        g_ = {}
        for who, row in (("lat", 0), ("ctx", 1)):
            g_[who] = P.sb("lg", [128, D])
            self.mod_load("sp", g_[who], l, row, gseg)
        lng = P.sb("llng", [128, D]); lnb = P.sb("llnb", [128, D])
        self.bcast_load("sp", lng, (self.ln1_g if which == 1 else self.ln2_g)[l:l + 1, :], self.ln1_g if which == 1 else self.ln2_g)
        self.bcast_load("pool", lnb, (self.ln1_b if which == 1 else self.ln2_b)[l:l + 1, :], self.ln1_b if which == 1 else self.ln2_b)
        tt_ = [P.sb("lt", [128, D]) for _ in range(2)]; xx = [P.sb("lx", [128, D]) for _ in range(2)]
        tmp = P.sb("ltmp", [128, D]); st = P.sb("lst", [128, 4])
        def ar(j):
            P.coll("AllReduce", ALU.add, RG, self.FFP[j * 128:(j + 1) * 128, :], self.FF[j * 128:(j + 1) * 128, :], [self.FFP], [self.FFt[j]])
        if which == 2:
            ar(0); ar(1)
        for i in range(NT):
            cols = slice(i * 128, (i + 1) * 128)
            t = tt_[i % 2]; x = xx[i % 2]
            if which == 2 and i + 2 < NT:
                ar(i + 2)
            if which == 1:
                c_ = i // 4; ii = i % 4; rc = 512 if c_ < 8 else 256
                mixf = self.MIXF[R * c_ * 512:R * c_ * 512 + R * rc, :].rearrange("(r t) c -> t r c", r=R)
                P.dma("sp", t[:, :].rearrange("p (r c) -> p r c", r=R), mixf[ii * 128:(ii + 1) * 128], reads=[self.MIXF], writes=[t])
            else:
                P.dma("sp", t[:], self.FF[cols, :], reads=[self.FFt[i]], writes=[t])
            P.dma("pool", x[:], self.X[cols, :], reads=[self.X], writes=[x])
            P.tt("dve", t[:], t[:], g_["ctx" if i < 2 else "lat"][:], ALU.mult, [t, g_["ctx"], g_["lat"]], [t])
            P.stt("dve", t[:], x[:], ALPHA, t[:], ALU.mult, ALU.add, [x, t], [t])
            _ln_tile(self, t, lng, lnb, tmp, st)
            P.dma("sp", self.X[cols, :], t[:], reads=[t], writes=[self.X])
    self.dump_dram("X%d" % which, lambda r0, n: self.X[r0:r0 + n, :], NTOK, D, self.X)


Net.stage_glu = _stage_glu
Net.stage_outproj = _stage_outproj


def _stage_moe(self, l):
    P = self.P
    ident = self.cst("ident"); lt = self.cst("lt")
    NE = 16
    with P.scope():
        modt = {}
        for who, row in (("lat", 0), ("ctx", 1)):
            scp = P.sb("mscp", [128, D]); sh = P.sb("msh", [128, D])
            self.mod_load("sp", scp, l, row, 4)
            self.mod_load("pool", sh, l, row, 3)
            modt[who] = (scp, sh)
        rw = P.sb("mrw", [128, 16, 16])
        P.dma("sp", rw[:], self.router_w[l].rearrange("(k p) e -> p k e", p=128), reads=[self.router_w], writes=[rw])
        x = [P.sb("mx", [128, D]) for _ in range(2)]; hb = [P.sb("mhb", [128, D], BF16) for _ in range(2)]
        hT = P.sb("mhT", [128, 16, 128]); pt = [P.ps("mpt", [128, 512]) for _ in range(2)]; pl = P.ps("mpl", [128, 512])
        af = P.sb("maf", [128, 16]); st = P.sb("mst", [128, 4]); aft = P.sb("maft", [16, 128])
        for i in range(NT):
            cols = slice(i * 128, (i + 1) * 128)
            x_ = x[i % 2]; h_ = hb[i % 2]
            scp, sh = modt["ctx" if i < 2 else "lat"]
            P.dma("sp", x_[:], self.X[cols, :], reads=[self.X], writes=[x_])
            P.tt("dve", x_[:], x_[:], scp[:], ALU.mult, [x_, scp], [x_])
            P.tt("pool", x_[:], x_[:], sh[:], ALU.add, [x_, sh], [x_])
            P.act(h_[:], x_[:], AF.Copy, [x_], [h_])
            P.dma("pool", self.H2[cols, :], h_[:], reads=[h_], writes=[self.H2])
            for kg in range(4):
                p_ = pt[kg % 2]
                for kk in range(4):
                    kc = kg * 4 + kk
                    P.mm(p_[:, kk * 128:(kk + 1) * 128], x_[:, kc * 128:(kc + 1) * 128], ident, True, True, [x_, self.C], [p_])
                P.act(hT[:, kg * 4:(kg + 1) * 4, :].rearrange("p k n -> p (k n)"), p_[:], AF.Copy, [p_], [hT])
            for kc in range(16):
                P.mm(pl[:, 0:16], hT[:, kc, :], rw[:, kc, :], kc == 0, kc == 15, [hT, rw], [pl])
            P.op("dve", lambda e: e.tensor_reduce(out=st[:, 0:1], in_=pl[:, 0:16], axis=AX.X, op=ALU.max), reads=[pl], writes=[st])
            P.ts("dve", st[:, 1:2], st[:, 0:1], -1.0, None, ALU.mult, None, [st], [st])
            P.act(af[:], pl[:, 0:16], AF.Exp, [pl, st], [af], bias=st[:, 1:2])
            P.op("dve", lambda e: e.tensor_reduce(out=st[:, 2:3], in_=af[:], axis=AX.X, op=ALU.add), reads=[af], writes=[st])
            P.op("dve", lambda e: e.reciprocal(out=st[:, 3:4], in_=st[:, 2:3]), reads=[st], writes=[st])
            P.ts("dve", af[:], af[:], st[:, 3:4], None, ALU.mult, None, [af, st], [af])
            P.dma("sp", self.AFF[cols, :], af[:], reads=[af], writes=[self.AFF])
            P.mm(pl[0:16, 128:256], af[:], ident, True, True, [af, self.C], [pl])
            P.act(aft[:], pl[0:16, 128:256], AF.Copy, [pl], [aft])
            P.dma("sp", self.AFFT[:, cols], aft[:], reads=[aft], writes=[self.AFFT])
    with P.scope():
        affall = P.sb("raff", [128, NT, 16]); rank = P.sb("rrank", [128, NT, 16])
        P.dma("sp", affall[:], self.AFF[:, :].rearrange("(t p) e -> p t e", p=128), reads=[self.AFF], writes=[affall])
        arow = [P.sb("rarow", [128, NLAT]) for _ in range(2)]; junk = P.sb("rjunk", [128, NLAT]); j2 = P.sb("rj2", [128, 128])
        c4 = [P.sb("rc4", [128, 4]) for _ in range(2)]
        pl = P.ps("rpl", [128, 512]); rkt = P.sb("rrkt", [16, 128])
        P.op("pool", lambda g: g.memset(rank[:], 0.0), writes=[rank])
        for tiles, col0 in (([0, 1], 0), (list(range(2, NT)), NCTX)):
            n = len(tiles) * 128
            for e in range(NEL):
                ar = arow[e % 2]
                P.dma("sp" if e % 2 == 0 else "pool", ar[:, 0:n], self.AFFT[e:e + 1, col0:col0 + n].partition_broadcast(128),
                      reads=[self.AFFT], writes=[ar])
                for tl, t in enumerate(tiles):
                    sc = affall[:, t, e:e + 1]
                    c = c4[tl % 2]
                    P.op("pool", lambda g, c=c: g.memset(c[:], 0.0), writes=[c])
                    b0 = tl * 128
                    if tl > 0:
                        P.ts("dve", junk[:, 0:b0], ar[:, 0:b0], sc, 0.0, ALU.is_ge, ALU.add, [ar, affall], [junk, c], accum=c[:, 0:1])
                    if b0 + 128 < n:
                        P.ts("dve", junk[:, b0 + 128:n], ar[:, b0 + 128:n], sc, 0.0, ALU.is_gt, ALU.add, [ar, affall], [junk, c], accum=c[:, 1:2])
                    P.ts("dve", junk[:, b0:b0 + 128], ar[:, b0:b0 + 128], sc, 0.0, ALU.is_gt, ALU.add, [ar, affall], [junk, c], accum=c[:, 2:3])
                    P.stt("dve", j2[:], ar[:, b0:b0 + 128], sc, lt, ALU.is_equal, ALU.mult, [ar, affall, self.C], [j2])
                    P.op("dve", lambda g, c=c: g.tensor_reduce(out=c[:, 3:4], in_=j2[:], axis=AX.X, op=ALU.add), reads=[j2], writes=[c])
                    P.op("dve", lambda g, c=c, t=t, e=e: g.tensor_reduce(out=rank[:, t, e:e + 1], in_=c[:], axis=AX.X, op=ALU.add), reads=[c], writes=[rank])
        P.dma("sp", self.RANKD[:, :].rearrange("(t p) e -> p t e", p=128), rank[:], reads=[rank], writes=[self.RANKD])
        for t in range(NT):
            P.mm(pl[0:16, 0:128], rank[:, t, :], ident, True, True, [rank, self.C], [pl])
            P.act(rkt[:], pl[0:16, 0:128], AF.Copy, [pl], [rkt])
            P.dma("sp", self.RANKT[:, t * 128:(t + 1) * 128], rkt[:], reads=[rkt], writes=[self.RANKT])
    with P.scope():
        rk = P.sb("erk", [128, NT, 16])
        P.dma("sp", rk[:], self.RANKD[:, :].rearrange("(t p) e -> p t e", p=128), reads=[self.RANKD], writes=[rk])
        perm = P.sb("eperm", [128, 4, 512], BF16); permc = P.sb("epermc", [128, 2, 32], BF16)
        w1b = P.sb("ew1", [128, 16, 1024], BF16); w3b = P.sb("ew3", [128, 16, 1024], BF16); w2b = P.sb("ew2", [128, 8, D], BF16)
        stg = [P.sb("estg", [128, D]) for _ in range(2)]
        HsT = P.sb("eHsT", [128, 16, 544], BF16); hidT = P.sb("ehid", [128, 8, 544], BF16)
        h2t = [P.sb("eh2t", [128, 1024], BF16) for _ in range(3)]
        rrow = P.sb("errow", [128, NLAT]); arow = P.sb("earow", [128, NLAT]); g2_ = P.sb("eg2", [128, 2])
        gate = P.sb("egate", [128, 8]); sa = P.sb("esa", [128, 512]); yg = [P.sb("eyg", [128, D], BF16) for _ in range(2)]
        pg = [P.ps("epg", [128, 512]) for _ in range(8)]
        for e in range(NEL):
            ld = 0
            for src, dst, nk, w in ((self.w1, w1b, 16, 1024), (self.w3, w3b, 16, 1024), (self.w2, w2b, 8, D)):
                for kc in range(nk):
                    s_ = stg[ld % 2]
                    P.dma("sp" if ld % 2 == 0 else "pool", s_[:, 0:w], src[l][e][kc * 128:(kc + 1) * 128, :], reads=[src], writes=[s_])
                    if ld % 2 == 0:
                        P.act(dst[:, kc, :], s_[:, 0:w], AF.Copy, [s_], [dst])
                    else:
                        P.op("pool", lambda g, dst=dst, kc=kc, s_=s_, w=w: g.tensor_copy(out=dst[:, kc, :], in_=s_[:, 0:w]), reads=[s_], writes=[dst])
                    ld += 1
            for t in range(2):
                P.ts("dve", permc[:, t, :], self.iota[:, 0:32], rk[:, t, e:e + 1], None, ALU.is_equal, None, [self.iota, rk], [permc])
            P.dma("sp", rrow[:], self.RANKT[e:e + 1, NCTX:NTOK].partition_broadcast(128), reads=[self.RANKT], writes=[rrow])
            P.dma("pool", arow[:], self.AFFT[e:e + 1, NCTX:NTOK].partition_broadcast(128), reads=[self.AFFT], writes=[arow])
            for sbk in range(4):
                for hf in range(2):
                    hs = slice(hf * 2048, (hf + 1) * 2048)
                    P.stt("dve", stg[hf][:], rrow[:, hs], self.iotc[:, sbk:sbk + 1], arow[:, hs], ALU.is_equal, ALU.mult, [rrow, arow, self.iotc], [stg[hf]])
                    P.op("dve", lambda g, hf=hf: g.tensor_reduce(out=g2_[:, hf:hf + 1], in_=stg[hf][:], axis=AX.X, op=ALU.add), reads=[stg[hf]], writes=[g2_])
                P.tt("dve", gate[:, sbk:sbk + 1], g2_[:, 0:1], g2_[:, 1:2], ALU.add, [g2_], [gate])
            P.dma("sp", rrow[:, 0:NCTX], self.RANKT[e:e + 1, 0:NCTX].partition_broadcast(128), reads=[self.RANKT], writes=[rrow])
            P.dma("pool", arow[:, 0:NCTX], self.AFFT[e:e + 1, 0:NCTX].partition_broadcast(128), reads=[self.AFFT], writes=[arow])
            P.stt("dve", stg[0][:, 0:NCTX], rrow[:, 0:NCTX], self.iotc[:, 0:1], arow[:, 0:NCTX], ALU.is_equal, ALU.mult, [rrow, arow, self.iotc], [stg[0]])
            P.op("dve", lambda g: g.tensor_reduce(out=gate[:, 4:5], in_=stg[0][:, 0:NCTX], axis=AX.X, op=ALU.add), reads=[stg[0]], writes=[gate])
            for kg in range(2):
                for t in range(32):
                    h_ = h2t[t % 3]
                    P.dma("sp" if t % 2 == 0 else "pool", h_[:], self.H2[NCTX + t * 128:NCTX + (t + 1) * 128, kg * 1024:(kg + 1) * 1024],
                          reads=[self.H2], writes=[h_])
                    pm_ = perm[:, t % 4, :]
                    P.ts("dve" if t % 2 == 0 else "pool", pm_, self.iota[:, 0:512], rk[:, 2 + t, e:e + 1], None, ALU.is_equal, None,
                         [self.iota, rk], [perm])
                    for kk in range(8):
                        P.mm(pg[kk][:], h_[:, kk * 128:(kk + 1) * 128], pm_, t == 0, t == 31, [h_, perm], [pg[kk]])
                for kk in range(8):
                    P.act(HsT[:, kg * 8 + kk, 0:512], pg[kk][:], AF.Copy, [pg[kk]], [HsT])
                for t in range(2):
                    h_ = h2t[t % 3]
                    P.dma("sp", h_[:], self.H2[t * 128:(t + 1) * 128, kg * 1024:(kg + 1) * 1024], reads=[self.H2], writes=[h_])
                    for kk in range(8):
                        P.mm(pg[kk][:, 0:32], h_[:, kk * 128:(kk + 1) * 128], permc[:, t, :], t == 0, t == 1, [h_, permc], [pg[kk]])
                for kk in range(8):
                    P.act(HsT[:, kg * 8 + kk, 512:544], pg[kk][:, 0:32], AF.Copy, [pg[kk]], [HsT])
            q = 0
            for fc in range(8):
                for s0, n in ((0, 512), (512, 32)):
                    p1 = pg[(2 * q) % 8]; p3 = pg[(2 * q + 1) % 8]; q += 1
                    for kc in range(16):
                        P.mm(p1[:, 0:n], w1b[:, kc, fc * 128:(fc + 1) * 128], HsT[:, kc, s0:s0 + n], kc == 0, kc == 15, [w1b, HsT], [p1])
                    for kc in range(16):
                        P.mm(p3[:, 0:n], w3b[:, kc, fc * 128:(fc + 1) * 128], HsT[:, kc, s0:s0 + n], kc == 0, kc == 15, [w3b, HsT], [p3])
                    P.act(sa[:, 0:n], p1[:, 0:n], AF.Silu, [p1], [sa])
                    P.tt("dve", hidT[:, fc, s0:s0 + n], sa[:, 0:n], p3[:, 0:n], ALU.mult, [sa, p3], [hidT])
            q = 0
            for sbk in range(5):
                nr = 128 if sbk < 4 else 32
                s0 = sbk * 128
                y_ = yg[sbk % 2]
                for cb in range(4):
                    p_ = pg[q % 8]; q += 1
                    for fc in range(8):
                        P.mm(p_[0:nr, :], hidT[:, fc, s0:s0 + nr], w2b[:, fc, cb * 512:(cb + 1) * 512], fc == 0, fc == 7, [hidT, w2b], [p_])
                    P.ts("dve", y_[0:nr, cb * 512:(cb + 1) * 512], p_[0:nr, :], gate[0:nr, sbk:sbk + 1], None, ALU.mult, None, [p_, gate], [y_])
                P.dma("sp", self.YG[e][s0:s0 + nr, :], y_[0:nr, :], reads=[y_], writes=[self.YG])
    with P.scope():
        ygs = P.sb("cygs", [128, 4, NEL, 4, 512], BF16); ygc = P.sb("cygc", [32, 4, NEL, 512], BF16)
        rr = [P.sb("crr", [128, NEL, 128]) for _ in range(2)]; pT = [P.sb("cpT", [128, 4, NEL, 128], BF16) for _ in range(2)]
        pf = [P.ps("cpf", [128, 512]) for _ in range(4)]; fo = [P.sb("cfo", [128, D]) for _ in range(2)]
        for cb in range(4):
            cs_ = slice(cb * 512, (cb + 1) * 512)
            for e in range(NEL):
                P.dma("sp" if e % 2 == 0 else "pool", ygs[:, cb, e], self.YG[e][0:512, cs_].rearrange("(s p) n -> p s n", p=128), reads=[self.YG], writes=[ygs])
                P.dma("sp", ygc[:, cb, e, :], self.YG[e][512:544, cs_], reads=[self.YG], writes=[ygc])
        for t in range(NT):
            cols = slice(t * 128, (t + 1) * 128)
            r_ = rr[t % 2]; p_ = pT[t % 2]; o_ = fo[t % 2]
            P.dma("sp" if t % 2 == 0 else "pool", r_[:], self.RANKT[0:NEL, cols].partition_broadcast(128), reads=[self.RANKT], writes=[r_])
            if t >= 2:
                for sbk in range(4):
                    P.ts("dve" if sbk % 2 == 0 else "pool", p_[:, sbk], r_[:], self.iotc[:, sbk:sbk + 1], None, ALU.is_equal, None, [r_, self.iotc], [p_])
            else:
                P.ts("dve", p_[:, 0], r_[:], self.iotc[:, 0:1], None, ALU.is_equal, None, [r_, self.iotc], [p_])
            for cb in range(4):
                f_ = pf[cb]
                if t >= 2:
                    k = 0
                    for e in range(NEL):
                        for sbk in range(4):
                            P.mm(f_[:], p_[:, sbk, e, :], ygs[:, cb, e, sbk, :], k == 0, k == 4 * NEL - 1, [p_, ygs], [f_])
                            k += 1
                else:
                    for e in range(NEL):
                        P.mm(f_[:], p_[0:32, 0, e, :], ygc[0:32, cb, e, :], e == 0, e == NEL - 1, [p_, ygc], [f_])
                if cb % 2 == 0:
                    P.act(o_[:, cb * 512:(cb + 1) * 512], f_[:], AF.Copy, [f_], [o_])
                else:
                    P.op("dve", lambda g, o_=o_, f_=f_, cb=cb: g.tensor_copy(out=o_[:, cb * 512:(cb + 1) * 512], in_=f_[:]), reads=[f_], writes=[o_])
            P.dma("sp" if t % 2 == 0 else "pool", self.FFP[cols, :], o_[:], reads=[o_], writes=[self.FFP])
    _stage_resln(self, l, 2)


Net.stage_moe = _stage_moe
```

```python
import numpy as np
from contextlib import ExitStack, contextmanager
import concourse.bass as bass
import concourse.mybir as mybir
from concourse.bass_utils import run_bass_kernel_spmd

F32 = mybir.dt.float32
BF16 = mybir.dt.bfloat16
ALU = mybir.AluOpType
AF = mybir.ActivationFunctionType
AX = mybir.AxisListType

D = 2048
NTOK = 4352
NT = 34
NCTX = 256
NLAT = 4096
DEPTH = 4
ALPHA = (2 * DEPTH) ** 0.25
NIN = 5152
R = 4
UC = 1024 // R
NGP = 32 // R
NH = 8 // R
NEL = 16 // R
DC = D // R
ABC = 4 * NH
NINL = 5 * UC + ABC
RG = [[0, 1, 2, 3], [4, 5, 6, 7]]
CONST_NAMES = ["ident", "J", "ones", "tri_f", "tri_b", "ms_f", "ms_b", "mi_f", "mi_b", "blk_a", "blk_b", "blk_d", "lt"]


def make_consts():
    i = np.arange(128)
    s, j = np.meshgrid(i, i, indexing="ij")
    same = (s // 64) == (j // 64)
    c = {}
    c["ident"] = (s == j)
    c["J"] = (s + j == 127)
    c["ones"] = np.ones((128, 128), bool)
    c["tri_f"] = same & (s <= j)
    c["tri_b"] = same & (s >= j)
    c["ms_f"] = same & (s < j)
    c["ms_b"] = same & (s > j)
    c["mi_f"] = same & (s <= j)
    c["mi_b"] = same & (s >= j)
    c["blk_a"] = (s < 64)
    c["blk_b"] = (s >= 64)
    c["blk_d"] = same
    c["lt"] = (j < s)
    return np.concatenate([c[n].astype(np.float32) for n in CONST_NAMES], axis=1)


class T:
    __slots__ = ("h", "lw", "rd", "name")

    def __init__(self, h, name=""):
        self.h = h
        self.lw = None
        self.rd = []
        self.name = name

    def __getitem__(self, k):
        return self.h[k]


class Prog:
    NDMA = 8
    SAME_ENG = True

    def __init__(self, nc, es):
        self.nc = nc
        self.stack = [es]
        self.eng = {"pe": nc.tensor, "act": nc.scalar, "dve": nc.vector, "pool": nc.gpsimd, "sp": nc.sync}
        self.sem = {}
        self.cnt = {}
        for e in ("pe", "act", "dve", "pool"):
            self.sem[e] = es.enter_context(nc.semaphore("s_" + e))
            self.cnt[e] = 0
        self.dq = {}
        for q in ("sp", "act", "pool"):
            sems = []
            for i in range(self.NDMA):
                k = "d_%s_%d" % (q, i)
                self.sem[k] = es.enter_context(nc.semaphore(k))
                self.cnt[k] = 0
                sems.append(k)
            self.dq[q] = [sems, 0]
        self.known = {e: {} for e in self.eng}
        self.ninst = 0
        self.uid = 0

    @contextmanager
    def scope(self):
        es = ExitStack()
        self.stack.append(es)
        try:
            yield
        finally:
            self.barrier()
            self.stack.pop()
            es.close()

    def _nm(self, name):
        self.uid += 1
        return "%s_%d" % (name, self.uid)

    def sb(self, name, shape, dt=F32):
        return T(self.stack[-1].enter_context(self.nc.sbuf_tensor(self._nm(name), list(shape), dt)), name)

    def ps(self, name, shape, dt=F32):
        return T(self.stack[-1].enter_context(self.nc.psum_tensor(self._nm(name), list(shape), dt)), name)

    def dram(self, name, shape, dt=F32, kind="Internal"):
        return T(self.nc.dram_tensor(name, list(shape), dt, kind=kind), name)

    def _wait(self, e, dep):
        if dep is None:
            return
        k, v = dep
        if v <= 0 or self.known[e].get(k, 0) >= v:
            return
        self.eng[e].wait_ge(self.sem[k], v)
        self.known[e][k] = v

    def _skip(self, e, key):
        return key == e and (e == "pe" or not self.SAME_ENG)

    def _deps(self, e, reads, writes):
        for t in reads:
            if t.lw is not None and not self._skip(e, t.lw[0]):
                self._wait(e, t.lw)
        for t in writes:
            if t.lw is not None and not self._skip(e, t.lw[0]):
                self._wait(e, t.lw)
            for d in t.rd:
                if not self._skip(e, d[0]):
                    self._wait(e, d)

    def _mark(self, key, val, reads, writes):
        for t in reads:
            t.rd.append((key, val))
            if len(t.rd) > 32:
                m = {}
                for k, v in t.rd:
                    if m.get(k, 0) < v:
                        m[k] = v
                t.rd = list(m.items())
        for t in writes:
            t.lw = (key, val)
            t.rd = []

    def op(self, e, fn, reads=(), writes=()):
        self._deps(e, reads, writes)
        ins = fn(self.eng[e])
        self.cnt[e] += 1
        ins.then_inc(self.sem[e], 1)
        self._mark(e, self.cnt[e], reads, writes)
        self.ninst += 1
        return ins

    def dma(self, q, out, in_, reads=(), writes=(), **kw):
        sems, i = self.dq[q]
        k = sems[i % self.NDMA]
        self.dq[q][1] = i + 1
        self._wait(q, (k, self.cnt[k]))
        self._deps(q, reads, writes)
        ins = self.eng[q].dma_start(out=out, in_=in_, **kw)
        self.cnt[k] += 16
        ins.then_inc(self.sem[k], 16)
        self._mark(k, self.cnt[k], reads, writes)
        self.ninst += 1
        return ins

    def coll(self, kind, op, rg, in_ap, out_ap, reads, writes):
        q = "pool"
        k = "cc"
        if k not in self.sem:
            self.sem[k] = self.stack[0].enter_context(self.nc.semaphore("s_cc"))
            self.cnt[k] = 0
        self._wait(q, (k, self.cnt[k]))
        self._deps(q, reads, writes)
        ins = self.nc.gpsimd.collective_compute(kind, op, replica_groups=rg, ins=[in_ap], outs=[out_ap])
        self.cnt[k] += 1
        ins.then_inc(self.sem[k], 1)
        self._mark(k, self.cnt[k], reads, writes)
        self.ninst += 1
        return ins

    def barrier(self):
        for e in self.eng:
            for k in self.sem:
                self._wait(e, (k, self.cnt[k]))

    def mm(self, out_ap, lhsT_ap, rhs_ap, start, stop, reads, writes):
        return self.op("pe", lambda e: e.matmul(out_ap, lhsT=lhsT_ap, rhs=rhs_ap, start=start, stop=stop),
                       reads=reads, writes=writes)

    def act(self, out_ap, in_ap, func, reads, writes, **kw):
        return self.op("act", lambda e: e.activation(out=out_ap, in_=in_ap, func=func, **kw), reads=reads, writes=writes)

    def tt(self, e, out_ap, a_ap, b_ap, op, reads, writes):
        return self.op(e, lambda g: g.tensor_tensor(out=out_ap, in0=a_ap, in1=b_ap, op=op), reads=reads, writes=writes)

    def ts(self, e, out_ap, a_ap, s1, s2, op0, op1, reads, writes, accum=None):
        if op1 is None:
            return self.op(e, lambda g: g.tensor_scalar(out=out_ap, in0=a_ap, scalar1=s1, scalar2=None, op0=op0),
                           reads=reads, writes=writes)
        if accum is not None:
            return self.op(e, lambda g: g.tensor_scalar(out=out_ap, in0=a_ap, scalar1=s1, scalar2=s2, op0=op0, op1=op1,
                                                        accum_out=accum), reads=reads, writes=writes)
        return self.op(e, lambda g: g.tensor_scalar(out=out_ap, in0=a_ap, scalar1=s1, scalar2=s2, op0=op0, op1=op1),
                       reads=reads, writes=writes)

    def stt(self, e, out_ap, a_ap, s, b_ap, op0, op1, reads, writes):
        return self.op(e, lambda g: g.scalar_tensor_tensor(out=out_ap, in0=a_ap, scalar=s, in1=b_ap, op0=op0, op1=op1),
                       reads=reads, writes=writes)


RT = [1, 0] + [35 - j for j in range(2, NT)]
RTINV = {t: j for j, t in enumerate(RT)}


class Net:
    def __init__(self, nlayers=DEPTH, dump=None, upto=None, L=DEPTH, EW=NEL):
        self.L = L
        self.EW = EW
        self.nlayers = nlayers
        self.dump = dump or []
        self.upto = upto

    def build(self):
        nc = bass.Bass("TRN2", target_bir_lowering=False)
        self.nc = nc
        with ExitStack() as es:
            P = Prog(nc, es)
            self.P = P
            self.declare()
            self.body()
            P.barrier()
        return nc

    def declare(self):
        P = self.P
        L = self.L
        ext = lambda n, s: P.dram(n, s, F32, kind="ExternalInput")
        self.xin = ext("xin", [NTOK, D])
        self.cT = ext("cT", [128, 32])
        self.consts = ext("consts", [128, 128 * len(CONST_NAMES)])
        self.iotar = ext("iotar", [128, 544])
        self.iotac = ext("iotac", [128, 8])
        self.ada_w = ext("ada_w", [L, D, 6 * DC])
        self.ada_b = ext("ada_b", [L, 6 * DC])
        self.w_in = ext("w_in", [L, D, NINL])
        self.w_out = ext("w_out", [L, D, DC])
        self.lam_re = ext("s5_lam_re", [L, 2, 2 * NGP, 64])
        self.lam_im = ext("s5_lam_im", [L, 2, 2 * NGP, 64])
        self.log_step = ext("s5_log_step", [L, 2, 2 * NGP])
        self.b_re = ext("s5_b_re", [L, 2, 2 * NGP, 64, 16])
        self.b_im = ext("s5_b_im", [L, 2, 2 * NGP, 64, 16])
        self.c_re = ext("s5_c_re", [L, 2, 2 * NGP, 16, 64])
        self.c_im = ext("s5_c_im", [L, 2, 2 * NGP, 16, 64])
        self.s5_d = ext("s5_d", [L, UC])
        self.glu_w = ext("s5_glu_w", [L, 1024, UC])
        self.glu_b = ext("s5_glu_b", [L, UC])
        self.conv_w = ext("dn_conv_w", [L, 5, 3 * UC])
        self.a_log = ext("dn_a_log", [L, 2 * NH])
        self.dt_bias = ext("dn_dt_bias", [L, 2 * NH])
        self.norm_w = ext("dn_norm_w", [L, 128])
        self.ln1_g = ext("ln1_g", [L, D]); self.ln1_b = ext("ln1_b", [L, D])
        self.ln2_g = ext("ln2_g", [L, D]); self.ln2_b = ext("ln2_b", [L, D])
        self.router_w = ext("router_w", [L, D, 16])
        self.w1 = ext("exp_w1", [L, self.EW, D, 1024])
        self.w3 = ext("exp_w3", [L, self.EW, D, 1024])
        self.w2 = ext("exp_w2", [L, self.EW, 1024, D])
        self.yout = P.dram("yout", [NLAT, D], F32, kind="ExternalOutput")
        self.X = P.dram("X", [NTOK, D])
        self.MODL = P.dram("MODL", [L * 2, 6 * DC]); self.MOD = P.dram("MODF", [R * L * 2, 6 * DC])
        self.PT = P.dram("PT", [4 * UC, NTOK])
        self.UR = P.dram("UR", [UC, NTOK])
        self.Z = P.dram("Z", [NTOK, UC])
        self.AB = P.dram("AB", [NTOK, ABC])
        self.GT = P.dram("GT", [UC, NTOK]); self.GTF = P.dram("GTF", [NGP, R * 32, NTOK])
        self.S5TF = P.dram("S5TF", [UC // 64, R * 64, NTOK], BF16); self.DNOF = P.dram("DNOF", [R * NTOK, UC])
        self.MIXL = P.dram("MIXL", [NTOK, DC]); self.MIXF = P.dram("MIXF", [R * NTOK, DC]); self.FFP = P.dram("FFP", [NTOK, D])
        self.H2 = P.dram("H2", [NTOK, D], BF16); self.H2t = [T(self.H2.h, "H2t") for _ in range(NT)]; self.AFF = P.dram("AFF", [NTOK, 16]); self.AFFT = P.dram("AFFT", [16, NTOK]); self.AFFw = [T(self.AFF.h), T(self.AFF.h)]; self.AFFTw = [T(self.AFFT.h), T(self.AFFT.h)]
        self.RANKD = P.dram("RANKD", [NTOK, 16]); self.RANKT = P.dram("RANKT", [16, NTOK])
        self.YG = P.dram("YG", [NEL, 544, D], BF16); self.FF = P.dram("FF", [NTOK, D]); self.FFt = [T(self.FF.h, "FFt") for _ in range(NT)]
        self.S5T = P.dram("S5T", [UC, NTOK], BF16)
        self.QT = P.dram("QT", [UC, NTOK]); self.KT = P.dram("KT", [UC, NTOK])
        self.KTOK = P.dram("KTOK", [NTOK, UC]); self.VTOK = P.dram("VTOK", [NTOK, UC])
        self.GB = P.dram("GB", [NTOK, 4 * NH]); self.OD = P.dram("OD", [2, NTOK, UC]); self.ODt = [T(self.OD.h, "OD0"), T(self.OD.h, "OD1")]; self.DNO = P.dram("DNO", [NTOK, UC])
        self.dbg = {}
        for n, shp in self.dump:
            self.dbg[n] = P.dram("dbg_" + n, shp, F32, kind="ExternalOutput")

    def cst(self, name):
        i = CONST_NAMES.index(name)
        return self.C[:, i * 128:(i + 1) * 128]

    def body(self):
        P = self.P
        with P.scope():
            self.C = P.sb("C", [128, 128 * len(CONST_NAMES)])
            P.dma("sp", self.C[:], self.consts[:], reads=[self.consts], writes=[self.C])
            self.Cb = P.sb("Cb", [128, 256], BF16)
            P.op("dve", lambda e: e.tensor_copy(out=self.Cb[:], in_=self.C[:, 0:256]), reads=[self.C], writes=[self.Cb])
            self.iota = P.sb("iota", [128, 544])
            P.dma("sp", self.iota[:], self.iotar[:], reads=[self.iotar], writes=[self.iota])
            self.iotc = P.sb("iotc", [128, 8])
            P.dma("sp", self.iotc[:], self.iotac[:], reads=[self.iotac], writes=[self.iotc])
            self.copy_x()
            self.stage_mod()
            if self.upto == "mod":
                return
            for l in range(self.nlayers):
                self.stage_inproj(l)
                if self.upto == "inproj":
                    return
                self.stage_s5(l)
                if self.upto == "s5":
                    return
                self.stage_dn(l)
                if self.upto == "dn":
                    return
                self.stage_glu(l)
                self.stage_outproj(l)
                if self.upto == "outproj":
                    return
                self.stage_moe(l)
            self.write_out()

    def copy_x(self):
        P = self.P
        with P.scope():
            tb = [P.sb("cx", [128, D]) for _ in range(2)]
            for i in range(NT):
                t = tb[i % 2]
                P.dma("sp", t[:], self.xin[i * 128:(i + 1) * 128, :], reads=[self.xin], writes=[t])
                P.dma("pool", self.X[i * 128:(i + 1) * 128, :], t[:], reads=[t], writes=[self.X])

    def write_out(self):
        P = self.P
        with P.scope():
            tb = [P.sb("wo", [128, D]) for _ in range(2)]
            for i in range(2, NT):
                t = tb[i % 2]
                P.dma("sp", t[:], self.X[i * 128:(i + 1) * 128, :], reads=[self.X], writes=[t])
                P.dma("pool", self.yout[(i - 2) * 128:(i - 1) * 128, :], t[:], reads=[t], writes=[self.yout])

    def dump_dram(self, name, src_ap_fn, rows, cols, src):
        if name not in self.dbg:
            return
        P = self.P
        dst = self.dbg[name]
        with P.scope():
            tb = [P.sb("dd", [128, cols]) for _ in range(2)]
            for r0 in range(0, rows, 128):
                n = min(128, rows - r0)
                t = tb[(r0 // 128) % 2]
                P.dma("sp", t[0:n, :], src_ap_fn(r0, n), reads=[src], writes=[t])
                P.dma("sp", dst[r0:r0 + n, :], t[0:n, :], reads=[t], writes=[dst])

    def stage_mod(self):
        P = self.P
        W = 6 * DC
        with P.scope():
            ct = P.sb("ct", [128, 32])
            P.dma("sp", ct[:], self.cT[:], reads=[self.cT], writes=[ct])
            sc = P.sb("sc", [128, 32])
            P.act(sc[:], ct[:], AF.Silu, [ct], [sc])
            wb = [P.sb("adw", [128, 16, 512]) for _ in range(2)]
            pm = [P.ps("pmod", [2, 512]) for _ in range(2)]
            row = P.sb("modrow", [2, W])
            bia = P.sb("modb", [2, W])
            for l in range(self.nlayers):
                P.dma("pool", bia[:], self.ada_b[l:l + 1, :].partition_broadcast(2), reads=[self.ada_b], writes=[bia])
                for nb in range(W // 512):
                    w = wb[nb % 2]
                    p = pm[nb % 2]
                    src = self.ada_w[l].rearrange("(k p) n -> p k n", p=128)[:, :, nb * 512:(nb + 1) * 512]
                    P.dma("sp" if nb % 2 == 0 else "pool", w[:], src, reads=[self.ada_w], writes=[w])
                    for kc in range(16):
                        P.mm(p[:], sc[:, kc:32:16], w[:, kc, :], kc == 0, kc == 15, [sc, w], [p])
                    P.tt("dve", row[:, nb * 512:(nb + 1) * 512], p[:], bia[:, nb * 512:(nb + 1) * 512], ALU.add,
                         [p, bia], [row])
                for sg in (1, 4):
                    P.ts("dve", row[:, sg * DC:(sg + 1) * DC], row[:, sg * DC:(sg + 1) * DC], 1.0, None, ALU.add, None, [row], [row])
                P.dma("sp", self.MODL[l * 2:(l + 1) * 2, :], row[:], reads=[row], writes=[self.MODL])
            P.coll("AllGather", ALU.bypass, RG, self.MODL[:, :], self.MOD[:, :], [self.MODL], [self.MOD])

    def mod_load(self, q, dst, l, row, seg):
        v = self.MOD[:, :].rearrange("(r l two) (s j) -> l two s r j", r=R, l=self.L, two=2, s=6)[l][row][seg]
        self.P.dma(q, dst[:, :].rearrange("p (r j) -> p r j", r=R), v.partition_broadcast(128), reads=[self.MOD], writes=[dst])

    def bcast_load(self, q, dst, src_row_ap, src_t):
        self.P.dma(q, dst[:], src_row_ap.partition_broadcast(128), reads=[src_t], writes=[dst])

    def stage_inproj(self, l):
        P = self.P
        fwd_tiles = list(range(NT))
        passes = [(fwd_tiles[:12], False, 0), (fwd_tiles[12:24], False, 12 * 128), (fwd_tiles[24:], False, 24 * 128),
                  (RT[:12], True, 0), (RT[12:24], True, 12 * 128), (RT[24:], True, 24 * 128)]
        with P.scope():
            modt = {}
            for who, row in (("lat", 0), ("ctx", 1)):
                scp = P.sb("scp", [128, D]); sh = P.sb("sh", [128, D])
                self.mod_load("sp", scp, l, row, 1)
                self.mod_load("pool", sh, l, row, 0)
                modt[who] = (scp, sh)
            hinT = P.sb("hinT", [128, 16, 12 * 128], BF16)
            xt = [P.sb("xt", [128, D]) for _ in range(2)]
            hb = [P.sb("hb", [128, D], BF16) for _ in range(2)]
            ptr = [P.ps("ptr", [128, 512]) for _ in range(2)]
            pacc = [P.ps("pacc", [128, 512]) for _ in range(3)]
            wst = [P.sb("wst", [128, 16, 128]) for _ in range(2)]
            wbf = [P.sb("wbf", [128, 16, 128], BF16) for _ in range(2)]
            wst2 = P.sb("wst2", [128, 16, 256])
            wbf2 = P.sb("wbf2", [128, 16, 256], BF16)
            ot = [P.sb("ot", [128, 12 * 128]) for _ in range(2)]
            ot2 = [P.sb("ot2", [128, 512]) for _ in range(2)]
            win = self.w_in[l].rearrange("(k p) n -> p k n", p=128)
            for tiles, rev, col0 in passes:
                for ti, t in enumerate(tiles):
                    x = xt[ti % 2]; h = hb[ti % 2]
                    scp, sh = modt["ctx" if t < 2 else "lat"]
                    P.dma("sp" if ti % 2 == 0 else "pool", x[:], self.X[t * 128:(t + 1) * 128, :], reads=[self.X], writes=[x])
                    P.tt("dve", x[:], x[:], scp[:], ALU.mult, [x, scp], [x])
                    P.tt("dve", h[:], x[:], sh[:], ALU.add, [x, sh], [h])
                    idm = self.Cb[:, 128:256] if rev else self.Cb[:, 0:128]
                    for kg in range(4):
                        pt = ptr[kg % 2]
                        for kk in range(4):
                            kc = kg * 4 + kk
                            P.mm(pt[:, kk * 128:(kk + 1) * 128], h[:, kc * 128:(kc + 1) * 128], idm, True, True, [h, self.Cb], [pt])
                        P.act(hinT[:, kg * 4:(kg + 1) * 4, ti * 128:(ti + 1) * 128],
                              pt[:].rearrange("p (k t) -> p k t", k=4), AF.Copy, [pt], [hinT])
                ntk = len(tiles) * 128
                noc = (UC // 128) if rev else (4 * UC // 128)
                dst = self.UR if rev else self.PT
                for oc in range(noc):
                    ws = wst[oc % 2]; wb = wbf[oc % 2]; o = ot[oc % 2]
                    P.dma("sp" if oc % 2 == 0 else "pool", ws[:], win[:, :, oc * 128:(oc + 1) * 128], reads=[self.w_in], writes=[ws])
                    P.op("dve", lambda e, wb=wb, ws=ws: e.tensor_copy(out=wb[:], in_=ws[:]), reads=[ws], writes=[wb])
                    for tb in range(0, ntk, 512):
                        n = min(512, ntk - tb)
                        pa = pacc[(tb // 512) % 3]
                        for kc in range(16):
                            P.mm(pa[:, 0:n], wb[:, kc, :], hinT[:, kc, tb:tb + n], kc == 0, kc == 15, [wb, hinT], [pa])
                        P.act(o[:, tb:tb + n], pa[:, 0:n], AF.Copy, [pa], [o])
                    P.dma("sp", dst[oc * 128:(oc + 1) * 128, col0:col0 + ntk], o[:, 0:ntk], reads=[o], writes=[dst])
                if rev:
                    continue
                nzb = UC // 256
                for zb in range(nzb + 1):
                    c0 = 4 * UC + zb * 256
                    ncol = 256 if zb < nzb else ABC
                    P.dma("sp", wst2[:, :, 0:ncol], win[:, :, c0:c0 + ncol], reads=[self.w_in], writes=[wst2])
                    P.op("dve", lambda e, ncol=ncol: e.tensor_copy(out=wbf2[:, :, 0:ncol], in_=wst2[:, :, 0:ncol]), reads=[wst2], writes=[wbf2])
                    for ti, t in enumerate(tiles):
                        pa = pacc[ti % 3]; o2 = ot2[ti % 2]
                        for kc in range(16):
                            P.mm(pa[:, 0:ncol], hinT[:, kc, ti * 128:(ti + 1) * 128], wbf2[:, kc, 0:ncol], kc == 0, kc == 15, [wbf2, hinT], [pa])
                        P.act(o2[:, 0:ncol], pa[:, 0:ncol], AF.Copy, [pa], [o2])
                        if zb < nzb:
                            P.dma("pool", self.Z[t * 128:(t + 1) * 128, zb * 256:(zb + 1) * 256], o2[:, 0:256], reads=[o2], writes=[self.Z])
                        else:
                            P.dma("pool", self.AB[t * 128:(t + 1) * 128, :], o2[:, 0:ABC], reads=[o2], writes=[self.AB])


def prep_inputs(inputs, b, r=0, L=DEPTH, EW=None):
    f = lambda a: np.ascontiguousarray(np.asarray(a, dtype=np.float32))
    g = lambda k: np.asarray(inputs[k])[:L]
    m = {}
    m["xin"] = f(np.concatenate([inputs["ctx"][b], inputs["x"][b]], axis=0))
    cT = np.concatenate([np.asarray(inputs["c"][b]).reshape(16, 128).T, np.asarray(inputs["c_ctx"]).reshape(16, 128).T], axis=1)
    m["cT"] = f(cT)
    m["consts"] = make_consts()
    m["iotar"] = f(np.tile(np.arange(544, dtype=np.float32)[None, :], (128, 1)))
    m["iotac"] = f(np.arange(128, dtype=np.float32)[:, None] + 128.0 * np.arange(8, dtype=np.float32)[None, :])
    for k in ["dn_norm_w", "ln1_g", "ln1_b", "ln2_g", "ln2_b"]:
        m[k] = f(g(k))
    aw = g("ada_w"); ab_ = g("ada_b")
    m["ada_w"] = f(np.concatenate([aw[:, :, sg * D + r * DC:sg * D + (r + 1) * DC] for sg in range(6)], axis=2))
    m["ada_b"] = f(np.concatenate([ab_[:, sg * D + r * DC:sg * D + (r + 1) * DC] for sg in range(6)], axis=1))
    cu = slice(r * UC, (r + 1) * UC)
    heads = list(range(r * NH, (r + 1) * NH))
    abcols = [5120 + d * 16 + k * 8 + h for d in range(2) for k in range(2) for h in heads]
    w_in = g("w_in")
    m["w_in"] = f(np.concatenate([w_in[:, :, j * 1024 + r * UC:j * 1024 + (r + 1) * UC] for j in range(5)] + [w_in[:, :, abcols]], axis=2))
    m["w_out"] = f(g("w_out")[:, :, r * DC:(r + 1) * DC])
    gs = slice(r * 2 * NGP, (r + 1) * 2 * NGP)
    for k in ["s5_lam_re", "s5_lam_im", "s5_log_step", "s5_b_re", "s5_b_im", "s5_c_re", "s5_c_im"]:
        m[k] = f(g(k)[:, :, gs])
    m["s5_d"] = f(g("s5_d")[:, cu])
    m["s5_glu_w"] = f(g("s5_glu_w")[:, :, cu])
    m["s5_glu_b"] = f(g("s5_glu_b")[:, cu])
    cw = g("dn_conv_w")
    m["dn_conv_w"] = f(np.concatenate([cw[:, :, j * 1024 + r * UC:j * 1024 + (r + 1) * UC] for j in range(3)], axis=2))
    m["dn_a_log"] = f(g("dn_a_log")[:, :, heads].reshape(L, 2 * NH))
    m["dn_dt_bias"] = f(g("dn_dt_bias")[:, :, heads].reshape(L, 2 * NH))
    eo = list(range(r * NEL, (r + 1) * NEL)) + [e for e in range(16) if not (r * NEL <= e < (r + 1) * NEL)]
    m["router_w"] = f(g("router_w")[:, :, eo])
    ne = NEL if EW is None else EW
    for k in ["exp_w1", "exp_w3", "exp_w2"]:
        m[k] = f(g(k)[:, r * NEL:r * NEL + ne])
    return m


def kernel(**inputs):
    net = Net()
    nc = net.build()
    in_maps = [prep_inputs(inputs, c // R, c % R) for c in range(2 * R)]
    res = run_bass_kernel_spmd(nc, in_maps, core_ids=list(range(2 * R)))
    return np.stack([np.asarray(res.results[b * R]["yout"], dtype=np.float32) for b in range(2)], axis=0)


MAGIC = 12582912.0
TWO_PI = 6.283185307179586


def _s5_prep(self, l):
    P = self.P
    ident = self.cst("ident")
    pr = {}
    pt = P.ps("s5pp", [128, 512])
    G = NGP
    pr["rho"] = [P.sb("s5rho", [128, G]) for _ in range(2)]; pr["f"] = [P.sb("s5f", [128, G]) for _ in range(2)]
    pr["BrT"] = [P.sb("s5BrT", [32, G, 128], BF16) for _ in range(2)]; pr["BiT"] = [P.sb("s5BiT", [32, G, 128], BF16) for _ in range(2)]
    pr["CTr"] = [P.sb("s5CTr", [128, G, 32], BF16) for _ in range(2)]; pr["CTi"] = [P.sb("s5CTi", [128, G, 32], BF16) for _ in range(2)]
    pr["dsk"] = P.sb("s5dsk", [32, G])

    def one_dir(d):
        rho = pr["rho"][d]; f = pr["f"][d]; BrT = pr["BrT"][d]; BiT = pr["BiT"][d]; CTr = pr["CTr"][d]; CTi = pr["CTi"][d]
        A = P.sb("s5A", [G, 3, 128])
        P.dma("sp", A[:, 0, :], self.lam_re[l][d].rearrange("(gp two) p -> gp (two p)", two=2), reads=[self.lam_re], writes=[A])
        P.dma("sp", A[:, 1, :], self.lam_im[l][d].rearrange("(gp two) p -> gp (two p)", two=2), reads=[self.lam_im], writes=[A])
        ls = P.sb("s5ls", [G, 2])
        P.dma("sp", ls[:], self.log_step[l][d:d + 1, :].rearrange("o (gp two) -> (o gp) two", two=2), reads=[self.log_step], writes=[ls])
        P.op("dve", lambda e, A=A, ls=ls: e.tensor_copy(out=A[:, 2, :].rearrange("g (t p) -> g t p", t=2),
                                                        in_=ls[:, :].unsqueeze(2).to_broadcast([G, 2, 64])), reads=[ls], writes=[A])
        q = P.sb("s5q", [128, 3, G])
        for k in range(3):
            P.mm(pt[:, k * G:(k + 1) * G], A[:, k, :], ident[0:G, 0:G], True, True, [A, self.C], [pt])
        P.act(q[:].rearrange("p k g -> p (k g)"), pt[:, 0:3 * G], AF.Copy, [pt], [q])
        lr = q[:, 0, :]; li = q[:, 1, :]
        dl = P.sb("s5dl", [128, G])
        P.act(dl[:], q[:, 2, :], AF.Exp, [q], [dl])
        P.tt("dve", rho[:], lr, dl[:], ALU.mult, [q, dl], [rho])
        P.act(rho[:], rho[:], AF.Exp, [rho], [rho])
        P.tt("dve", f[:], li, dl[:], ALU.mult, [q, dl], [f])
        P.ts("dve", f[:], f[:], 1.0 / TWO_PI, None, ALU.mult, None, [f], [f])
        w = P.sb("s5w", [128, 6, G])
        for k, off in ((0, 0.0), (1, 0.25)):
            P.ts("dve", w[:, 2, :], f[:], off, None, ALU.add, None, [f], [w])
            P.ts("dve", w[:, 3, :], w[:, 2, :], MAGIC, MAGIC, ALU.add, ALU.subtract, [w], [w])
            P.tt("dve", w[:, 2, :], w[:, 2, :], w[:, 3, :], ALU.subtract, [w], [w])
            P.act(w[:, k, :], w[:, 2, :], AF.Sin, [w], [w], scale=TWO_PI)
        sn = w[:, 0, :]; cs = w[:, 1, :]
        nr = P.sb("s5nr", [128, G]); ni = P.sb("s5ni", [128, G]); den = P.sb("s5den", [128, G])
        cr = P.sb("s5cr", [128, G]); ci = P.sb("s5ci", [128, G]); tmp = P.sb("s5tmp", [128, G])
        P.tt("dve", nr[:], rho[:], cs, ALU.mult, [rho, w], [nr])
        P.ts("dve", nr[:], nr[:], -1.0, None, ALU.add, None, [nr], [nr])
        P.tt("dve", ni[:], rho[:], sn, ALU.mult, [rho, w], [ni])
        P.tt("dve", den[:], lr, lr, ALU.mult, [q], [den])
        P.tt("dve", tmp[:], li, li, ALU.mult, [q], [tmp])
        P.tt("dve", den[:], den[:], tmp[:], ALU.add, [den, tmp], [den])
        P.op("dve", lambda e, den=den: e.reciprocal(out=den[:], in_=den[:]), reads=[den], writes=[den])
        P.tt("dve", cr[:], nr[:], lr, ALU.mult, [nr, q], [cr])
        P.tt("dve", tmp[:], ni[:], li, ALU.mult, [ni, q], [tmp])
        P.tt("dve", cr[:], cr[:], tmp[:], ALU.add, [cr, tmp], [cr])
        P.tt("dve", cr[:], cr[:], den[:], ALU.mult, [cr, den], [cr])
        P.tt("dve", ci[:], ni[:], lr, ALU.mult, [ni, q], [ci])
        P.tt("dve", tmp[:], nr[:], li, ALU.mult, [nr, q], [tmp])
        P.tt("dve", ci[:], ci[:], tmp[:], ALU.subtract, [ci, tmp], [ci])
        P.tt("dve", ci[:], ci[:], den[:], ALU.mult, [ci, den], [ci])
        Bl = P.sb("s5Bl", [128, 2, G, 16])
        P.dma("sp", Bl[:, 0], self.b_re[l][d].rearrange("(gp two) p c -> (two p) gp c", two=2), reads=[self.b_re], writes=[Bl])
        P.dma("pool", Bl[:, 1], self.b_im[l][d].rearrange("(gp two) p c -> (two p) gp c", two=2), reads=[self.b_im], writes=[Bl])
        S = P.sb("s5S", [128, 2, G, 32])
        P.op("pool", lambda e, S=S: e.memset(S[:], 0.0), writes=[S])
        t1 = P.sb("s5t1", [128, G, 16]); t2 = P.sb("s5t2", [128, G, 16])
        crb = cr[:, :].unsqueeze(2).to_broadcast([128, G, 16]); cib = ci[:, :].unsqueeze(2).to_broadcast([128, G, 16])
        for k, (a0, a1, op) in enumerate(((0, 1, ALU.subtract), (1, 0, ALU.add))):
            P.tt("dve", t1[:], Bl[:, a0], crb, ALU.mult, [Bl, cr], [t1])
            P.tt("dve", t2[:], Bl[:, a1], cib, ALU.mult, [Bl, ci], [t2])
            for half in range(2):
                ps_ = slice(half * 64, half * 64 + 64)
                P.tt("dve", S[ps_, k, :, half * 16:half * 16 + 16], t1[ps_], t2[ps_], op, [t1, t2], [S])
        for k, dst in ((0, BrT), (1, BiT)):
            for g4 in range(G // 4):
                for gg in range(4):
                    gp = g4 * 4 + gg
                    P.mm(pt[0:32, gg * 128:(gg + 1) * 128], S[:, k, gp, :], ident, True, True, [S, self.C], [pt])
                P.act(dst[:, g4 * 4:(g4 + 1) * 4, :], pt[0:32, :].rearrange("p (g s) -> p g s", g=4), AF.Copy, [pt], [dst])
        Cx = P.sb("s5Cx", [64, 2, G, 128])
        P.op("pool", lambda e, Cx=Cx: e.memset(Cx[:], 0.0), writes=[Cx])
        for k, src in ((0, self.c_re), (1, self.c_im)):
            v = src[l][d].rearrange("(gp two) c p -> two c gp p", two=2)
            P.dma("sp", Cx[0:16, k, :, 0:64], v[0], reads=[src], writes=[Cx])
            P.dma("pool", Cx[32:48, k, :, 64:128], v[1], reads=[src], writes=[Cx])
        for k, dst, scl in ((0, CTr, 1.0), (1, CTi, -1.0)):
            for g8 in range(G // 8):
                for gg in range(8):
                    gp = g8 * 8 + gg
                    P.mm(pt[:, gg * 64:(gg + 1) * 64], Cx[:, k, gp, :], ident[0:64, 0:64], True, True, [Cx, self.C], [pt])
                P.act(dst[:, g8 * 8:(g8 + 1) * 8, :].rearrange("p g (t c) -> p g t c", t=2),
                      pt[:, :].rearrange("p (g t c) -> p g t c", g=8, t=2)[:, :, :, 0:16], AF.Copy, [pt], [dst], scale=scl)
    for d in range(2):
        with P.scope():
            one_dir(d)
    dA = P.sb("s5dA", [G, 32])
    P.dma("sp", dA[:], self.s5_d[l:l + 1, :].rearrange("o (gp j) -> (o gp) j", j=32), reads=[self.s5_d], writes=[dA])
    P.mm(pt[0:32, 0:G], dA[:], ident[0:G, 0:G], True, True, [dA, self.C], [pt])
    dsk = pr["dsk"]
    P.act(dsk[:], pt[0:32, 0:G], AF.Copy, [pt], [dsk])
    return pr


def _stage_s5(self, l):
    P = self.P
    J = self.cst("J")
    with P.scope():
        pr = _s5_prep(self, l)
        H = [[P.sb("s5H", [128, NTOK], BF16) for _ in range(2)] for _ in range(2)]
        uf = [P.sb("s5uf", [32, NTOK]) for _ in range(2)]
        ub = [P.sb("s5ub", [32, NTOK], BF16) for _ in range(2)]
        DB = []
        for d in range(2):
            b = {"cos": P.sb("s5cos", [128, 513]), "sin": P.sb("s5sin", [128, 513]), "rho": P.sb("s5rhoT", [128, 512]),
                 "ph": P.sb("s5ph", [128, 513]), "rr": P.sb("s5rr", [128, 513]),
                 "px": [P.ps("s5px", [128, 512]) for _ in range(2)],
                 "tm": [P.sb("s5tm", [128, 512]) for _ in range(4)], "xt": [P.sb("s5xt", [128, 512]) for _ in range(2)],
                 "g": [[P.sb("s5g", [128, 512]) for _ in range(2)] for _ in range(2)], "ini": [P.sb("s5ini", [128, 4]) for _ in range(2)]}
            DB.append(b)
        pf = P.ps("s5pf", [32, 512]); pb = P.ps("s5pb", [128, 128]); pj = P.ps("s5pj", [32, 512])
        ybt = P.sb("s5ybt", [128, 32]); ybs = P.sb("s5ybs", [32, 512]); yy = P.sb("s5yy", [32, 512]); ww = P.sb("s5ww", [32, 512])
        GTt = P.sb("s5GT", [32, NTOK])
        srcs = [self.PT, self.UR]

        def dir_gen(d, gp):
            b = DB[d]
            cosT, sinT, rhoT, ph, rr, px, tm, xt_, gg_, ini = b["cos"], b["sin"], b["rho"], b["ph"], b["rr"], b["px"], b["tm"], b["xt"], b["g"], b["ini"]
            f = pr["f"][d]; rho = pr["rho"][d]
            P.ts("dve", ph[:], self.iota[:, 0:513], f[:, gp:gp + 1], None, ALU.mult, None, [self.iota, f], [ph])
            yield
            for dst, off in ((sinT, 0.0), (cosT, 0.25)):
                if off:
                    P.ts("dve", ph[:], ph[:], off, None, ALU.add, None, [ph], [ph])
                    yield
                P.ts("dve", rr[:], ph[:], MAGIC, MAGIC, ALU.add, ALU.subtract, [ph], [rr])
                yield
                P.tt("dve", rr[:], ph[:], rr[:], ALU.subtract, [ph, rr], [rr])
                yield
                P.act(dst[:], rr[:], AF.Sin, [rr], [dst], scale=TWO_PI)
                yield
            P.ts("dve", rhoT[:], self.iota[:, 0:512], 0.0, rho[:, gp:gp + 1], ALU.mult, ALU.add, [self.iota, rho], [rhoT])
            nch = 9
            for k in range(nch):
                c0 = k * 512
                n = min(512, NTOK - c0)
                xr = px[0]; xi = px[1]
                P.mm(xr[:, 0:n], pr["BrT"][d][:, gp, :], ub[d][:, c0:c0 + n], True, True, [pr["BrT"][d], ub[d]], [xr])
                P.mm(xi[:, 0:n], pr["BiT"][d][:, gp, :], ub[d][:, c0:c0 + n], True, True, [pr["BiT"][d], ub[d]], [xi])
                yield
                c = cosT[:, 0:n]; s = sinT[:, 0:n]
                P.tt("dve", tm[0][:, 0:n], xr[:, 0:n], c, ALU.mult, [xr, cosT], [tm[0]])
                P.tt("dve", tm[1][:, 0:n], xi[:, 0:n], s, ALU.mult, [xi, sinT], [tm[1]])
                yield
                P.tt("pool", xt_[0][:, 0:n], tm[0][:, 0:n], tm[1][:, 0:n], ALU.add, [tm[0], tm[1]], [xt_[0]])
                P.tt("dve", tm[2][:, 0:n], xi[:, 0:n], c, ALU.mult, [xi, cosT], [tm[2]])
                P.tt("dve", tm[3][:, 0:n], xr[:, 0:n], s, ALU.mult, [xr, sinT], [tm[3]])
                yield
                P.tt("pool", xt_[1][:, 0:n], tm[2][:, 0:n], tm[3][:, 0:n], ALU.subtract, [tm[2], tm[3]], [xt_[1]])
                g = gg_[k % 2]
                icur = ini[k % 2]; inxt = ini[(k + 1) % 2]
                for ri in range(2):
                    init = 0.0 if k == 0 else icur[:, ri:ri + 1]
                    P.op("dve", lambda e, g=g, ri=ri, init=init, n=n: e.tensor_tensor_scan(
                        out=g[ri][:, 0:n], data0=rhoT[:, 0:n], data1=xt_[ri][:, 0:n], initial=init,
                        op0=ALU.mult, op1=ALU.add), reads=[rhoT, xt_[ri]] + ([icur] if k else []), writes=[g[ri]])
                    yield
                if k + 1 < nch:
                    grl = g[0][:, n - 1:n]; gil = g[1][:, n - 1:n]; cT = cosT[:, n:n + 1]; sT = sinT[:, n:n + 1]
                    P.ts("dve", inxt[:, 2:3], gil, sT, None, ALU.mult, None, [g[1], sinT], [inxt])
                    yield
                    P.stt("dve", inxt[:, 0:1], grl, cT, inxt[:, 2:3], ALU.mult, ALU.subtract, [g[0], cosT, inxt], [inxt])
                    yield
                    P.ts("dve", inxt[:, 3:4], gil, cT, None, ALU.mult, None, [g[1], cosT], [inxt])
                    yield
                    P.stt("dve", inxt[:, 1:2], grl, sT, inxt[:, 3:4], ALU.mult, ALU.add, [g[0], sinT, inxt], [inxt])
                    yield
                P.tt("pool", tm[0][:, 0:n], g[0][:, 0:n], c, ALU.mult, [g[0], cosT], [tm[0]])
                P.tt("dve", tm[1][:, 0:n], g[1][:, 0:n], s, ALU.mult, [g[1], sinT], [tm[1]])
                yield
                P.tt("pool", H[d][0][:, c0:c0 + n], tm[0][:, 0:n], tm[1][:, 0:n], ALU.subtract, [tm[0], tm[1]], [H[d][0]])
                P.tt("dve", tm[2][:, 0:n], g[0][:, 0:n], s, ALU.mult, [g[0], sinT], [tm[2]])
                yield
                P.tt("pool", tm[3][:, 0:n], g[1][:, 0:n], c, ALU.mult, [g[1], cosT], [tm[3]])
                yield
                P.tt("pool", H[d][1][:, c0:c0 + n], tm[2][:, 0:n], tm[3][:, 0:n], ALU.add, [tm[2], tm[3]], [H[d][1]])
                yield

        for gp in range(NGP):
            for d in range(2):
                P.dma("sp" if d == 0 else "pool", uf[d][:], srcs[d][gp * 32:(gp + 1) * 32, :], reads=[srcs[d]], writes=[uf[d]])
                P.act(ub[d][:], uf[d][:], AF.Copy, [uf[d]], [ub[d]])
            _interleave([dir_gen(0, gp), dir_gen(1, gp)])
            for nb in range(9):
                c0 = nb * 512
                n = min(512, NTOK - c0)
                P.mm(pf[:, 0:n], pr["CTr"][0][:, gp, :], H[0][0][:, c0:c0 + n], True, False, [pr["CTr"][0], H[0][0]], [pf])
                P.mm(pf[:, 0:n], pr["CTi"][0][:, gp, :], H[0][1][:, c0:c0 + n], False, True, [pr["CTi"][0], H[0][1]], [pf])
                for sbk in range(n // 128):
                    tau = nb * 4 + sbk
                    j = RTINV[tau]
                    P.mm(pb[:, 0:32], H[1][0][:, j * 128:(j + 1) * 128], pr["CTr"][1][:, gp, :], True, False, [pr["CTr"][1], H[1][0]], [pb])
                    P.mm(pb[:, 0:32], H[1][1][:, j * 128:(j + 1) * 128], pr["CTi"][1][:, gp, :], False, True, [pr["CTi"][1], H[1][1]], [pb])
                    P.act(ybt[:], pb[:, 0:32], AF.Copy, [pb], [ybt])
                    P.mm(pj[:, sbk * 128:(sbk + 1) * 128], ybt[:], J, True, True, [ybt, self.C], [pj])
                P.act(ybs[:, 0:n], pj[:, 0:n], AF.Copy, [pj], [ybs])
                P.tt("dve", yy[:, 0:n], pf[:, 0:n], ybs[:, 0:n], ALU.add, [pf, ybs], [yy])
                P.stt("dve", yy[:, 0:n], uf[0][:, c0:c0 + n], pr["dsk"][:, gp:gp + 1], yy[:, 0:n], ALU.mult, ALU.add, [uf[0], pr["dsk"], yy], [yy])
                P.act(ww[:, 0:n], yy[:, 0:n], AF.Square, [yy], [ww])
                P.ts("dve", ww[:, 0:n], ww[:, 0:n], 0.044715, 1.0, ALU.mult, ALU.add, [ww], [ww])
                P.tt("dve", ww[:, 0:n], ww[:, 0:n], yy[:, 0:n], ALU.mult, [ww, yy], [ww])
                P.act(ww[:, 0:n], ww[:, 0:n], AF.Sigmoid, [ww], [ww], scale=1.5957691216)
                P.tt("dve", GTt[:, c0:c0 + n], ww[:, 0:n], yy[:, 0:n], ALU.mult, [ww, yy], [GTt])
            P.dma("sp", self.GT[gp * 32:(gp + 1) * 32, :], GTt[:], reads=[GTt], writes=[self.GT])
            P.coll("AllGather", ALU.bypass, RG, self.GT[gp * 32:(gp + 1) * 32, :], self.GTF[gp], [self.GT], [self.GTF])


Net.stage_s5 = _stage_s5


def dn_tile_rows(dram_t, i, ncols_slice=None):
    if i < 2:
        return [(slice(0, 128), dram_t[i * 128:(i + 1) * 128, :])]
    c0 = 2 * (i - 2)
    v = dram_t[NCTX:NTOK, :].rearrange("(r c) n -> c r n", c=64)
    return [(slice(0, 64), v[c0]), (slice(64, 128), v[c0 + 1])]


def _stage_dn_prep(self, l):
    P = self.P
    ident = self.cst("ident"); ones = self.cst("ones")
    with P.scope():
        pt = P.ps("dpt", [128, 512]); pn = [P.ps("dpn", [128, 512]) for _ in range(2)]
        NCH = 3 * UC // 128
        HC = UC // 128
        cwl = P.sb("cwl", [5, 3 * UC])
        P.dma("sp", cwl[:], self.conv_w[l], reads=[self.conv_w], writes=[cwl])
        cwT = P.sb("cwT", [128, NCH, 5])
        for c in range(NCH):
            P.mm(pt[:, c * 8:c * 8 + 5], cwl[0:5, c * 128:(c + 1) * 128], ident[0:5, 0:5], True, True, [cwl, self.C], [pt])
        P.act(cwT[:], pt[:, 0:8 * NCH].rearrange("p (c j) -> p c j", j=8)[:, :, 0:5], AF.Copy, [pt], [cwT])
        raw = P.sb("draw", [128, NTOK]); pd = P.sb("dpd", [128, NTOK + 8]); acc = P.sb("dacc", [128, NTOK]); cs = P.sb("dcs", [128, NTOK])
        sq = P.sb("dsq", [128, 512]); rs = P.sb("drs", [128, 512]); tk = [P.sb("dtk", [128, 512]) for _ in range(2)]
        P.op("pool", lambda e: e.memset(pd[:], 0.0), writes=[pd])
        for c in range(NCH):
            kind = c // HC
            h = c % HC
            P.dma("sp", raw[:], self.PT[UC + c * 128:UC + (c + 1) * 128, :], reads=[self.PT], writes=[raw])
            P.act(pd[:, 2:258], raw[:, 0:256], AF.Copy, [raw], [pd])
            P.op("pool", lambda e: e.tensor_copy(out=pd[:, 262:262 + NLAT].rearrange("p (c r) -> p c r", r=64),
                                                 in_=raw[:, 256:NTOK].rearrange("p (r c) -> p c r", c=64)), reads=[raw], writes=[pd])
            for base, o0, n in ((0, 0, 256), (260, 256, NLAT)):
                P.ts("dve", acc[:, o0:o0 + n], pd[:, base:base + n], cwT[:, c, 0:1], None, ALU.mult, None, [pd, cwT], [acc])
                for j in range(1, 5):
                    P.stt("dve", acc[:, o0:o0 + n], pd[:, base + j:base + j + n], cwT[:, c, j:j + 1], acc[:, o0:o0 + n],
                          ALU.mult, ALU.add, [pd, cwT, acc], [acc])
            P.act(cs[:], acc[:], AF.Silu, [acc], [cs])
            if kind < 2:
                for nb in range(9):
                    c0 = nb * 512; n = min(512, NTOK - c0)
                    p_ = pn[nb % 2]
                    P.act(sq[:, 0:n], cs[:, c0:c0 + n], AF.Square, [cs], [sq])
                    P.mm(p_[:, 0:n], ones, sq[:, 0:n], True, True, [self.C, sq], [p_])
                    P.act(rs[:, 0:n], p_[:, 0:n], AF.Sqrt, [p_], [rs], bias=1e-6)
                    P.op("dve", lambda e, n=n: e.reciprocal(out=rs[:, 0:n], in_=rs[:, 0:n]), reads=[rs], writes=[rs])
                    if kind == 0:
                        P.stt("dve", cs[:, c0:c0 + n], cs[:, c0:c0 + n], 128.0 ** -0.5, rs[:, 0:n], ALU.mult, ALU.mult, [cs, rs], [cs])
                    else:
                        P.tt("dve", cs[:, c0:c0 + n], cs[:, c0:c0 + n], rs[:, 0:n], ALU.mult, [cs, rs], [cs])
                P.dma("sp", (self.QT if kind == 0 else self.KT)[h * 128:(h + 1) * 128, :], cs[:], reads=[cs],
                      writes=[self.QT if kind == 0 else self.KT])
            if kind >= 1:
                dst = self.KTOK if kind == 1 else self.VTOK
                for i4 in range(0, NT, 4):
                    nn = min(4, NT - i4)
                    for ii in range(nn):
                        i = i4 + ii
                        P.mm(pt[:, ii * 128:(ii + 1) * 128], cs[:, i * 128:(i + 1) * 128], ident, True, True, [cs, self.C], [pt])
                    t_ = tk[(i4 // 4) % 2]
                    P.act(t_[:, 0:nn * 128], pt[:, 0:nn * 128], AF.Copy, [pt], [t_])
                    for ii in range(nn):
                        i = i4 + ii
                        P.dma("pool", dst[i * 128:(i + 1) * 128, h * 128:(h + 1) * 128], t_[:, ii * 128:(ii + 1) * 128], reads=[t_], writes=[dst])
        dtb = P.sb("ddtb", [128, 2 * NH]); nea = P.sb("dnea", [128, 2 * NH])
        self.bcast_load("sp", dtb, self.dt_bias[l:l + 1, :], self.dt_bias)
        self.bcast_load("sp", nea, self.a_log[l:l + 1, :], self.a_log)
        P.act(nea[:], nea[:], AF.Exp, [nea], [nea])
        P.ts("dve", nea[:], nea[:], -1.0, None, ALU.mult, None, [nea], [nea])
        abt = [P.sb("dabt", [128, ABC]) for _ in range(2)]; gbt = [P.sb("dgbt", [128, ABC]) for _ in range(2)]
        for i in range(NT):
            a_ = abt[i % 2]; g_ = gbt[i % 2]
            for ps_, ap in dn_tile_rows(self.AB, i):
                P.dma("sp", a_[ps_, :], ap, reads=[self.AB], writes=[a_])
            av = a_[:, :].rearrange("p (d k h) -> p d k h", d=2, k=2)
            G2 = 2 * NH
            gv = g_[:, 0:G2].rearrange("p (d h) -> p d h", d=2)
            P.tt("dve", gv, av[:, :, 0, :], dtb[:, :].rearrange("p (d h) -> p d h", d=2), ALU.add, [a_, dtb], [g_])
            P.act(g_[:, 0:G2], g_[:, 0:G2], AF.Exp, [g_], [g_])
            P.act(g_[:, 0:G2], g_[:, 0:G2], AF.Ln, [g_], [g_], bias=1.0)
            P.tt("dve", g_[:, 0:G2], g_[:, 0:G2], nea[:], ALU.mult, [g_, nea], [g_])
            P.act(g_[:, G2:2 * G2].rearrange("p (d h) -> p d h", d=2), av[:, :, 1, :], AF.Sigmoid, [a_], [g_])
            P.dma("pool", self.GB[i * 128:(i + 1) * 128, :], g_[:], reads=[g_], writes=[self.GB])


def _interleave(gens):
    gens = list(gens)
    while gens:
        for g in list(gens):
            try:
                next(g)
            except StopIteration:
                gens.remove(g)


def _stage_dn_main(self, l):
    P = self.P
    ident = self.cst("ident")
    v3 = lambda t: t[:, 0:NH * 128].rearrange("p (h n) -> p h n", h=NH)
    bc = lambda ap, n=128: ap.unsqueeze(2).to_broadcast([128, NH, n])
    with P.scope():
        B = []
        for d in range(2):
            b = {}
            for nm in ("pA", "pB", "pC", "pD"):
                b[nm] = P.ps("d" + nm, [128, 512])
            for nm in ("S", "qT", "kT", "ktok", "vtok", "Gall", "Dm", "QKm", "WT", "kdec", "vnew", "o1", "O"):
                b[nm] = P.sb("d" + nm, [128, NH, 128])
            b["gb"] = P.sb("dgb", [128, 4 * NH]); b["sm"] = P.sb("dsm", [128, 6, NH])
            b["MT"] = [P.sb("dMT", [128, NH, 128]) for _ in range(2)]; b["MA"] = [P.sb("dMA", [128, NH, 128]) for _ in range(2)]
            b["r"] = P.sb("dr", [128, NH, 256])
            b["OD"] = self.ODt[d]
            B.append(b)

        def tile_gen(d, i, b):
            pA, pB, pC, pD = b["pA"], b["pB"], b["pC"], b["pD"]
            S, qT, kT, ktok, vtok, Gall, Dm, QKm = b["S"], b["qT"], b["kT"], b["ktok"], b["vtok"], b["Gall"], b["Dm"], b["QKm"]
            WT, kdec, vnew, o1, O, gb, sm, MT, MA, r = b["WT"], b["kdec"], b["vnew"], b["o1"], b["O"], b["gb"], b["sm"], b["MT"], b["MA"], b["r"]
            tri = self.cst("tri_f" if d == 0 else "tri_b"); ms = self.cst("ms_f" if d == 0 else "ms_b"); mi = self.cst("mi_f" if d == 0 else "mi_b")
            halves = (slice(0, 64), slice(64, 128)) if d == 0 else (slice(64, 128), slice(0, 64))
            q0, q1 = ("sp", "pool") if d == 0 else ("pool", "sp")
            cols = slice(i * 128, (i + 1) * 128)
            P.dma(q0, qT[:], self.QT[:, cols].rearrange("(h p) n -> p h n", p=128), reads=[self.QT], writes=[qT])
            P.dma(q1, kT[:], self.KT[:, cols].rearrange("(h p) n -> p h n", p=128), reads=[self.KT], writes=[kT])
            P.dma(q0, ktok[:].rearrange("p h n -> p (h n)"), self.KTOK[cols, :], reads=[self.KTOK], writes=[ktok])
            P.dma(q1, vtok[:].rearrange("p h n -> p (h n)"), self.VTOK[cols, :], reads=[self.VTOK], writes=[vtok])
            P.dma(q0, gb[:], self.GB[cols, :], reads=[self.GB], writes=[gb])
            yield
            g = gb[:, d * NH:d * NH + NH]; beta = gb[:, 2 * NH + d * NH:2 * NH + d * NH + NH]
            P.mm(pA[:, 0:NH], tri, g, True, True, [self.C, gb], [pA])
            P.mm(pA[:, NH:2 * NH], self.cst("blk_d"), g, True, True, [self.C, gb], [pA])
            P.mm(pA[:, 2 * NH:3 * NH], self.cst("blk_a"), g, True, True, [self.C, gb], [pA])
            P.mm(pA[:, 3 * NH:4 * NH], self.cst("blk_b"), g, True, True, [self.C, gb], [pA])
            P.op("dve", lambda e: e.tensor_copy(out=Gall[:], in_=bc(g)), reads=[gb], writes=[Gall])
            yield
            P.act(sm[:, 0:4, :].rearrange("p a h -> p (a h)"), pA[:, 0:4 * NH], AF.Copy, [pA], [sm])
            for h in range(NH):
                P.mm(pB[:, h * 128:(h + 1) * 128], Gall[:, h, :], tri, True, True, [Gall, self.C], [pB])
            for h in range(NH):
                P.mm(pC[:, h * 128:(h + 1) * 128], kT[:, h, :], kT[:, h, :], True, True, [kT], [pC])
                P.mm(pD[:, h * 128:(h + 1) * 128], kT[:, h, :], qT[:, h, :], True, True, [kT, qT], [pD])
            yield
            P.act(sm[:, 4, :], sm[:, 0, :], AF.Exp, [sm], [sm])
            P.tt("dve", sm[:, 5, :], sm[:, 1, :], sm[:, 0, :], ALU.subtract, [sm], [sm])
            yield
            P.act(sm[:, 5, :], sm[:, 5, :], AF.Exp, [sm], [sm])
            P.act(sm[:, 2:4, :], sm[:, 2:4, :], AF.Exp, [sm], [sm])
            gc = sm[:, 0, :]; egc = sm[:, 4, :]; ekd = sm[:, 5, :]
            P.tt("dve", Dm[:], v3(pB), bc(gc), ALU.subtract, [pB, sm], [Dm])
            yield
            P.ts("dve", Dm[:], Dm[:], 0.0, None, ALU.min, None, [Dm], [Dm])
            yield
            P.act(Dm[:], Dm[:], AF.Exp, [Dm], [Dm])
            yield
            msb = ms.unsqueeze(1).to_broadcast([128, NH, 128]); mib = mi.unsqueeze(1).to_broadcast([128, NH, 128])
            P.tt("dve", MT[0][:], v3(pC), Dm[:], ALU.mult, [pC, Dm], [MT[0]])
            P.tt("dve", QKm[:], v3(pD), Dm[:], ALU.mult, [pD, Dm], [QKm])
            yield
            P.tt("pool", MT[0][:], MT[0][:], msb, ALU.mult, [MT[0], self.C], [MT[0]])
            P.tt("pool", QKm[:], QKm[:], mib, ALU.mult, [QKm, self.C], [QKm])
            P.op("pool", lambda e: e.tensor_copy(out=r[:, :, 0:128], in_=vtok[:]), reads=[vtok], writes=[r])
            yield
            P.tt("dve", MT[0][:], MT[0][:], bc(beta), ALU.mult, [MT[0], gb], [MT[0]])
            P.tt("dve", r[:, :, 128:256], ktok[:], bc(egc), ALU.mult, [ktok, sm], [r])
            P.tt("pool", kdec[:], ktok[:], bc(ekd), ALU.mult, [ktok, sm], [kdec])
            yield
            for h in range(NH):
                P.mm(pC[:, h * 128:(h + 1) * 128], MT[0][:, h, :], ident, True, True, [MT[0], self.C], [pC])
            yield
            P.act(MA[0][:], v3(pC), AF.Copy, [pC], [MA[0]])
            yield
            cur = 0
            for k in range(6):
                mt = MT[cur]; ma = MA[cur]
                for h in range(NH):
                    P.mm(pA[:, h * 256:(h + 1) * 256], mt[:, h, :], r[:, h, :], True, True, [mt, r], [pA])
                if k < 5:
                    nx = 1 - cur
                    for h in range(NH):
                        P.mm(pC[:, h * 128:(h + 1) * 128], ma[:, h, :], mt[:, h, :], True, True, [ma, mt], [pC])
                        P.mm(pD[:, h * 128:(h + 1) * 128], mt[:, h, :], ma[:, h, :], True, True, [ma, mt], [pD])
                yield
                P.tt("dve", r[:], r[:], pA[:, 0:NH * 256].rearrange("p (h n) -> p h n", h=NH),
                     ALU.subtract if k == 0 else ALU.add, [r, pA], [r])
                if k < 5:
                    P.act(MT[nx][:], v3(pC), AF.Copy, [pC], [MT[nx]])
                    P.act(MA[nx][:], v3(pD), AF.Copy, [pD], [MA[nx]])
                    cur = nx
                yield
            P.tt("dve", r[:], r[:], bc(beta, 256), ALU.mult, [r, gb], [r])
            yield
            for h in range(NH):
                P.mm(pC[:, h * 128:(h + 1) * 128], r[:, h, 128:256], ident, True, True, [r, self.C], [pC])
            yield
            P.act(WT[:], v3(pC), AF.Copy, [pC], [WT])
            yield
            for hi, rows in enumerate(halves):
                egl = sm[:, 2 if rows.start == 0 else 3, :]
                for h in range(NH):
                    P.mm(pA[rows, h * 128:(h + 1) * 128], WT[:, h, rows], S[:, h, :], True, True, [WT, S], [pA])
                    P.mm(pB[rows, h * 128:(h + 1) * 128], qT[:, h, rows], S[:, h, :], True, True, [qT, S], [pB])
                yield
                P.tt("dve", vnew[rows], r[rows, :, 0:128], v3(pA)[rows], ALU.subtract, [r, pA], [vnew])
                P.tt("pool", S[:], S[:], bc(egl), ALU.mult, [S, sm], [S])
                yield
                P.tt("dve", o1[rows], v3(pB)[rows], egc[rows].unsqueeze(2).to_broadcast([64, NH, 128]), ALU.mult, [pB, sm], [o1])
                for h in range(NH):
                    P.mm(pC[rows, h * 128:(h + 1) * 128], QKm[rows, h, rows], vnew[rows, h, :], True, True, [QKm, vnew], [pC])
                    P.mm(pD[:, h * 128:(h + 1) * 128], kdec[rows, h, :], vnew[rows, h, :], True, True, [kdec, vnew], [pD])
                yield
                P.tt("dve", O[rows], o1[rows], v3(pC)[rows], ALU.add, [o1, pC], [O])
                P.tt("dve", S[:], S[:], v3(pD), ALU.add, [S, pD], [S])
                yield
            P.dma(q0, self.OD[d][cols, :], O[:].rearrange("p h n -> p (h n)"), reads=[O], writes=[b["OD"]])
            yield

        def chain(d):
            b = B[d]
            order = list(range(NT)) if d == 0 else [1, 0] + list(range(NT - 1, 1, -1))
            P.op("pool", lambda e: e.memset(b["S"][:], 0.0), writes=[b["S"]])
            for i in order:
                yield from tile_gen(d, i, b)

        _interleave([chain(0), chain(1)])


def _stage_dn_fin(self, l):
    P = self.P
    with P.scope():
        nw = P.sb("dnw", [128, 128])
        self.bcast_load("sp", nw, self.norm_w[l:l + 1, :], self.norm_w)
        oa = [P.sb("foa", [128, NH, 128]) for _ in range(2)]; ob = [P.sb("fob", [128, NH, 128]) for _ in range(2)]
        zt = [P.sb("fzt", [128, NH, 128]) for _ in range(2)]; sq = P.sb("fsq", [128, NH, 128]); ss = P.sb("fss", [128, NH])
        for i in range(NT):
            a = oa[i % 2]; b = ob[i % 2]; z = zt[i % 2]
            cols = slice(i * 128, (i + 1) * 128)
            P.dma("sp", a[:].rearrange("p h n -> p (h n)"), self.OD[0][cols, :], reads=[self.ODt[0]], writes=[a])
            P.dma("pool", b[:].rearrange("p h n -> p (h n)"), self.OD[1][cols, :], reads=[self.ODt[1]], writes=[b])
            for ps_, ap in dn_tile_rows(self.Z, i):
                P.dma("sp", z[ps_].rearrange("p h n -> p (h n)"), ap, reads=[self.Z], writes=[z])
            P.tt("dve", a[:], a[:], b[:], ALU.add, [a, b], [a])
            P.tt("pool", sq[:], a[:], a[:], ALU.mult, [a], [sq])
            P.op("dve", lambda e: e.tensor_reduce(out=ss[:], in_=sq[:], axis=AX.X, op=ALU.add), reads=[sq], writes=[ss])
            P.act(ss[:], ss[:], AF.Sqrt, [ss], [ss], scale=1.0 / 128.0, bias=1e-6)
            P.op("dve", lambda e: e.reciprocal(out=ss[:], in_=ss[:]), reads=[ss], writes=[ss])
            P.tt("dve", a[:], a[:], ss[:, :].unsqueeze(2).to_broadcast([128, NH, 128]), ALU.mult, [a, ss], [a])
            P.tt("pool", a[:], a[:], nw[:, :].unsqueeze(1).to_broadcast([128, NH, 128]), ALU.mult, [a, nw], [a])
            P.act(b[:], z[:], AF.Silu, [z], [b])
            P.tt("dve", a[:], a[:], b[:], ALU.mult, [a, b], [a])
            for ps_, ap in dn_tile_rows(self.DNO, i):
                P.dma("pool", ap, a[ps_].rearrange("p h n -> p (h n)"), reads=[a], writes=[self.DNO])
        for c in range(5):
            rc = 1024 if c < 4 else 256
            P.coll("AllGather", ALU.bypass, RG, self.DNO[c * 1024:c * 1024 + rc, :], self.DNOF[R * c * 1024:R * c * 1024 + R * rc, :],
                   [self.DNO], [self.DNOF])


def _stage_dn(self, l):
    _stage_dn_prep(self, l)
    _stage_dn_main(self, l)
    _stage_dn_fin(self, l)


Net.stage_dn = _stage_dn


def _ln_tile(self, t, lng, lnb, tmp, st):
    P = self.P
    P.op("dve", lambda e: e.tensor_reduce(out=st[:, 0:1], in_=t[:], axis=AX.X, op=ALU.add), reads=[t], writes=[st])
    P.ts("dve", st[:, 1:2], st[:, 0:1], -1.0 / D, None, ALU.mult, None, [st], [st])
    P.ts("dve", t[:], t[:], st[:, 1:2], None, ALU.add, None, [t, st], [t])
    P.act(tmp[:], t[:], AF.Square, [t], [tmp])
    P.op("dve", lambda e: e.tensor_reduce(out=st[:, 2:3], in_=tmp[:], axis=AX.X, op=ALU.add), reads=[tmp], writes=[st])
    P.act(st[:, 3:4], st[:, 2:3], AF.Sqrt, [st], [st], scale=1.0 / D, bias=1e-5)
    P.op("dve", lambda e: e.reciprocal(out=st[:, 3:4], in_=st[:, 3:4]), reads=[st], writes=[st])
    P.stt("dve", t[:], t[:], st[:, 3:4], lng[:], ALU.mult, ALU.mult, [t, st, lng], [t])
    P.tt("pool", t[:], t[:], lnb[:], ALU.add, [t, lnb], [t])


def _stage_glu(self, l):
    P = self.P
    ident = self.cst("ident")
    with P.scope():
        ps = [P.ps("gps", [128, 512]) for _ in range(3)]
        gw = P.sb("ggw", [128, 8, UC], BF16); stg = P.sb("gstg", [128, NTOK])
        for kc in range(8):
            P.dma("sp", stg[:, 0:UC], self.glu_w[l][kc * 128:(kc + 1) * 128, :], reads=[self.glu_w], writes=[stg])
            P.act(gw[:, kc, :], stg[:, 0:UC], AF.Copy, [stg], [gw])
        NOC = UC // 128
        gbl = P.sb("ggbl", [NOC, 128]); glub = P.sb("gglub", [128, NOC])
        P.dma("sp", gbl[:], self.glu_b[l:l + 1, :].rearrange("o (k p) -> (o k) p", p=128), reads=[self.glu_b], writes=[gbl])
        P.mm(ps[0][:, 0:NOC], gbl[:], ident[0:NOC, 0:NOC], True, True, [gbl, self.C], [ps[0]])
        P.act(glub[:], ps[0][:, 0:NOC], AF.Copy, [ps[0]], [glub])
        Gb = P.sb("gGb", [128, 8, NTOK], BF16)
        for kc in range(8):
            rr_ = kc // (UC // 128); lb = kc % (UC // 128)
            for j in range(4):
                P.dma("sp" if j % 2 == 0 else "pool", stg[32 * j:32 * (j + 1), :], self.GTF[lb * 4 + j][rr_ * 32:(rr_ + 1) * 32, :],
                      reads=[self.GTF], writes=[stg])
            P.act(Gb[:, kc, :], stg[:], AF.Copy, [stg], [Gb])
        ob = [P.sb("gob", [128, NTOK], BF16) for _ in range(2)]; sg = [P.sb("gsg", [128, 512]) for _ in range(2)]
        for oc in range(NOC):
            o = ob[oc % 2]
            P.dma("sp", stg[:], self.GT[oc * 128:(oc + 1) * 128, :], reads=[self.GT], writes=[stg])
            for nb in range(9):
                c0 = nb * 512; n = min(512, NTOK - c0)
                p_ = ps[nb % 3]; s_ = sg[nb % 2]
                for kc in range(8):
                    P.mm(p_[:, 0:n], gw[:, kc, oc * 128:(oc + 1) * 128], Gb[:, kc, c0:c0 + n], kc == 0, kc == 7, [gw, Gb], [p_])
                P.act(s_[:, 0:n], p_[:, 0:n], AF.Sigmoid, [p_, glub], [s_], bias=glub[:, oc:oc + 1])
                P.tt("dve", o[:, c0:c0 + n], s_[:, 0:n], stg[:, c0:c0 + n], ALU.mult, [s_, stg], [o])
            P.dma("pool", self.S5T[oc * 128:(oc + 1) * 128, :], o[:], reads=[o], writes=[self.S5T])
            for hh in range(2):
                P.coll("AllGather", ALU.bypass, RG, self.S5T[oc * 128 + hh * 64:oc * 128 + (hh + 1) * 64, :], self.S5TF[oc * 2 + hh],
                       [self.S5T], [self.S5TF])


def _stage_outproj(self, l):
    P = self.P
    with P.scope():
        wo = P.sb("owo", [128, 16, DC], BF16); stg = P.sb("ostg", [128, DC])
        for kc in range(16):
            P.dma("sp" if kc % 2 == 0 else "pool", stg[:], self.w_out[l][kc * 128:(kc + 1) * 128, :], reads=[self.w_out], writes=[stg])
            P.act(wo[:, kc, :], stg[:], AF.Copy, [stg], [wo])
        s5t = [P.sb("os5t", [128, 8, 128], BF16) for _ in range(2)]
        dnt = [P.sb("odnt", [128, 1024]) for _ in range(2)]; dnb = P.sb("odnb", [128, 1024], BF16); dnT = P.sb("odnT", [128, 8, 128], BF16)
        pt = P.ps("opt", [128, 1024]); pa = [P.ps("opa", [128, 512]) for _ in range(3)]
        t = [P.sb("ot_", [128, DC]) for _ in range(2)]
        for i in range(NT):
            cols = slice(i * 128, (i + 1) * 128)
            s5 = s5t[i % 2]; dn = dnt[i % 2]; t_ = t[i % 2]
            for kc in range(8):
                rr_ = kc // (UC // 128); lb = kc % (UC // 128)
                for hh in range(2):
                    P.dma("sp" if hh == 0 else "pool", s5[hh * 64:(hh + 1) * 64, kc, :], self.S5TF[lb * 2 + hh][rr_ * 64:(rr_ + 1) * 64, cols],
                          reads=[self.S5TF], writes=[s5])
            c_ = i // 8; ii = i % 8; rc = 1024 if c_ < 4 else 256
            dnf = self.DNOF[R * c_ * 1024:R * c_ * 1024 + R * rc, :].rearrange("(r t) c -> t r c", r=R)
            P.dma("pool", dn[:, :].rearrange("p (r c) -> p r c", r=R), dnf[ii * 128:(ii + 1) * 128], reads=[self.DNOF], writes=[dn])
            P.act(dnb[:], dn[:], AF.Copy, [dn], [dnb])
            for kc in range(8):
                P.mm(pt[:, kc * 128:(kc + 1) * 128], dnb[:, kc * 128:(kc + 1) * 128], self.Cb[:, 0:128], True, True, [dnb, self.Cb], [pt])
            P.act(dnT[:].rearrange("p k n -> p (k n)"), pt[:], AF.Copy, [pt], [dnT])
            for cb in range(DC // 512):
                p_ = pa[(i + cb) % 3]
                cs_ = slice(cb * 512, (cb + 1) * 512)
                for kc in range(8):
                    P.mm(p_[:], s5[:, kc, :], wo[:, kc, cs_], kc == 0, False, [s5, wo], [p_])
                for kc in range(8):
                    P.mm(p_[:], dnT[:, kc, :], wo[:, 8 + kc, cs_], False, kc == 7, [dnT, wo], [p_])
                P.act(t_[:, cs_], p_[:], AF.Copy, [p_], [t_])
            P.dma("sp", self.MIXL[cols, :], t_[:], reads=[t_], writes=[self.MIXL])
            if i % 4 == 3 or i == NT - 1:
                c_ = i // 4; rc = 512 if c_ < 8 else 256
                P.coll("AllGather", ALU.bypass, RG, self.MIXL[c_ * 512:c_ * 512 + rc, :], self.MIXF[R * c_ * 512:R * c_ * 512 + R * rc, :],
                       [self.MIXL], [self.MIXF])
    _stage_resln(self, l, 1)


def _stage_resln(self, l, which):
    P = self.P
    with P.scope():
        gseg = 2 if which == 1 else 5
        g_ = {}
        for who, row in (("lat", 0), ("ctx", 1)):
            g_[who] = P.sb("lg", [128, D])
            self.mod_load("sp", g_[who], l, row, gseg)
        lng = P.sb("llng", [128, D]); lnb = P.sb("llnb", [128, D])
        self.bcast_load("sp", lng, (self.ln1_g if which == 1 else self.ln2_g)[l:l + 1, :], self.ln1_g if which == 1 else self.ln2_g)
        self.bcast_load("pool", lnb, (self.ln1_b if which == 1 else self.ln2_b)[l:l + 1, :], self.ln1_b if which == 1 else self.ln2_b)
        tt_ = [P.sb("lt", [128, D]) for _ in range(2)]; xx = [P.sb("lx", [128, D]) for _ in range(2)]
        tmp = P.sb("ltmp", [128, D]); st = P.sb("lst", [128, 4])
        def ar(j):
            P.coll("AllReduce", ALU.add, RG, self.FFP[j * 128:(j + 1) * 128, :], self.FF[j * 128:(j + 1) * 128, :], [self.FFP], [self.FFt[j]])
        if which == 2:
            ar(0); ar(1)
        for i in range(NT):
            cols = slice(i * 128, (i + 1) * 128)
            t = tt_[i % 2]; x = xx[i % 2]
            if which == 2 and i + 2 < NT:
                ar(i + 2)
            if which == 1:
                c_ = i // 4; ii = i % 4; rc = 512 if c_ < 8 else 256
                mixf = self.MIXF[R * c_ * 512:R * c_ * 512 + R * rc, :].rearrange("(r t) c -> t r c", r=R)
                P.dma("sp", t[:, :].rearrange("p (r c) -> p r c", r=R), mixf[ii * 128:(ii + 1) * 128], reads=[self.MIXF], writes=[t])
            else:
                P.dma("sp", t[:], self.FF[cols, :], reads=[self.FFt[i]], writes=[t])
            P.dma("pool", x[:], self.X[cols, :], reads=[self.X], writes=[x])
            P.tt("dve", t[:], t[:], g_["ctx" if i < 2 else "lat"][:], ALU.mult, [t, g_["ctx"], g_["lat"]], [t])
            P.stt("dve", t[:], x[:], ALPHA, t[:], ALU.mult, ALU.add, [x, t], [t])
            _ln_tile(self, t, lng, lnb, tmp, st)
            P.dma("sp", self.X[cols, :], t[:], reads=[t], writes=[self.X])
    self.dump_dram("X%d" % which, lambda r0, n: self.X[r0:r0 + n, :], NTOK, D, self.X)


Net.stage_glu = _stage_glu
Net.stage_outproj = _stage_outproj


def _stage_moe(self, l):
    P = self.P
    ident = self.cst("ident"); lt = self.cst("lt")
    NE = 16
    with P.scope():
        modt = {}
        for who, row in (("lat", 0), ("ctx", 1)):
            scp = P.sb("mscp", [128, D]); sh = P.sb("msh", [128, D])
            self.mod_load("sp", scp, l, row, 4)
            self.mod_load("pool", sh, l, row, 3)
            modt[who] = (scp, sh)
        rw = P.sb("mrw", [128, 16, 16])
        P.dma("sp", rw[:], self.router_w[l].rearrange("(k p) e -> p k e", p=128), reads=[self.router_w], writes=[rw])
        RB_ = []
        for par in range(2):
            RB_.append({"x": P.sb("mx", [128, D]), "hb": P.sb("mhb", [128, D], BF16), "hT": P.sb("mhT", [128, 16, 128]),
                        "pt": [P.ps("mpt", [128, 512]) for _ in range(2)], "pl": P.ps("mpl", [128, 512]),
                        "af": P.sb("maf", [128, 16]), "st": P.sb("mst", [128, 4]), "aft": P.sb("maft", [16, 128])})

        def rtile(i, b):
            cols = slice(i * 128, (i + 1) * 128)
            x_, h_, hT, pt, pl, af, st, aft = b["x"], b["hb"], b["hT"], b["pt"], b["pl"], b["af"], b["st"], b["aft"]
            scp, sh = modt["ctx" if i < 2 else "lat"]
            q0, q1 = ("sp", "pool") if i % 2 == 0 else ("pool", "sp")
            P.dma(q0, x_[:], self.X[cols, :], reads=[self.X], writes=[x_])
            yield
            P.tt("dve", x_[:], x_[:], scp[:], ALU.mult, [x_, scp], [x_])
            yield
            P.tt("dve", x_[:], x_[:], sh[:], ALU.add, [x_, sh], [x_])
            yield
            P.act(h_[:], x_[:], AF.Copy, [x_], [h_])
            for kg in range(4):
                p_ = pt[kg % 2]
                for kk in range(4):
                    kc = kg * 4 + kk
                    P.mm(p_[:, kk * 128:(kk + 1) * 128], x_[:, kc * 128:(kc + 1) * 128], ident, True, True, [x_, self.C], [p_])
                yield
                P.act(hT[:, kg * 4:(kg + 1) * 4, :].rearrange("p k n -> p (k n)"), p_[:], AF.Copy, [p_], [hT])
            P.dma(q1, self.H2[cols, :], h_[:], reads=[h_], writes=[self.H2t[i]])
            yield
            for kc in range(16):
                P.mm(pl[:, 0:16], hT[:, kc, :], rw[:, kc, :], kc == 0, kc == 15, [hT, rw], [pl])
            yield
            P.op("dve", lambda e: e.tensor_reduce(out=st[:, 0:1], in_=pl[:, 0:16], axis=AX.X, op=ALU.max), reads=[pl], writes=[st])
            yield
            P.ts("dve", st[:, 1:2], st[:, 0:1], -1.0, None, ALU.mult, None, [st], [st])
            yield
            P.act(af[:], pl[:, 0:16], AF.Exp, [pl, st], [af], bias=st[:, 1:2])
            yield
            P.op("dve", lambda e: e.tensor_reduce(out=st[:, 2:3], in_=af[:], axis=AX.X, op=ALU.add), reads=[af], writes=[st])
            yield
            P.op("dve", lambda e: e.reciprocal(out=st[:, 3:4], in_=st[:, 2:3]), reads=[st], writes=[st])
            yield
            P.ts("dve", af[:], af[:], st[:, 3:4], None, ALU.mult, None, [af, st], [af])
            yield
            P.dma(q0, self.AFF[cols, :], af[:], reads=[af], writes=[self.AFFw[i % 2]])
            P.mm(pl[0:16, 128:256], af[:], ident, True, True, [af, self.C], [pl])
            yield
            P.act(aft[:], pl[0:16, 128:256], AF.Copy, [pl], [aft])
            yield
            P.dma(q1, self.AFFT[:, cols], aft[:], reads=[aft], writes=[self.AFFTw[i % 2]])
            yield

        def rchain(par):
            for i in range(par, NT, 2):
                yield from rtile(i, RB_[par])

        _interleave([rchain(0), rchain(1)])
        P.barrier()
    with P.scope():
        affall = P.sb("raff", [128, NT, 16]); rank = P.sb("rrank", [128, NT, 16])
        P.dma("sp", affall[:], self.AFF[:, :].rearrange("(t p) e -> p t e", p=128), reads=[self.AFF], writes=[affall])
        arow = [P.sb("rarow", [128, NLAT]) for _ in range(2)]; junk = P.sb("rjunk", [128, NLAT]); j2 = P.sb("rj2", [128, 128])
        c4 = [P.sb("rc4", [128, 4]) for _ in range(2)]
        pl = P.ps("rpl", [128, 512]); rkt = P.sb("rrkt", [16, 128])
        P.op("pool", lambda g: g.memset(rank[:], 0.0), writes=[rank])
        for tiles, col0 in (([0, 1], 0), (list(range(2, NT)), NCTX)):
            n = len(tiles) * 128
            for e in range(NEL):
                ar = arow[e % 2]
                P.dma("sp" if e % 2 == 0 else "pool", ar[:, 0:n], self.AFFT[e:e + 1, col0:col0 + n].partition_broadcast(128),
                      reads=[self.AFFT], writes=[ar])
                for tl, t in enumerate(tiles):
                    sc = affall[:, t, e:e + 1]
                    c = c4[tl % 2]
                    P.op("pool", lambda g, c=c: g.memset(c[:], 0.0), writes=[c])
                    b0 = tl * 128
                    if tl > 0:
                        P.ts("dve", junk[:, 0:b0], ar[:, 0:b0], sc, 0.0, ALU.is_ge, ALU.add, [ar, affall], [c], accum=c[:, 0:1])
                    if b0 + 128 < n:
                        P.ts("dve", junk[:, b0 + 128:n], ar[:, b0 + 128:n], sc, 0.0, ALU.is_gt, ALU.add, [ar, affall], [c], accum=c[:, 1:2])
                    P.ts("dve", junk[:, b0:b0 + 128], ar[:, b0:b0 + 128], sc, 0.0, ALU.is_gt, ALU.add, [ar, affall], [c], accum=c[:, 2:3])
                    P.stt("dve", j2[:], ar[:, b0:b0 + 128], sc, lt, ALU.is_equal, ALU.mult, [ar, affall, self.C], [j2])
                    P.op("dve", lambda g, c=c: g.tensor_reduce(out=c[:, 3:4], in_=j2[:], axis=AX.X, op=ALU.add), reads=[j2], writes=[c])
                    P.op("dve", lambda g, c=c, t=t, e=e: g.tensor_reduce(out=rank[:, t, e:e + 1], in_=c[:], axis=AX.X, op=ALU.add), reads=[c], writes=[rank])
        P.dma("sp", self.RANKD[:, :].rearrange("(t p) e -> p t e", p=128), rank[:], reads=[rank], writes=[self.RANKD])
        for t in range(NT):
            P.mm(pl[0:16, 0:128], rank[:, t, :], ident, True, True, [rank, self.C], [pl])
            P.act(rkt[:], pl[0:16, 0:128], AF.Copy, [pl], [rkt])
            P.dma("sp", self.RANKT[:, t * 128:(t + 1) * 128], rkt[:], reads=[rkt], writes=[self.RANKT])
    with P.scope():
        rk = P.sb("erk", [128, NT, 16])
        P.dma("sp", rk[:], self.RANKD[:, :].rearrange("(t p) e -> p t e", p=128), reads=[self.RANKD], writes=[rk])
        perm = P.sb("eperm", [128, 4, 512], BF16); permc = P.sb("epermc", [128, 2, 32], BF16)
        w1b = P.sb("ew1", [128, 16, 1024], BF16); w3b = P.sb("ew3", [128, 16, 1024], BF16); w2b = P.sb("ew2", [128, 8, D], BF16)
        stg = [P.sb("estg", [128, D]) for _ in range(2)]
        HsT = P.sb("eHsT", [128, 16, 544], BF16); hidT = P.sb("ehid", [128, 8, 544], BF16)
        h2t = [P.sb("eh2t", [128, 1024], BF16) for _ in range(3)]
        rrow = P.sb("errow", [128, NLAT]); arow = P.sb("earow", [128, NLAT]); g2_ = P.sb("eg2", [128, 2])
        gate = P.sb("egate", [128, 8]); sa = P.sb("esa", [128, 512]); yg = [P.sb("eyg", [128, D], BF16) for _ in range(2)]
        pg = [P.ps("epg", [128, 512]) for _ in range(8)]
        for e in range(NEL):
            ld = 0
            for src, dst, nk, w in ((self.w1, w1b, 16, 1024), (self.w3, w3b, 16, 1024), (self.w2, w2b, 8, D)):
                for kc in range(nk):
                    s_ = stg[ld % 2]
                    P.dma("sp" if ld % 2 == 0 else "pool", s_[:, 0:w], src[l][e][kc * 128:(kc + 1) * 128, :], reads=[src], writes=[s_])
                    if ld % 2 == 0:
                        P.act(dst[:, kc, :], s_[:, 0:w], AF.Copy, [s_], [dst])
                    else:
                        P.op("dve", lambda g, dst=dst, kc=kc, s_=s_, w=w: g.tensor_copy(out=dst[:, kc, :], in_=s_[:, 0:w]), reads=[s_], writes=[dst])
                    ld += 1
            for t in range(2):
                P.ts("dve", permc[:, t, :], self.iota[:, 0:32], rk[:, t, e:e + 1], None, ALU.is_equal, None, [self.iota, rk], [permc])
            P.dma("sp", rrow[:], self.RANKT[e:e + 1, NCTX:NTOK].partition_broadcast(128), reads=[self.RANKT], writes=[rrow])
            P.dma("pool", arow[:], self.AFFT[e:e + 1, NCTX:NTOK].partition_broadcast(128), reads=[self.AFFT], writes=[arow])
            for sbk in range(4):
                for hf in range(2):
                    hs = slice(hf * 2048, (hf + 1) * 2048)
                    P.stt("dve", stg[hf][:], rrow[:, hs], self.iotc[:, sbk:sbk + 1], arow[:, hs], ALU.is_equal, ALU.mult, [rrow, arow, self.iotc], [stg[hf]])
                    P.op("dve", lambda g, hf=hf: g.tensor_reduce(out=g2_[:, hf:hf + 1], in_=stg[hf][:], axis=AX.X, op=ALU.add), reads=[stg[hf]], writes=[g2_])
                P.tt("dve", gate[:, sbk:sbk + 1], g2_[:, 0:1], g2_[:, 1:2], ALU.add, [g2_], [gate])
            P.dma("sp", rrow[:, 0:NCTX], self.RANKT[e:e + 1, 0:NCTX].partition_broadcast(128), reads=[self.RANKT], writes=[rrow])
            P.dma("pool", arow[:, 0:NCTX], self.AFFT[e:e + 1, 0:NCTX].partition_broadcast(128), reads=[self.AFFT], writes=[arow])
            P.stt("dve", stg[0][:, 0:NCTX], rrow[:, 0:NCTX], self.iotc[:, 0:1], arow[:, 0:NCTX], ALU.is_equal, ALU.mult, [rrow, arow, self.iotc], [stg[0]])
            P.op("dve", lambda g: g.tensor_reduce(out=gate[:, 4:5], in_=stg[0][:, 0:NCTX], axis=AX.X, op=ALU.add), reads=[stg[0]], writes=[gate])
            for kg in range(2):
                for t in range(32):
                    h_ = h2t[t % 3]
                    P.dma("sp" if t % 2 == 0 else "pool", h_[:], self.H2[NCTX + t * 128:NCTX + (t + 1) * 128, kg * 1024:(kg + 1) * 1024],
                          reads=[self.H2], writes=[h_])
                    pm_ = perm[:, t % 4, :]
                    P.ts("dve" if t % 2 == 0 else "pool", pm_, self.iota[:, 0:512], rk[:, 2 + t, e:e + 1], None, ALU.is_equal, None,
                         [self.iota, rk], [perm])
                    for kk in range(8):
                        P.mm(pg[kk][:], h_[:, kk * 128:(kk + 1) * 128], pm_, t == 0, t == 31, [h_, perm], [pg[kk]])
                for kk in range(8):
                    P.act(HsT[:, kg * 8 + kk, 0:512], pg[kk][:], AF.Copy, [pg[kk]], [HsT])
                for t in range(2):
                    h_ = h2t[t % 3]
                    P.dma("sp", h_[:], self.H2[t * 128:(t + 1) * 128, kg * 1024:(kg + 1) * 1024], reads=[self.H2], writes=[h_])
                    for kk in range(8):
                        P.mm(pg[kk][:, 0:32], h_[:, kk * 128:(kk + 1) * 128], permc[:, t, :], t == 0, t == 1, [h_, permc], [pg[kk]])
                for kk in range(8):
                    P.act(HsT[:, kg * 8 + kk, 512:544], pg[kk][:, 0:32], AF.Copy, [pg[kk]], [HsT])
            q = 0
            for fc in range(8):
                for s0, n in ((0, 512), (512, 32)):
                    p1 = pg[(2 * q) % 8]; p3 = pg[(2 * q + 1) % 8]; q += 1
                    for kc in range(16):
                        P.mm(p1[:, 0:n], w1b[:, kc, fc * 128:(fc + 1) * 128], HsT[:, kc, s0:s0 + n], kc == 0, kc == 15, [w1b, HsT], [p1])
                    for kc in range(16):
                        P.mm(p3[:, 0:n], w3b[:, kc, fc * 128:(fc + 1) * 128], HsT[:, kc, s0:s0 + n], kc == 0, kc == 15, [w3b, HsT], [p3])
                    P.act(sa[:, 0:n], p1[:, 0:n], AF.Silu, [p1], [sa])
                    P.tt("dve", hidT[:, fc, s0:s0 + n], sa[:, 0:n], p3[:, 0:n], ALU.mult, [sa, p3], [hidT])
            q = 0
            for sbk in range(5):
                nr = 128 if sbk < 4 else 32
                s0 = sbk * 128
                y_ = yg[sbk % 2]
                for cb in range(4):
                    p_ = pg[q % 8]; q += 1
                    for fc in range(8):
                        P.mm(p_[0:nr, :], hidT[:, fc, s0:s0 + nr], w2b[:, fc, cb * 512:(cb + 1) * 512], fc == 0, fc == 7, [hidT, w2b], [p_])
                    P.ts("dve", y_[0:nr, cb * 512:(cb + 1) * 512], p_[0:nr, :], gate[0:nr, sbk:sbk + 1], None, ALU.mult, None, [p_, gate], [y_])
                P.dma("sp", self.YG[e][s0:s0 + nr, :], y_[0:nr, :], reads=[y_], writes=[self.YG])
    with P.scope():
        ygs = P.sb("cygs", [128, 4, NEL, 4, 512], BF16); ygc = P.sb("cygc", [32, 4, NEL, 512], BF16)
        rr = [P.sb("crr", [128, NEL, 128]) for _ in range(2)]; pT = [P.sb("cpT", [128, 4, NEL, 128], BF16) for _ in range(2)]
        pf = [P.ps("cpf", [128, 512]) for _ in range(4)]; fo = [P.sb("cfo", [128, D]) for _ in range(2)]
        for cb in range(4):
            cs_ = slice(cb * 512, (cb + 1) * 512)
            for e in range(NEL):
                P.dma("sp" if e % 2 == 0 else "pool", ygs[:, cb, e], self.YG[e][0:512, cs_].rearrange("(s p) n -> p s n", p=128), reads=[self.YG], writes=[ygs])
                P.dma("sp", ygc[:, cb, e, :], self.YG[e][512:544, cs_], reads=[self.YG], writes=[ygc])
        for t in range(NT):
            cols = slice(t * 128, (t + 1) * 128)
            r_ = rr[t % 2]; p_ = pT[t % 2]; o_ = fo[t % 2]
            P.dma("sp" if t % 2 == 0 else "pool", r_[:], self.RANKT[0:NEL, cols].partition_broadcast(128), reads=[self.RANKT], writes=[r_])
            if t >= 2:
                for sbk in range(4):
                    P.ts("dve" if sbk % 2 == 0 else "pool", p_[:, sbk], r_[:], self.iotc[:, sbk:sbk + 1], None, ALU.is_equal, None, [r_, self.iotc], [p_])
            else:
                P.ts("dve", p_[:, 0], r_[:], self.iotc[:, 0:1], None, ALU.is_equal, None, [r_, self.iotc], [p_])
            for cb in range(4):
                f_ = pf[cb]
                if t >= 2:
                    k = 0
                    for e in range(NEL):
                        for sbk in range(4):
                            P.mm(f_[:], p_[:, sbk, e, :], ygs[:, cb, e, sbk, :], k == 0, k == 4 * NEL - 1, [p_, ygs], [f_])
                            k += 1
                else:
                    for e in range(NEL):
                        P.mm(f_[:], p_[0:32, 0, e, :], ygc[0:32, cb, e, :], e == 0, e == NEL - 1, [p_, ygc], [f_])
                if cb % 2 == 0:
                    P.act(o_[:, cb * 512:(cb + 1) * 512], f_[:], AF.Copy, [f_], [o_])
                else:
                    P.op("dve", lambda g, o_=o_, f_=f_, cb=cb: g.tensor_copy(out=o_[:, cb * 512:(cb + 1) * 512], in_=f_[:]), reads=[f_], writes=[o_])
            P.dma("sp" if t % 2 == 0 else "pool", self.FFP[cols, :], o_[:], reads=[o_], writes=[self.FFP])
    _stage_resln(self, l, 2)


Net.stage_moe = _stage_moe
```

```python
import numpy as np
from contextlib import ExitStack, contextmanager
import concourse.bass as bass
import concourse.mybir as mybir
from concourse.bass_utils import run_bass_kernel_spmd

F32 = mybir.dt.float32
BF16 = mybir.dt.bfloat16
ALU = mybir.AluOpType
AF = mybir.ActivationFunctionType
AX = mybir.AxisListType

D = 2048
NTOK = 4352
NT = 34
NCTX = 256
NLAT = 4096
DEPTH = 4
ALPHA = (2 * DEPTH) ** 0.25
NIN = 5152
R = 4
UC = 1024 // R
NGP = 32 // R
NH = 8 // R
NEL = 16 // R
DC = D // R
ABC = 4 * NH
NINL = 5 * UC + ABC
RG = [[0, 1, 2, 3], [4, 5, 6, 7]]
CONST_NAMES = ["ident", "J", "ones", "tri_f", "tri_b", "ms_f", "ms_b", "mi_f", "mi_b", "blk_a", "blk_b", "blk_d", "lt"]


def make_consts():
    i = np.arange(128)
    s, j = np.meshgrid(i, i, indexing="ij")
    same = (s // 64) == (j // 64)
    c = {}
    c["ident"] = (s == j)
    c["J"] = (s + j == 127)
    c["ones"] = np.ones((128, 128), bool)
    c["tri_f"] = same & (s <= j)
    c["tri_b"] = same & (s >= j)
    c["ms_f"] = same & (s < j)
    c["ms_b"] = same & (s > j)
    c["mi_f"] = same & (s <= j)
    c["mi_b"] = same & (s >= j)
    c["blk_a"] = (s < 64)
    c["blk_b"] = (s >= 64)
    c["blk_d"] = same
    c["lt"] = (j < s)
    return np.concatenate([c[n].astype(np.float32) for n in CONST_NAMES], axis=1)


class T:
    __slots__ = ("h", "lw", "rd", "name")

    def __init__(self, h, name=""):
        self.h = h
        self.lw = None
        self.rd = []
        self.name = name

    def __getitem__(self, k):
        return self.h[k]


class Prog:
    NDMA = 8
    SAME_ENG = True

    def __init__(self, nc, es):
        self.nc = nc
        self.stack = [es]
        self.eng = {"pe": nc.tensor, "act": nc.scalar, "dve": nc.vector, "pool": nc.gpsimd, "sp": nc.sync}
        self.sem = {}
        self.cnt = {}
        for e in ("pe", "act", "dve", "pool"):
            self.sem[e] = es.enter_context(nc.semaphore("s_" + e))
            self.cnt[e] = 0
        self.dq = {}
        for q in ("sp", "act", "pool"):
            sems = []
            for i in range(self.NDMA):
                k = "d_%s_%d" % (q, i)
                self.sem[k] = es.enter_context(nc.semaphore(k))
                self.cnt[k] = 0
                sems.append(k)
            self.dq[q] = [sems, 0]
        self.known = {e: {} for e in self.eng}
        self.ninst = 0
        self.uid = 0

    @contextmanager
    def scope(self):
        es = ExitStack()
        self.stack.append(es)
        try:
            yield
        finally:
            self.barrier()
            self.stack.pop()
            es.close()

    def _nm(self, name):
        self.uid += 1
        return "%s_%d" % (name, self.uid)

    def sb(self, name, shape, dt=F32):
        return T(self.stack[-1].enter_context(self.nc.sbuf_tensor(self._nm(name), list(shape), dt)), name)

    def ps(self, name, shape, dt=F32):
        return T(self.stack[-1].enter_context(self.nc.psum_tensor(self._nm(name), list(shape), dt)), name)

    def dram(self, name, shape, dt=F32, kind="Internal"):
        return T(self.nc.dram_tensor(name, list(shape), dt, kind=kind), name)

    def _wait(self, e, dep):
        if dep is None:
            return
        k, v = dep
        if v <= 0 or self.known[e].get(k, 0) >= v:
            return
        self.eng[e].wait_ge(self.sem[k], v)
        self.known[e][k] = v

    def _skip(self, e, key):
        return key == e and (e == "pe" or not self.SAME_ENG)

    def _deps(self, e, reads, writes):
        for t in reads:
            if t.lw is not None and not self._skip(e, t.lw[0]):
                self._wait(e, t.lw)
        for t in writes:
            if t.lw is not None and not self._skip(e, t.lw[0]):
                self._wait(e, t.lw)
            for d in t.rd:
                if not self._skip(e, d[0]):
                    self._wait(e, d)

    def _mark(self, key, val, reads, writes):
        for t in reads:
            t.rd.append((key, val))
            if len(t.rd) > 32:
                m = {}
                for k, v in t.rd:
                    if m.get(k, 0) < v:
                        m[k] = v
                t.rd = list(m.items())
        for t in writes:
            t.lw = (key, val)
            t.rd = []

    def op(self, e, fn, reads=(), writes=()):
        self._deps(e, reads, writes)
        ins = fn(self.eng[e])
        self.cnt[e] += 1
        ins.then_inc(self.sem[e], 1)
        self._mark(e, self.cnt[e], reads, writes)
        self.ninst += 1
        return ins

    def dma(self, q, out, in_, reads=(), writes=(), **kw):
        sems, i = self.dq[q]
        k = sems[i % self.NDMA]
        self.dq[q][1] = i + 1
        self._wait(q, (k, self.cnt[k]))
        self._deps(q, reads, writes)
        ins = self.eng[q].dma_start(out=out, in_=in_, **kw)
        self.cnt[k] += 16
        ins.then_inc(self.sem[k], 16)
        self._mark(k, self.cnt[k], reads, writes)
        self.ninst += 1
        return ins

    def coll(self, kind, op, rg, in_ap, out_ap, reads, writes):
        q = "pool"
        k = "cc"
        if k not in self.sem:
            self.sem[k] = self.stack[0].enter_context(self.nc.semaphore("s_cc"))
            self.cnt[k] = 0
        self._wait(q, (k, self.cnt[k]))
        self._deps(q, reads, writes)
        ins = self.nc.gpsimd.collective_compute(kind, op, replica_groups=rg, ins=[in_ap], outs=[out_ap])
        self.cnt[k] += 1
        ins.then_inc(self.sem[k], 1)
        self._mark(k, self.cnt[k], reads, writes)
        self.ninst += 1
        return ins

    def barrier(self):
        for e in self.eng:
            for k in self.sem:
                self._wait(e, (k, self.cnt[k]))

    def mm(self, out_ap, lhsT_ap, rhs_ap, start, stop, reads, writes):
        return self.op("pe", lambda e: e.matmul(out_ap, lhsT=lhsT_ap, rhs=rhs_ap, start=start, stop=stop),
                       reads=reads, writes=writes)

    def act(self, out_ap, in_ap, func, reads, writes, **kw):
        return self.op("act", lambda e: e.activation(out=out_ap, in_=in_ap, func=func, **kw), reads=reads, writes=writes)

    def tt(self, e, out_ap, a_ap, b_ap, op, reads, writes):
        return self.op(e, lambda g: g.tensor_tensor(out=out_ap, in0=a_ap, in1=b_ap, op=op), reads=reads, writes=writes)

    def ts(self, e, out_ap, a_ap, s1, s2, op0, op1, reads, writes, accum=None):
        if op1 is None:
            return self.op(e, lambda g: g.tensor_scalar(out=out_ap, in0=a_ap, scalar1=s1, scalar2=None, op0=op0),
                           reads=reads, writes=writes)
        if accum is not None:
            return self.op(e, lambda g: g.tensor_scalar(out=out_ap, in0=a_ap, scalar1=s1, scalar2=s2, op0=op0, op1=op1,
                                                        accum_out=accum), reads=reads, writes=writes)
        return self.op(e, lambda g: g.tensor_scalar(out=out_ap, in0=a_ap, scalar1=s1, scalar2=s2, op0=op0, op1=op1),
                       reads=reads, writes=writes)

    def stt(self, e, out_ap, a_ap, s, b_ap, op0, op1, reads, writes):
        return self.op(e, lambda g: g.scalar_tensor_tensor(out=out_ap, in0=a_ap, scalar=s, in1=b_ap, op0=op0, op1=op1),
                       reads=reads, writes=writes)


RT = [1, 0] + [35 - j for j in range(2, NT)]
RTINV = {t: j for j, t in enumerate(RT)}


class Net:
    def __init__(self, nlayers=DEPTH, dump=None, upto=None, L=DEPTH, EW=NEL):
        self.L = L
        self.EW = EW
        self.nlayers = nlayers
        self.dump = dump or []
        self.upto = upto

    def build(self):
        nc = bass.Bass("TRN2", target_bir_lowering=False)
        self.nc = nc
        with ExitStack() as es:
            P = Prog(nc, es)
            self.P = P
            self.declare()
            self.body()
            P.barrier()
        return nc

    def declare(self):
        P = self.P
        L = self.L
        ext = lambda n, s: P.dram(n, s, F32, kind="ExternalInput")
        self.xin = ext("xin", [NTOK, D])
        self.cT = ext("cT", [128, 32])
        self.consts = ext("consts", [128, 128 * len(CONST_NAMES)])
        self.iotar = ext("iotar", [128, 544])
        self.iotac = ext("iotac", [128, 8])
        self.ada_w = ext("ada_w", [L, D, 6 * DC])
        self.ada_b = ext("ada_b", [L, 6 * DC])
        self.w_in = ext("w_in", [L, D, NINL])
        self.w_out = ext("w_out", [L, D, DC])
        self.lam_re = ext("s5_lam_re", [L, 2, 2 * NGP, 64])
        self.lam_im = ext("s5_lam_im", [L, 2, 2 * NGP, 64])
        self.log_step = ext("s5_log_step", [L, 2, 2 * NGP])
        self.b_re = ext("s5_b_re", [L, 2, 2 * NGP, 64, 16])
        self.b_im = ext("s5_b_im", [L, 2, 2 * NGP, 64, 16])
        self.c_re = ext("s5_c_re", [L, 2, 2 * NGP, 16, 64])
        self.c_im = ext("s5_c_im", [L, 2, 2 * NGP, 16, 64])
        self.s5_d = ext("s5_d", [L, UC])
        self.glu_w = ext("s5_glu_w", [L, 1024, UC])
        self.glu_b = ext("s5_glu_b", [L, UC])
        self.conv_w = ext("dn_conv_w", [L, 5, 3 * UC])
        self.a_log = ext("dn_a_log", [L, 2 * NH])
        self.dt_bias = ext("dn_dt_bias", [L, 2 * NH])
        self.norm_w = ext("dn_norm_w", [L, 128])
        self.ln1_g = ext("ln1_g", [L, D]); self.ln1_b = ext("ln1_b", [L, D])
        self.ln2_g = ext("ln2_g", [L, D]); self.ln2_b = ext("ln2_b", [L, D])
        self.router_w = ext("router_w", [L, D, 16])
        self.w1 = ext("exp_w1", [L, self.EW, D, 1024])
        self.w3 = ext("exp_w3", [L, self.EW, D, 1024])
        self.w2 = ext("exp_w2", [L, self.EW, 1024, D])
        self.yout = P.dram("yout", [NLAT, D], F32, kind="ExternalOutput")
        self.X = P.dram("X", [NTOK, D]); self.Xr = T(self.X.h, "Xr"); self.Xw = [T(self.X.h, "Xw0"), T(self.X.h, "Xw1")]
        self.MODL = P.dram("MODL", [L * 2, 6 * DC]); self.MOD = P.dram("MODF", [R * L * 2, 6 * DC])
        self.PT = P.dram("PT", [4 * UC, NTOK])
        self.UR = P.dram("UR", [UC, NTOK])
        self.Z = P.dram("Z", [NTOK, UC])
        self.AB = P.dram("AB", [NTOK, ABC])
        self.GT = P.dram("GT", [UC, NTOK]); self.GTF = P.dram("GTF", [NGP, R * 32, NTOK])
        self.S5TF = P.dram("S5TF", [UC // 64, R * 64, NTOK], BF16); self.DNOF = P.dram("DNOF", [R * NTOK, UC])
        self.MIXL = P.dram("MIXL", [NTOK, DC]); self.MIXF = P.dram("MIXF", [R * NTOK, DC]); self.FFP = P.dram("FFP", [NTOK, D])
        self.H2 = P.dram("H2", [NTOK, D], BF16); self.H2t = [T(self.H2.h, "H2t") for _ in range(NT)]; self.AFF = P.dram("AFF", [NTOK, 16]); self.AFFT = P.dram("AFFT", [16, NTOK]); self.AFFw = [T(self.AFF.h), T(self.AFF.h)]; self.AFFTw = [T(self.AFFT.h), T(self.AFFT.h)]
        self.RANKD = P.dram("RANKD", [NTOK, 16]); self.RANKT = P.dram("RANKT", [16, NTOK])
        self.YG = P.dram("YG", [NEL, 544, D], BF16); self.FF = P.dram("FF", [NTOK, D]); self.FFt = [T(self.FF.h, "FFt") for _ in range(NT)]
        self.S5T = P.dram("S5T", [UC, NTOK], BF16)
        self.QT = P.dram("QT", [UC, NTOK]); self.KT = P.dram("KT", [UC, NTOK])
        self.KTOK = P.dram("KTOK", [NTOK, UC]); self.VTOK = P.dram("VTOK", [NTOK, UC])
        self.GB = P.dram("GB", [NTOK, 4 * NH]); self.OD = P.dram("OD", [2, NTOK, UC]); self.ODt = [T(self.OD.h, "OD0"), T(self.OD.h, "OD1")]; self.DNO = P.dram("DNO", [NTOK, UC])
        self.dbg = {}
        for n, shp in self.dump:
            self.dbg[n] = P.dram("dbg_" + n, shp, F32, kind="ExternalOutput")

    def cst(self, name):
        i = CONST_NAMES.index(name)
        return self.C[:, i * 128:(i + 1) * 128]

    def body(self):
        P = self.P
        with P.scope():
            self.C = P.sb("C", [128, 128 * len(CONST_NAMES)])
            P.dma("sp", self.C[:], self.consts[:], reads=[self.consts], writes=[self.C])
            self.Cb = P.sb("Cb", [128, 256], BF16)
            P.op("dve", lambda e: e.tensor_copy(out=self.Cb[:], in_=self.C[:, 0:256]), reads=[self.C], writes=[self.Cb])
            self.iota = P.sb("iota", [128, 544])
            P.dma("sp", self.iota[:], self.iotar[:], reads=[self.iotar], writes=[self.iota])
            self.iotc = P.sb("iotc", [128, 8])
            P.dma("sp", self.iotc[:], self.iotac[:], reads=[self.iotac], writes=[self.iotc])
            self.copy_x()
            self.stage_mod()
            if self.upto == "mod":
                return
            for l in range(self.nlayers):
                self.stage_inproj(l)
                if self.upto == "inproj":
                    return
                self.stage_s5(l)
                if self.upto == "s5":
                    return
                self.stage_dn(l)
                if self.upto == "dn":
                    return
                self.stage_glu(l)
                self.stage_outproj(l)
                if self.upto == "outproj":
                    return
                self.stage_moe(l)
            self.write_out()

    def copy_x(self):
        P = self.P
        with P.scope():
            tb = [P.sb("cx", [128, D]) for _ in range(2)]
            for i in range(NT):
                t = tb[i % 2]
                P.dma("sp", t[:], self.xin[i * 128:(i + 1) * 128, :], reads=[self.xin], writes=[t])
                P.dma("pool", self.X[i * 128:(i + 1) * 128, :], t[:], reads=[t], writes=[self.X])

    def write_out(self):
        P = self.P
        with P.scope():
            tb = [P.sb("wo", [128, D]) for _ in range(2)]
            for i in range(2, NT):
                t = tb[i % 2]
                P.dma("sp", t[:], self.X[i * 128:(i + 1) * 128, :], reads=[self.X], writes=[t])
                P.dma("pool", self.yout[(i - 2) * 128:(i - 1) * 128, :], t[:], reads=[t], writes=[self.yout])

    def dump_dram(self, name, src_ap_fn, rows, cols, src):
        if name not in self.dbg:
            return
        P = self.P
        dst = self.dbg[name]
        with P.scope():
            tb = [P.sb("dd", [128, cols]) for _ in range(2)]
            for r0 in range(0, rows, 128):
                n = min(128, rows - r0)
                t = tb[(r0 // 128) % 2]
                P.dma("sp", t[0:n, :], src_ap_fn(r0, n), reads=[src], writes=[t])
                P.dma("sp", dst[r0:r0 + n, :], t[0:n, :], reads=[t], writes=[dst])

    def stage_mod(self):
        P = self.P
        W = 6 * DC
        with P.scope():
            ct = P.sb("ct", [128, 32])
            P.dma("sp", ct[:], self.cT[:], reads=[self.cT], writes=[ct])
            sc = P.sb("sc", [128, 32])
            P.act(sc[:], ct[:], AF.Silu, [ct], [sc])
            wb = [P.sb("adw", [128, 16, 512]) for _ in range(2)]
            pm = [P.ps("pmod", [2, 512]) for _ in range(2)]
            row = P.sb("modrow", [2, W])
            bia = P.sb("modb", [2, W])
            for l in range(self.nlayers):
                P.dma("pool", bia[:], self.ada_b[l:l + 1, :].partition_broadcast(2), reads=[self.ada_b], writes=[bia])
                for nb in range(W // 512):
                    w = wb[nb % 2]
                    p = pm[nb % 2]
                    src = self.ada_w[l].rearrange("(k p) n -> p k n", p=128)[:, :, nb * 512:(nb + 1) * 512]
                    P.dma("sp" if nb % 2 == 0 else "pool", w[:], src, reads=[self.ada_w], writes=[w])
                    for kc in range(16):
                        P.mm(p[:], sc[:, kc:32:16], w[:, kc, :], kc == 0, kc == 15, [sc, w], [p])
                    P.tt("dve", row[:, nb * 512:(nb + 1) * 512], p[:], bia[:, nb * 512:(nb + 1) * 512], ALU.add,
                         [p, bia], [row])
                for sg in (1, 4):
                    P.ts("dve", row[:, sg * DC:(sg + 1) * DC], row[:, sg * DC:(sg + 1) * DC], 1.0, None, ALU.add, None, [row], [row])
                P.dma("sp", self.MODL[l * 2:(l + 1) * 2, :], row[:], reads=[row], writes=[self.MODL])
            P.coll("AllGather", ALU.bypass, RG, self.MODL[:, :], self.MOD[:, :], [self.MODL], [self.MOD])

    def mod_load(self, q, dst, l, row, seg):
        v = self.MOD[:, :].rearrange("(r l two) (s j) -> l two s r j", r=R, l=self.L, two=2, s=6)[l][row][seg]
        self.P.dma(q, dst[:, :].rearrange("p (r j) -> p r j", r=R), v.partition_broadcast(128), reads=[self.MOD], writes=[dst])

    def bcast_load(self, q, dst, src_row_ap, src_t):
        self.P.dma(q, dst[:], src_row_ap.partition_broadcast(128), reads=[src_t], writes=[dst])

    def stage_inproj(self, l):
        P = self.P
        fwd_tiles = list(range(NT))
        passes = [(fwd_tiles[:12], False, 0), (fwd_tiles[12:24], False, 12 * 128), (fwd_tiles[24:], False, 24 * 128),
                  (RT[:12], True, 0), (RT[12:24], True, 12 * 128), (RT[24:], True, 24 * 128)]
        with P.scope():
            modt = {}
            for who, row in (("lat", 0), ("ctx", 1)):
                scp = P.sb("scp", [128, D]); sh = P.sb("sh", [128, D])
                self.mod_load("sp", scp, l, row, 1)
                self.mod_load("pool", sh, l, row, 0)
                modt[who] = (scp, sh)
            hinT = P.sb("hinT", [128, 16, 12 * 128], BF16)
            xt = [P.sb("xt", [128, D]) for _ in range(2)]
            hb = [P.sb("hb", [128, D], BF16) for _ in range(2)]
            ptr = [P.ps("ptr", [128, 512]) for _ in range(2)]
            pacc = [P.ps("pacc", [128, 512]) for _ in range(3)]
            wst = [P.sb("wst", [128, 16, 128]) for _ in range(2)]
            wbf = [P.sb("wbf", [128, 16, 128], BF16) for _ in range(2)]
            wst2 = P.sb("wst2", [128, 16, 256])
            wbf2 = P.sb("wbf2", [128, 16, 256], BF16)
            ot = [P.sb("ot", [128, 12 * 128]) for _ in range(2)]
            ot2 = [P.sb("ot2", [128, 512]) for _ in range(2)]
            win = self.w_in[l].rearrange("(k p) n -> p k n", p=128)
            for tiles, rev, col0 in passes:
                for ti, t in enumerate(tiles):
                    x = xt[ti % 2]; h = hb[ti % 2]
                    scp, sh = modt["ctx" if t < 2 else "lat"]
                    P.dma("sp" if ti % 2 == 0 else "pool", x[:], self.X[t * 128:(t + 1) * 128, :], reads=[self.X], writes=[x])
                    P.tt("dve", x[:], x[:], scp[:], ALU.mult, [x, scp], [x])
                    P.tt("dve", h[:], x[:], sh[:], ALU.add, [x, sh], [h])
                    idm = self.Cb[:, 128:256] if rev else self.Cb[:, 0:128]
                    for kg in range(4):
                        pt = ptr[kg % 2]
                        for kk in range(4):
                            kc = kg * 4 + kk
                            P.mm(pt[:, kk * 128:(kk + 1) * 128], h[:, kc * 128:(kc + 1) * 128], idm, True, True, [h, self.Cb], [pt])
                        P.act(hinT[:, kg * 4:(kg + 1) * 4, ti * 128:(ti + 1) * 128],
                              pt[:].rearrange("p (k t) -> p k t", k=4), AF.Copy, [pt], [hinT])
                ntk = len(tiles) * 128
                noc = (UC // 128) if rev else (4 * UC // 128)
                dst = self.UR if rev else self.PT
                for oc in range(noc):
                    ws = wst[oc % 2]; wb = wbf[oc % 2]; o = ot[oc % 2]
                    P.dma("sp" if oc % 2 == 0 else "pool", ws[:], win[:, :, oc * 128:(oc + 1) * 128], reads=[self.w_in], writes=[ws])
                    P.op("dve", lambda e, wb=wb, ws=ws: e.tensor_copy(out=wb[:], in_=ws[:]), reads=[ws], writes=[wb])
                    for tb in range(0, ntk, 512):
                        n = min(512, ntk - tb)
                        pa = pacc[(tb // 512) % 3]
                        for kc in range(16):
                            P.mm(pa[:, 0:n], wb[:, kc, :], hinT[:, kc, tb:tb + n], kc == 0, kc == 15, [wb, hinT], [pa])
                        P.act(o[:, tb:tb + n], pa[:, 0:n], AF.Copy, [pa], [o])
                    P.dma("sp", dst[oc * 128:(oc + 1) * 128, col0:col0 + ntk], o[:, 0:ntk], reads=[o], writes=[dst])
                if rev:
                    continue
                nzb = UC // 256
                for zb in range(nzb + 1):
                    c0 = 4 * UC + zb * 256
                    ncol = 256 if zb < nzb else ABC
                    P.dma("sp", wst2[:, :, 0:ncol], win[:, :, c0:c0 + ncol], reads=[self.w_in], writes=[wst2])
                    P.op("dve", lambda e, ncol=ncol: e.tensor_copy(out=wbf2[:, :, 0:ncol], in_=wst2[:, :, 0:ncol]), reads=[wst2], writes=[wbf2])
                    for ti, t in enumerate(tiles):
                        pa = pacc[ti % 3]; o2 = ot2[ti % 2]
                        for kc in range(16):
                            P.mm(pa[:, 0:ncol], hinT[:, kc, ti * 128:(ti + 1) * 128], wbf2[:, kc, 0:ncol], kc == 0, kc == 15, [wbf2, hinT], [pa])
                        P.act(o2[:, 0:ncol], pa[:, 0:ncol], AF.Copy, [pa], [o2])
                        if zb < nzb:
                            P.dma("pool", self.Z[t * 128:(t + 1) * 128, zb * 256:(zb + 1) * 256], o2[:, 0:256], reads=[o2], writes=[self.Z])
                        else:
                            P.dma("pool", self.AB[t * 128:(t + 1) * 128, :], o2[:, 0:ABC], reads=[o2], writes=[self.AB])


def prep_inputs(inputs, b, r=0, L=DEPTH, EW=None):
    f = lambda a: np.ascontiguousarray(np.asarray(a, dtype=np.float32))
    g = lambda k: np.asarray(inputs[k])[:L]
    m = {}
    m["xin"] = f(np.concatenate([inputs["ctx"][b], inputs["x"][b]], axis=0))
    cT = np.concatenate([np.asarray(inputs["c"][b]).reshape(16, 128).T, np.asarray(inputs["c_ctx"]).reshape(16, 128).T], axis=1)
    m["cT"] = f(cT)
    m["consts"] = make_consts()
    m["iotar"] = f(np.tile(np.arange(544, dtype=np.float32)[None, :], (128, 1)))
    m["iotac"] = f(np.arange(128, dtype=np.float32)[:, None] + 128.0 * np.arange(8, dtype=np.float32)[None, :])
    for k in ["dn_norm_w", "ln1_g", "ln1_b", "ln2_g", "ln2_b"]:
        m[k] = f(g(k))
    aw = g("ada_w"); ab_ = g("ada_b")
    m["ada_w"] = f(np.concatenate([aw[:, :, sg * D + r * DC:sg * D + (r + 1) * DC] for sg in range(6)], axis=2))
    m["ada_b"] = f(np.concatenate([ab_[:, sg * D + r * DC:sg * D + (r + 1) * DC] for sg in range(6)], axis=1))
    cu = slice(r * UC, (r + 1) * UC)
    heads = list(range(r * NH, (r + 1) * NH))
    abcols = [5120 + d * 16 + k * 8 + h for d in range(2) for k in range(2) for h in heads]
    w_in = g("w_in")
    m["w_in"] = f(np.concatenate([w_in[:, :, j * 1024 + r * UC:j * 1024 + (r + 1) * UC] for j in range(5)] + [w_in[:, :, abcols]], axis=2))
    m["w_out"] = f(g("w_out")[:, :, r * DC:(r + 1) * DC])
    gs = slice(r * 2 * NGP, (r + 1) * 2 * NGP)
    for k in ["s5_lam_re", "s5_lam_im", "s5_log_step", "s5_b_re", "s5_b_im", "s5_c_re", "s5_c_im"]:
        m[k] = f(g(k)[:, :, gs])
    m["s5_d"] = f(g("s5_d")[:, cu])
    m["s5_glu_w"] = f(g("s5_glu_w")[:, :, cu])
    m["s5_glu_b"] = f(g("s5_glu_b")[:, cu])
    cw = g("dn_conv_w")
    m["dn_conv_w"] = f(np.concatenate([cw[:, :, j * 1024 + r * UC:j * 1024 + (r + 1) * UC] for j in range(3)], axis=2))
    m["dn_a_log"] = f(g("dn_a_log")[:, :, heads].reshape(L, 2 * NH))
    m["dn_dt_bias"] = f(g("dn_dt_bias")[:, :, heads].reshape(L, 2 * NH))
    eo = list(range(r * NEL, (r + 1) * NEL)) + [e for e in range(16) if not (r * NEL <= e < (r + 1) * NEL)]
    m["router_w"] = f(g("router_w")[:, :, eo])
    ne = NEL if EW is None else EW
    for k in ["exp_w1", "exp_w3", "exp_w2"]:
        m[k] = f(g(k)[:, r * NEL:r * NEL + ne])
    return m


def kernel(**inputs):
    net = Net()
    nc = net.build()
    in_maps = [prep_inputs(inputs, c // R, c % R) for c in range(2 * R)]
    res = run_bass_kernel_spmd(nc, in_maps, core_ids=list(range(2 * R)))
    return np.stack([np.asarray(res.results[b * R]["yout"], dtype=np.float32) for b in range(2)], axis=0)


MAGIC = 12582912.0
TWO_PI = 6.283185307179586


def _s5_prep(self, l):
    P = self.P
    ident = self.cst("ident")
    pr = {}
    pt = P.ps("s5pp", [128, 512])
    G = NGP
    pr["rho"] = [P.sb("s5rho", [128, G]) for _ in range(2)]; pr["f"] = [P.sb("s5f", [128, G]) for _ in range(2)]
    pr["BrT"] = [P.sb("s5BrT", [32, G, 128], BF16) for _ in range(2)]; pr["BiT"] = [P.sb("s5BiT", [32, G, 128], BF16) for _ in range(2)]
    pr["CTr"] = [P.sb("s5CTr", [128, G, 32], BF16) for _ in range(2)]; pr["CTi"] = [P.sb("s5CTi", [128, G, 32], BF16) for _ in range(2)]
    pr["dsk"] = P.sb("s5dsk", [32, G])

    def one_dir(d):
        rho = pr["rho"][d]; f = pr["f"][d]; BrT = pr["BrT"][d]; BiT = pr["BiT"][d]; CTr = pr["CTr"][d]; CTi = pr["CTi"][d]
        A = P.sb("s5A", [G, 3, 128])
        P.dma("sp", A[:, 0, :], self.lam_re[l][d].rearrange("(gp two) p -> gp (two p)", two=2), reads=[self.lam_re], writes=[A])
        P.dma("sp", A[:, 1, :], self.lam_im[l][d].rearrange("(gp two) p -> gp (two p)", two=2), reads=[self.lam_im], writes=[A])
        ls = P.sb("s5ls", [G, 2])
        P.dma("sp", ls[:], self.log_step[l][d:d + 1, :].rearrange("o (gp two) -> (o gp) two", two=2), reads=[self.log_step], writes=[ls])
        P.op("dve", lambda e, A=A, ls=ls: e.tensor_copy(out=A[:, 2, :].rearrange("g (t p) -> g t p", t=2),
                                                        in_=ls[:, :].unsqueeze(2).to_broadcast([G, 2, 64])), reads=[ls], writes=[A])
        q = P.sb("s5q", [128, 3, G])
        for k in range(3):
            P.mm(pt[:, k * G:(k + 1) * G], A[:, k, :], ident[0:G, 0:G], True, True, [A, self.C], [pt])
        P.act(q[:].rearrange("p k g -> p (k g)"), pt[:, 0:3 * G], AF.Copy, [pt], [q])
        lr = q[:, 0, :]; li = q[:, 1, :]
        dl = P.sb("s5dl", [128, G])
        P.act(dl[:], q[:, 2, :], AF.Exp, [q], [dl])
        P.tt("dve", rho[:], lr, dl[:], ALU.mult, [q, dl], [rho])
        P.act(rho[:], rho[:], AF.Exp, [rho], [rho])
        P.tt("dve", f[:], li, dl[:], ALU.mult, [q, dl], [f])
        P.ts("dve", f[:], f[:], 1.0 / TWO_PI, None, ALU.mult, None, [f], [f])
        w = P.sb("s5w", [128, 6, G])
        for k, off in ((0, 0.0), (1, 0.25)):
            P.ts("dve", w[:, 2, :], f[:], off, None, ALU.add, None, [f], [w])
            P.ts("dve", w[:, 3, :], w[:, 2, :], MAGIC, MAGIC, ALU.add, ALU.subtract, [w], [w])
            P.tt("dve", w[:, 2, :], w[:, 2, :], w[:, 3, :], ALU.subtract, [w], [w])
            P.act(w[:, k, :], w[:, 2, :], AF.Sin, [w], [w], scale=TWO_PI)
        sn = w[:, 0, :]; cs = w[:, 1, :]
        nr = P.sb("s5nr", [128, G]); ni = P.sb("s5ni", [128, G]); den = P.sb("s5den", [128, G])
        cr = P.sb("s5cr", [128, G]); ci = P.sb("s5ci", [128, G]); tmp = P.sb("s5tmp", [128, G])
        P.tt("dve", nr[:], rho[:], cs, ALU.mult, [rho, w], [nr])
        P.ts("dve", nr[:], nr[:], -1.0, None, ALU.add, None, [nr], [nr])
        P.tt("dve", ni[:], rho[:], sn, ALU.mult, [rho, w], [ni])
        P.tt("dve", den[:], lr, lr, ALU.mult, [q], [den])
        P.tt("dve", tmp[:], li, li, ALU.mult, [q], [tmp])
        P.tt("dve", den[:], den[:], tmp[:], ALU.add, [den, tmp], [den])
        P.op("dve", lambda e, den=den: e.reciprocal(out=den[:], in_=den[:]), reads=[den], writes=[den])
        P.tt("dve", cr[:], nr[:], lr, ALU.mult, [nr, q], [cr])
        P.tt("dve", tmp[:], ni[:], li, ALU.mult, [ni, q], [tmp])
        P.tt("dve", cr[:], cr[:], tmp[:], ALU.add, [cr, tmp], [cr])
        P.tt("dve", cr[:], cr[:], den[:], ALU.mult, [cr, den], [cr])
        P.tt("dve", ci[:], ni[:], lr, ALU.mult, [ni, q], [ci])
        P.tt("dve", tmp[:], nr[:], li, ALU.mult, [nr, q], [tmp])
        P.tt("dve", ci[:], ci[:], tmp[:], ALU.subtract, [ci, tmp], [ci])
        P.tt("dve", ci[:], ci[:], den[:], ALU.mult, [ci, den], [ci])
        Bl = P.sb("s5Bl", [128, 2, G, 16])
        P.dma("sp", Bl[:, 0], self.b_re[l][d].rearrange("(gp two) p c -> (two p) gp c", two=2), reads=[self.b_re], writes=[Bl])
        P.dma("pool", Bl[:, 1], self.b_im[l][d].rearrange("(gp two) p c -> (two p) gp c", two=2), reads=[self.b_im], writes=[Bl])
        S = P.sb("s5S", [128, 2, G, 32])
        P.op("pool", lambda e, S=S: e.memset(S[:], 0.0), writes=[S])
        t1 = P.sb("s5t1", [128, G, 16]); t2 = P.sb("s5t2", [128, G, 16])
        crb = cr[:, :].unsqueeze(2).to_broadcast([128, G, 16]); cib = ci[:, :].unsqueeze(2).to_broadcast([128, G, 16])
        for k, (a0, a1, op) in enumerate(((0, 1, ALU.subtract), (1, 0, ALU.add))):
            P.tt("dve", t1[:], Bl[:, a0], crb, ALU.mult, [Bl, cr], [t1])
            P.tt("dve", t2[:], Bl[:, a1], cib, ALU.mult, [Bl, ci], [t2])
            for half in range(2):
                ps_ = slice(half * 64, half * 64 + 64)
                P.tt("dve", S[ps_, k, :, half * 16:half * 16 + 16], t1[ps_], t2[ps_], op, [t1, t2], [S])
        for k, dst in ((0, BrT), (1, BiT)):
            for g4 in range(G // 4):
                for gg in range(4):
                    gp = g4 * 4 + gg
                    P.mm(pt[0:32, gg * 128:(gg + 1) * 128], S[:, k, gp, :], ident, True, True, [S, self.C], [pt])
                P.act(dst[:, g4 * 4:(g4 + 1) * 4, :], pt[0:32, :].rearrange("p (g s) -> p g s", g=4), AF.Copy, [pt], [dst])
        Cx = P.sb("s5Cx", [64, 2, G, 128])
        P.op("pool", lambda e, Cx=Cx: e.memset(Cx[:], 0.0), writes=[Cx])
        for k, src in ((0, self.c_re), (1, self.c_im)):
            v = src[l][d].rearrange("(gp two) c p -> two c gp p", two=2)
            P.dma("sp", Cx[0:16, k, :, 0:64], v[0], reads=[src], writes=[Cx])
            P.dma("pool", Cx[32:48, k, :, 64:128], v[1], reads=[src], writes=[Cx])
        for k, dst, scl in ((0, CTr, 1.0), (1, CTi, -1.0)):
            for g8 in range(G // 8):
                for gg in range(8):
                    gp = g8 * 8 + gg
                    P.mm(pt[:, gg * 64:(gg + 1) * 64], Cx[:, k, gp, :], ident[0:64, 0:64], True, True, [Cx, self.C], [pt])
                P.act(dst[:, g8 * 8:(g8 + 1) * 8, :].rearrange("p g (t c) -> p g t c", t=2),
                      pt[:, :].rearrange("p (g t c) -> p g t c", g=8, t=2)[:, :, :, 0:16], AF.Copy, [pt], [dst], scale=scl)
    for d in range(2):
        with P.scope():
            one_dir(d)
    dA = P.sb("s5dA", [G, 32])
    P.dma("sp", dA[:], self.s5_d[l:l + 1, :].rearrange("o (gp j) -> (o gp) j", j=32), reads=[self.s5_d], writes=[dA])
    P.mm(pt[0:32, 0:G], dA[:], ident[0:G, 0:G], True, True, [dA, self.C], [pt])
    dsk = pr["dsk"]
    P.act(dsk[:], pt[0:32, 0:G], AF.Copy, [pt], [dsk])
    return pr


def _stage_s5(self, l):
    P = self.P
    J = self.cst("J")
    with P.scope():
        pr = _s5_prep(self, l)
        H = [[P.sb("s5H", [128, NTOK], BF16) for _ in range(2)] for _ in range(2)]
        uf = [P.sb("s5uf", [32, NTOK]) for _ in range(2)]
        ub = [P.sb("s5ub", [32, NTOK], BF16) for _ in range(2)]
        DB = []
        for d in range(2):
            b = {"cos": P.sb("s5cos", [128, 513]), "sin": P.sb("s5sin", [128, 513]), "rho": P.sb("s5rhoT", [128, 512]),
                 "ph": P.sb("s5ph", [128, 513]), "rr": P.sb("s5rr", [128, 513]),
                 "px": [P.ps("s5px", [128, 512]) for _ in range(2)],
                 "tm": [P.sb("s5tm", [128, 512]) for _ in range(4)], "xt": [P.sb("s5xt", [128, 512]) for _ in range(2)],
                 "g": [[P.sb("s5g", [128, 512]) for _ in range(2)] for _ in range(2)], "ini": [P.sb("s5ini", [128, 4]) for _ in range(2)]}
            DB.append(b)
        pf = P.ps("s5pf", [32, 512]); pb = P.ps("s5pb", [128, 128]); pj = P.ps("s5pj", [32, 512])
        ybt = P.sb("s5ybt", [128, 32]); ybs = P.sb("s5ybs", [32, 512]); yy = P.sb("s5yy", [32, 512]); ww = P.sb("s5ww", [32, 512])
        GTt = P.sb("s5GT", [32, NTOK])
        srcs = [self.PT, self.UR]
        OB = [(pf, pb, pj, ybt, ybs, yy, ww),
              (DB[0]["px"][0], DB[0]["px"][1], DB[1]["px"][0], P.sb("s5ybt2", [128, 32]), P.sb("s5ybs2", [32, 512]),
               P.sb("s5yy2", [32, 512]), P.sb("s5ww2", [32, 512]))]

        def dir_gen(d, gp):
            b = DB[d]
            cosT, sinT, rhoT, ph, rr, px, tm, xt_, gg_, ini = b["cos"], b["sin"], b["rho"], b["ph"], b["rr"], b["px"], b["tm"], b["xt"], b["g"], b["ini"]
            f = pr["f"][d]; rho = pr["rho"][d]
            P.ts("dve", ph[:], self.iota[:, 0:513], f[:, gp:gp + 1], None, ALU.mult, None, [self.iota, f], [ph])
            yield
            for dst, off in ((sinT, 0.0), (cosT, 0.25)):
                if off:
                    P.ts("dve", ph[:], ph[:], off, None, ALU.add, None, [ph], [ph])
                    yield
                P.ts("dve", rr[:], ph[:], MAGIC, MAGIC, ALU.add, ALU.subtract, [ph], [rr])
                yield
                P.tt("dve", rr[:], ph[:], rr[:], ALU.subtract, [ph, rr], [rr])
                yield
                P.act(dst[:], rr[:], AF.Sin, [rr], [dst], scale=TWO_PI)
                yield
            P.ts("dve", rhoT[:], self.iota[:, 0:512], 0.0, rho[:, gp:gp + 1], ALU.mult, ALU.add, [self.iota, rho], [rhoT])
            nch = 9
            for k in range(nch):
                c0 = k * 512
                n = min(512, NTOK - c0)
                xr = px[0]; xi = px[1]
                P.mm(xr[:, 0:n], pr["BrT"][d][:, gp, :], ub[d][:, c0:c0 + n], True, True, [pr["BrT"][d], ub[d]], [xr])
                P.mm(xi[:, 0:n], pr["BiT"][d][:, gp, :], ub[d][:, c0:c0 + n], True, True, [pr["BiT"][d], ub[d]], [xi])
                yield
                c = cosT[:, 0:n]; s = sinT[:, 0:n]
                P.tt("dve", tm[0][:, 0:n], xr[:, 0:n], c, ALU.mult, [xr, cosT], [tm[0]])
                P.tt("dve", tm[1][:, 0:n], xi[:, 0:n], s, ALU.mult, [xi, sinT], [tm[1]])
                yield
                P.tt("pool", xt_[0][:, 0:n], tm[0][:, 0:n], tm[1][:, 0:n], ALU.add, [tm[0], tm[1]], [xt_[0]])
                P.tt("dve", tm[2][:, 0:n], xi[:, 0:n], c, ALU.mult, [xi, cosT], [tm[2]])
                P.tt("dve", tm[3][:, 0:n], xr[:, 0:n], s, ALU.mult, [xr, sinT], [tm[3]])
                yield
                P.tt("pool", xt_[1][:, 0:n], tm[2][:, 0:n], tm[3][:, 0:n], ALU.subtract, [tm[2], tm[3]], [xt_[1]])
                g = gg_[k % 2]
                icur = ini[k % 2]; inxt = ini[(k + 1) % 2]
                for ri in range(2):
                    init = 0.0 if k == 0 else icur[:, ri:ri + 1]
                    P.op("dve", lambda e, g=g, ri=ri, init=init, n=n: e.tensor_tensor_scan(
                        out=g[ri][:, 0:n], data0=rhoT[:, 0:n], data1=xt_[ri][:, 0:n], initial=init,
                        op0=ALU.mult, op1=ALU.add), reads=[rhoT, xt_[ri]] + ([icur] if k else []), writes=[g[ri]])
                    yield
                if k + 1 < nch:
                    grl = g[0][:, n - 1:n]; gil = g[1][:, n - 1:n]; cT = cosT[:, n:n + 1]; sT = sinT[:, n:n + 1]
                    P.ts("dve", inxt[:, 2:3], gil, sT, None, ALU.mult, None, [g[1], sinT], [inxt])
                    yield
                    P.stt("dve", inxt[:, 0:1], grl, cT, inxt[:, 2:3], ALU.mult, ALU.subtract, [g[0], cosT, inxt], [inxt])
                    yield
                    P.ts("dve", inxt[:, 3:4], gil, cT, None, ALU.mult, None, [g[1], cosT], [inxt])
                    yield
                    P.stt("dve", inxt[:, 1:2], grl, sT, inxt[:, 3:4], ALU.mult, ALU.add, [g[0], sinT, inxt], [inxt])
                    yield
                P.tt("pool", tm[0][:, 0:n], g[0][:, 0:n], c, ALU.mult, [g[0], cosT], [tm[0]])
                P.tt("dve", tm[1][:, 0:n], g[1][:, 0:n], s, ALU.mult, [g[1], sinT], [tm[1]])
                yield
                P.tt("pool", H[d][0][:, c0:c0 + n], tm[0][:, 0:n], tm[1][:, 0:n], ALU.subtract, [tm[0], tm[1]], [H[d][0]])
                P.tt("dve", tm[2][:, 0:n], g[0][:, 0:n], s, ALU.mult, [g[0], sinT], [tm[2]])
                yield
                P.tt("pool", tm[3][:, 0:n], g[1][:, 0:n], c, ALU.mult, [g[1], cosT], [tm[3]])
                yield
                P.tt("pool", H[d][1][:, c0:c0 + n], tm[2][:, 0:n], tm[3][:, 0:n], ALU.add, [tm[2], tm[3]], [H[d][1]])
                yield

        for gp in range(NGP):
            for d in range(2):
                P.dma("sp" if d == 0 else "pool", uf[d][:], srcs[d][gp * 32:(gp + 1) * 32, :], reads=[srcs[d]], writes=[uf[d]])
                P.act(ub[d][:], uf[d][:], AF.Copy, [uf[d]], [ub[d]])
            _interleave([dir_gen(0, gp), dir_gen(1, gp)])
            def out_gen(nb, ob):
                pf_, pb_, pj_, ybt_, ybs_, yy_, ww_ = ob
                c0 = nb * 512
                n = min(512, NTOK - c0)
                P.mm(pf_[0:32, 0:n], pr["CTr"][0][:, gp, :], H[0][0][:, c0:c0 + n], True, False, [pr["CTr"][0], H[0][0]], [pf_])
                P.mm(pf_[0:32, 0:n], pr["CTi"][0][:, gp, :], H[0][1][:, c0:c0 + n], False, True, [pr["CTi"][0], H[0][1]], [pf_])
                for sbk in range(n // 128):
                    tau = nb * 4 + sbk
                    j = RTINV[tau]
                    P.mm(pb_[:, 0:32], H[1][0][:, j * 128:(j + 1) * 128], pr["CTr"][1][:, gp, :], True, False, [pr["CTr"][1], H[1][0]], [pb_])
                    P.mm(pb_[:, 0:32], H[1][1][:, j * 128:(j + 1) * 128], pr["CTi"][1][:, gp, :], False, True, [pr["CTi"][1], H[1][1]], [pb_])
                    yield
                    P.act(ybt_[:], pb_[:, 0:32], AF.Copy, [pb_], [ybt_])
                    yield
                    P.mm(pj_[0:32, sbk * 128:(sbk + 1) * 128], ybt_[:], J, True, True, [ybt_, self.C], [pj_])
                yield
                P.act(ybs_[:, 0:n], pj_[0:32, 0:n], AF.Copy, [pj_], [ybs_])
                yield
                P.tt("dve", yy_[:, 0:n], pf_[0:32, 0:n], ybs_[:, 0:n], ALU.add, [pf_, ybs_], [yy_])
                yield
                P.stt("dve", yy_[:, 0:n], uf[0][:, c0:c0 + n], pr["dsk"][:, gp:gp + 1], yy_[:, 0:n], ALU.mult, ALU.add, [uf[0], pr["dsk"], yy_], [yy_])
                yield
                P.act(ww_[:, 0:n], yy_[:, 0:n], AF.Square, [yy_], [ww_])
                yield
                P.ts("dve", ww_[:, 0:n], ww_[:, 0:n], 0.044715, 1.0, ALU.mult, ALU.add, [ww_], [ww_])
                yield
                P.tt("dve", ww_[:, 0:n], ww_[:, 0:n], yy_[:, 0:n], ALU.mult, [ww_, yy_], [ww_])
                yield
                P.act(ww_[:, 0:n], ww_[:, 0:n], AF.Sigmoid, [ww_], [ww_], scale=1.5957691216)
                yield
                P.tt("dve", GTt[:, c0:c0 + n], ww_[:, 0:n], yy_[:, 0:n], ALU.mult, [ww_, yy_], [GTt])
                yield

            def out_chain(par):
                for nb in range(par, 9, 2):
                    yield from out_gen(nb, OB[par])

            _interleave([out_chain(0), out_chain(1)])
            P.dma("sp", self.GT[gp * 32:(gp + 1) * 32, :], GTt[:], reads=[GTt], writes=[self.GT])
            P.coll("AllGather", ALU.bypass, RG, self.GT[gp * 32:(gp + 1) * 32, :], self.GTF[gp], [self.GT], [self.GTF])


Net.stage_s5 = _stage_s5


def dn_tile_rows(dram_t, i, ncols_slice=None):
    if i < 2:
        return [(slice(0, 128), dram_t[i * 128:(i + 1) * 128, :])]
    c0 = 2 * (i - 2)
    v = dram_t[NCTX:NTOK, :].rearrange("(r c) n -> c r n", c=64)
    return [(slice(0, 64), v[c0]), (slice(64, 128), v[c0 + 1])]


def _stage_dn_prep(self, l):
    P = self.P
    ident = self.cst("ident"); ones = self.cst("ones")
    with P.scope():
        pt = P.ps("dpt", [128, 512]); pn = [P.ps("dpn", [128, 512]) for _ in range(2)]
        NCH = 3 * UC // 128
        HC = UC // 128
        cwl = P.sb("cwl", [5, 3 * UC])
        P.dma("sp", cwl[:], self.conv_w[l], reads=[self.conv_w], writes=[cwl])
        cwT = P.sb("cwT", [128, NCH, 5])
        for c in range(NCH):
            P.mm(pt[:, c * 8:c * 8 + 5], cwl[0:5, c * 128:(c + 1) * 128], ident[0:5, 0:5], True, True, [cwl, self.C], [pt])
        P.act(cwT[:], pt[:, 0:8 * NCH].rearrange("p (c j) -> p c j", j=8)[:, :, 0:5], AF.Copy, [pt], [cwT])
        raw = P.sb("draw", [128, NTOK]); pd = P.sb("dpd", [128, NTOK + 8]); acc = P.sb("dacc", [128, NTOK]); cs = P.sb("dcs", [128, NTOK])
        sq = P.sb("dsq", [128, 512]); rs = P.sb("drs", [128, 512]); tk = [P.sb("dtk", [128, 512]) for _ in range(2)]
        P.op("pool", lambda e: e.memset(pd[:], 0.0), writes=[pd])
        for c in range(NCH):
            kind = c // HC
            h = c % HC
            P.dma("sp", raw[:], self.PT[UC + c * 128:UC + (c + 1) * 128, :], reads=[self.PT], writes=[raw])
            P.act(pd[:, 2:258], raw[:, 0:256], AF.Copy, [raw], [pd])
            P.op("dve", lambda e: e.tensor_copy(out=pd[:, 262:262 + NLAT].rearrange("p (c r) -> p c r", r=64),
                                                in_=raw[:, 256:NTOK].rearrange("p (r c) -> p c r", c=64)), reads=[raw], writes=[pd])
            for base, o0, n in ((0, 0, 256), (260, 256, NLAT)):
                P.ts("dve", acc[:, o0:o0 + n], pd[:, base:base + n], cwT[:, c, 0:1], None, ALU.mult, None, [pd, cwT], [acc])
                for j in range(1, 5):
                    P.stt("dve", acc[:, o0:o0 + n], pd[:, base + j:base + j + n], cwT[:, c, j:j + 1], acc[:, o0:o0 + n],
                          ALU.mult, ALU.add, [pd, cwT, acc], [acc])
            P.act(cs[:], acc[:], AF.Silu, [acc], [cs])
            if kind < 2:
                for nb in range(9):
                    c0 = nb * 512; n = min(512, NTOK - c0)
                    p_ = pn[nb % 2]
                    P.act(sq[:, 0:n], cs[:, c0:c0 + n], AF.Square, [cs], [sq])
                    P.mm(p_[:, 0:n], ones, sq[:, 0:n], True, True, [self.C, sq], [p_])
                    P.act(rs[:, 0:n], p_[:, 0:n], AF.Sqrt, [p_], [rs], bias=1e-6)
                    P.op("dve", lambda e, n=n: e.reciprocal(out=rs[:, 0:n], in_=rs[:, 0:n]), reads=[rs], writes=[rs])
                    if kind == 0:
                        P.stt("dve", cs[:, c0:c0 + n], cs[:, c0:c0 + n], 128.0 ** -0.5, rs[:, 0:n], ALU.mult, ALU.mult, [cs, rs], [cs])
                    else:
                        P.tt("dve", cs[:, c0:c0 + n], cs[:, c0:c0 + n], rs[:, 0:n], ALU.mult, [cs, rs], [cs])
                P.dma("sp", (self.QT if kind == 0 else self.KT)[h * 128:(h + 1) * 128, :], cs[:], reads=[cs],
                      writes=[self.QT if kind == 0 else self.KT])
            if kind >= 1:
                dst = self.KTOK if kind == 1 else self.VTOK
                for i4 in range(0, NT, 4):
                    nn = min(4, NT - i4)
                    for ii in range(nn):
                        i = i4 + ii
                        P.mm(pt[:, ii * 128:(ii + 1) * 128], cs[:, i * 128:(i + 1) * 128], ident, True, True, [cs, self.C], [pt])
                    t_ = tk[(i4 // 4) % 2]
                    P.act(t_[:, 0:nn * 128], pt[:, 0:nn * 128], AF.Copy, [pt], [t_])
                    for ii in range(nn):
                        i = i4 + ii
                        P.dma("pool", dst[i * 128:(i + 1) * 128, h * 128:(h + 1) * 128], t_[:, ii * 128:(ii + 1) * 128], reads=[t_], writes=[dst])
        dtb = P.sb("ddtb", [128, 2 * NH]); nea = P.sb("dnea", [128, 2 * NH])
        self.bcast_load("sp", dtb, self.dt_bias[l:l + 1, :], self.dt_bias)
        self.bcast_load("sp", nea, self.a_log[l:l + 1, :], self.a_log)
        P.act(nea[:], nea[:], AF.Exp, [nea], [nea])
        P.ts("dve", nea[:], nea[:], -1.0, None, ALU.mult, None, [nea], [nea])
        abt = [P.sb("dabt", [128, ABC]) for _ in range(2)]; gbt = [P.sb("dgbt", [128, ABC]) for _ in range(2)]
        for i in range(NT):
            a_ = abt[i % 2]; g_ = gbt[i % 2]
            for ps_, ap in dn_tile_rows(self.AB, i):
                P.dma("sp", a_[ps_, :], ap, reads=[self.AB], writes=[a_])
            av = a_[:, :].rearrange("p (d k h) -> p d k h", d=2, k=2)
            G2 = 2 * NH
            gv = g_[:, 0:G2].rearrange("p (d h) -> p d h", d=2)
            P.tt("dve", gv, av[:, :, 0, :], dtb[:, :].rearrange("p (d h) -> p d h", d=2), ALU.add, [a_, dtb], [g_])
            P.act(g_[:, 0:G2], g_[:, 0:G2], AF.Exp, [g_], [g_])
            P.act(g_[:, 0:G2], g_[:, 0:G2], AF.Ln, [g_], [g_], bias=1.0)
            P.tt("dve", g_[:, 0:G2], g_[:, 0:G2], nea[:], ALU.mult, [g_, nea], [g_])
            P.act(g_[:, G2:2 * G2].rearrange("p (d h) -> p d h", d=2), av[:, :, 1, :], AF.Sigmoid, [a_], [g_])
            P.dma("pool", self.GB[i * 128:(i + 1) * 128, :], g_[:], reads=[g_], writes=[self.GB])


def _interleave(gens):
    gens = list(gens)
    while gens:
        for g in list(gens):
            try:
                next(g)
            except StopIteration:
                gens.remove(g)


def _stage_dn_main(self, l):
    P = self.P
    ident = self.cst("ident")
    v3 = lambda t: t[:, 0:NH * 128].rearrange("p (h n) -> p h n", h=NH)
    bc = lambda ap, n=128: ap.unsqueeze(2).to_broadcast([128, NH, n])
    with P.scope():
        B = []
        for d in range(2):
            b = {}
            for nm in ("pA", "pB", "pC", "pD"):
                b[nm] = P.ps("d" + nm, [128, 512])
            for nm in ("S", "qT", "kT", "ktok", "vtok", "Gall", "Dm", "QKm", "WT", "kdec", "vnew", "o1", "O"):
                b[nm] = P.sb("d" + nm, [128, NH, 128])
            b["gb"] = P.sb("dgb", [128, 4 * NH]); b["sm"] = P.sb("dsm", [128, 6, NH])
            b["MT"] = [P.sb("dMT", [128, NH, 128]) for _ in range(2)]; b["MA"] = [P.sb("dMA", [128, NH, 128]) for _ in range(2)]
            b["r"] = P.sb("dr", [128, NH, 256])
            b["OD"] = self.ODt[d]
            B.append(b)

        def tile_gen(d, i, b):
            pA, pB, pC, pD = b["pA"], b["pB"], b["pC"], b["pD"]
            S, qT, kT, ktok, vtok, Gall, Dm, QKm = b["S"], b["qT"], b["kT"], b["ktok"], b["vtok"], b["Gall"], b["Dm"], b["QKm"]
            WT, kdec, vnew, o1, O, gb, sm, MT, MA, r = b["WT"], b["kdec"], b["vnew"], b["o1"], b["O"], b["gb"], b["sm"], b["MT"], b["MA"], b["r"]
            tri = self.cst("tri_f" if d == 0 else "tri_b"); ms = self.cst("ms_f" if d == 0 else "ms_b"); mi = self.cst("mi_f" if d == 0 else "mi_b")
            halves = (slice(0, 64), slice(64, 128)) if d == 0 else (slice(64, 128), slice(0, 64))
            q0, q1 = ("sp", "pool") if d == 0 else ("pool", "sp")
            cols = slice(i * 128, (i + 1) * 128)
            P.dma(q0, qT[:], self.QT[:, cols].rearrange("(h p) n -> p h n", p=128), reads=[self.QT], writes=[qT])
            P.dma(q1, kT[:], self.KT[:, cols].rearrange("(h p) n -> p h n", p=128), reads=[self.KT], writes=[kT])
            P.dma(q0, ktok[:].rearrange("p h n -> p (h n)"), self.KTOK[cols, :], reads=[self.KTOK], writes=[ktok])
            P.dma(q1, vtok[:].rearrange("p h n -> p (h n)"), self.VTOK[cols, :], reads=[self.VTOK], writes=[vtok])
            P.dma(q0, gb[:], self.GB[cols, :], reads=[self.GB], writes=[gb])
            yield
            g = gb[:, d * NH:d * NH + NH]; beta = gb[:, 2 * NH + d * NH:2 * NH + d * NH + NH]
            P.mm(pA[:, 0:NH], tri, g, True, True, [self.C, gb], [pA])
            P.mm(pA[:, NH:2 * NH], self.cst("blk_d"), g, True, True, [self.C, gb], [pA])
            P.mm(pA[:, 2 * NH:3 * NH], self.cst("blk_a"), g, True, True, [self.C, gb], [pA])
            P.mm(pA[:, 3 * NH:4 * NH], self.cst("blk_b"), g, True, True, [self.C, gb], [pA])
            P.op("dve", lambda e: e.tensor_copy(out=Gall[:], in_=bc(g)), reads=[gb], writes=[Gall])
            yield
            P.act(sm[:, 0:4, :].rearrange("p a h -> p (a h)"), pA[:, 0:4 * NH], AF.Copy, [pA], [sm])
            for h in range(NH):
                P.mm(pB[:, h * 128:(h + 1) * 128], Gall[:, h, :], tri, True, True, [Gall, self.C], [pB])
            for h in range(NH):
                P.mm(pC[:, h * 128:(h + 1) * 128], kT[:, h, :], kT[:, h, :], True, True, [kT], [pC])
                P.mm(pD[:, h * 128:(h + 1) * 128], kT[:, h, :], qT[:, h, :], True, True, [kT, qT], [pD])
            yield
            P.act(sm[:, 4, :], sm[:, 0, :], AF.Exp, [sm], [sm])
            P.tt("dve", sm[:, 5, :], sm[:, 1, :], sm[:, 0, :], ALU.subtract, [sm], [sm])
            yield
            P.act(sm[:, 5, :], sm[:, 5, :], AF.Exp, [sm], [sm])
            P.act(sm[:, 2:4, :], sm[:, 2:4, :], AF.Exp, [sm], [sm])
            gc = sm[:, 0, :]; egc = sm[:, 4, :]; ekd = sm[:, 5, :]
            P.tt("dve", Dm[:], v3(pB), bc(gc), ALU.subtract, [pB, sm], [Dm])
            yield
            P.ts("dve", Dm[:], Dm[:], 0.0, None, ALU.min, None, [Dm], [Dm])
            yield
            P.act(Dm[:], Dm[:], AF.Exp, [Dm], [Dm])
            yield
            msb = ms.unsqueeze(1).to_broadcast([128, NH, 128]); mib = mi.unsqueeze(1).to_broadcast([128, NH, 128])
            P.tt("dve", MT[0][:], v3(pC), Dm[:], ALU.mult, [pC, Dm], [MT[0]])
            P.tt("dve", QKm[:], v3(pD), Dm[:], ALU.mult, [pD, Dm], [QKm])
            yield
            P.tt("pool", MT[0][:], MT[0][:], msb, ALU.mult, [MT[0], self.C], [MT[0]])
            P.tt("pool", QKm[:], QKm[:], mib, ALU.mult, [QKm, self.C], [QKm])
            P.op("pool", lambda e: e.tensor_copy(out=r[:, :, 0:128], in_=vtok[:]), reads=[vtok], writes=[r])
            yield
            P.tt("dve", MT[0][:], MT[0][:], bc(beta), ALU.mult, [MT[0], gb], [MT[0]])
            P.tt("dve", r[:, :, 128:256], ktok[:], bc(egc), ALU.mult, [ktok, sm], [r])
            P.tt("pool", kdec[:], ktok[:], bc(ekd), ALU.mult, [ktok, sm], [kdec])
            yield
            for h in range(NH):
                P.mm(pC[:, h * 128:(h + 1) * 128], MT[0][:, h, :], ident, True, True, [MT[0], self.C], [pC])
            yield
            P.act(MA[0][:], v3(pC), AF.Copy, [pC], [MA[0]])
            yield
            cur = 0
            for k in range(6):
                mt = MT[cur]; ma = MA[cur]
                for h in range(NH):
                    P.mm(pA[:, h * 256:(h + 1) * 256], mt[:, h, :], r[:, h, :], True, True, [mt, r], [pA])
                if k < 5:
                    nx = 1 - cur
                    for h in range(NH):
                        P.mm(pC[:, h * 128:(h + 1) * 128], ma[:, h, :], mt[:, h, :], True, True, [ma, mt], [pC])
                        P.mm(pD[:, h * 128:(h + 1) * 128], mt[:, h, :], ma[:, h, :], True, True, [ma, mt], [pD])
                yield
                P.tt("dve", r[:], r[:], pA[:, 0:NH * 256].rearrange("p (h n) -> p h n", h=NH),
                     ALU.subtract if k == 0 else ALU.add, [r, pA], [r])
                if k < 5:
                    P.act(MT[nx][:], v3(pC), AF.Copy, [pC], [MT[nx]])
                    P.act(MA[nx][:], v3(pD), AF.Copy, [pD], [MA[nx]])
                    cur = nx
                yield
            P.tt("dve", r[:], r[:], bc(beta, 256), ALU.mult, [r, gb], [r])
            yield
            for h in range(NH):
                P.mm(pC[:, h * 128:(h + 1) * 128], r[:, h, 128:256], ident, True, True, [r, self.C], [pC])
            yield
            P.act(WT[:], v3(pC), AF.Copy, [pC], [WT])
            yield
            for hi, rows in enumerate(halves):
                egl = sm[:, 2 if rows.start == 0 else 3, :]
                for h in range(NH):
                    P.mm(pA[rows, h * 128:(h + 1) * 128], WT[:, h, rows], S[:, h, :], True, True, [WT, S], [pA])
                    P.mm(pB[rows, h * 128:(h + 1) * 128], qT[:, h, rows], S[:, h, :], True, True, [qT, S], [pB])
                yield
                P.tt("dve", vnew[rows], r[rows, :, 0:128], v3(pA)[rows], ALU.subtract, [r, pA], [vnew])
                P.tt("pool", S[:], S[:], bc(egl), ALU.mult, [S, sm], [S])
                yield
                P.tt("dve", o1[rows], v3(pB)[rows], egc[rows].unsqueeze(2).to_broadcast([64, NH, 128]), ALU.mult, [pB, sm], [o1])
                for h in range(NH):
                    P.mm(pC[rows, h * 128:(h + 1) * 128], QKm[rows, h, rows], vnew[rows, h, :], True, True, [QKm, vnew], [pC])
                    P.mm(pD[:, h * 128:(h + 1) * 128], kdec[rows, h, :], vnew[rows, h, :], True, True, [kdec, vnew], [pD])
                yield
                P.tt("dve", O[rows], o1[rows], v3(pC)[rows], ALU.add, [o1, pC], [O])
                P.tt("dve", S[:], S[:], v3(pD), ALU.add, [S, pD], [S])
                yield
            P.dma(q0, self.OD[d][cols, :], O[:].rearrange("p h n -> p (h n)"), reads=[O], writes=[b["OD"]])
            yield

        def chain(d):
            b = B[d]
            order = list(range(NT)) if d == 0 else [1, 0] + list(range(NT - 1, 1, -1))
            P.op("pool", lambda e: e.memset(b["S"][:], 0.0), writes=[b["S"]])
            for i in order:
                yield from tile_gen(d, i, b)

        _interleave([chain(0), chain(1)])


def _stage_dn_fin(self, l):
    P = self.P
    with P.scope():
        nw = P.sb("dnw", [128, 128])
        self.bcast_load("sp", nw, self.norm_w[l:l + 1, :], self.norm_w)
        oa = [P.sb("foa", [128, NH, 128]) for _ in range(2)]; ob = [P.sb("fob", [128, NH, 128]) for _ in range(2)]
        zt = [P.sb("fzt", [128, NH, 128]) for _ in range(2)]; sq = P.sb("fsq", [128, NH, 128]); ss = P.sb("fss", [128, NH])
        for i in range(NT):
            a = oa[i % 2]; b = ob[i % 2]; z = zt[i % 2]
            cols = slice(i * 128, (i + 1) * 128)
            P.dma("sp", a[:].rearrange("p h n -> p (h n)"), self.OD[0][cols, :], reads=[self.ODt[0]], writes=[a])
            P.dma("pool", b[:].rearrange("p h n -> p (h n)"), self.OD[1][cols, :], reads=[self.ODt[1]], writes=[b])
            for ps_, ap in dn_tile_rows(self.Z, i):
                P.dma("sp", z[ps_].rearrange("p h n -> p (h n)"), ap, reads=[self.Z], writes=[z])
            P.tt("dve", a[:], a[:], b[:], ALU.add, [a, b], [a])
            P.tt("pool", sq[:], a[:], a[:], ALU.mult, [a], [sq])
            P.op("dve", lambda e: e.tensor_reduce(out=ss[:], in_=sq[:], axis=AX.X, op=ALU.add), reads=[sq], writes=[ss])
            P.act(ss[:], ss[:], AF.Sqrt, [ss], [ss], scale=1.0 / 128.0, bias=1e-6)
            P.op("dve", lambda e: e.reciprocal(out=ss[:], in_=ss[:]), reads=[ss], writes=[ss])
            P.tt("dve", a[:], a[:], ss[:, :].unsqueeze(2).to_broadcast([128, NH, 128]), ALU.mult, [a, ss], [a])
            P.tt("pool", a[:], a[:], nw[:, :].unsqueeze(1).to_broadcast([128, NH, 128]), ALU.mult, [a, nw], [a])
            P.act(b[:], z[:], AF.Silu, [z], [b])
            P.tt("dve", a[:], a[:], b[:], ALU.mult, [a, b], [a])
            for ps_, ap in dn_tile_rows(self.DNO, i):
                P.dma("pool", ap, a[ps_].rearrange("p h n -> p (h n)"), reads=[a], writes=[self.DNO])
        for c in range(5):
            rc = 1024 if c < 4 else 256
            P.coll("AllGather", ALU.bypass, RG, self.DNO[c * 1024:c * 1024 + rc, :], self.DNOF[R * c * 1024:R * c * 1024 + R * rc, :],
                   [self.DNO], [self.DNOF])


def _stage_dn(self, l):
    _stage_dn_prep(self, l)
    _stage_dn_main(self, l)
    _stage_dn_fin(self, l)


Net.stage_dn = _stage_dn


def _ln_tile(self, t, lng, lnb, tmp, st):
    P = self.P
    P.op("dve", lambda e: e.tensor_reduce(out=st[:, 0:1], in_=t[:], axis=AX.X, op=ALU.add), reads=[t], writes=[st])
    P.ts("dve", st[:, 1:2], st[:, 0:1], -1.0 / D, None, ALU.mult, None, [st], [st])
    P.ts("dve", t[:], t[:], st[:, 1:2], None, ALU.add, None, [t, st], [t])
    P.act(tmp[:], t[:], AF.Square, [t], [tmp])
    P.op("dve", lambda e: e.tensor_reduce(out=st[:, 2:3], in_=tmp[:], axis=AX.X, op=ALU.add), reads=[tmp], writes=[st])
    P.act(st[:, 3:4], st[:, 2:3], AF.Sqrt, [st], [st], scale=1.0 / D, bias=1e-5)
    P.op("dve", lambda e: e.reciprocal(out=st[:, 3:4], in_=st[:, 3:4]), reads=[st], writes=[st])
    P.stt("dve", t[:], t[:], st[:, 3:4], lng[:], ALU.mult, ALU.mult, [t, st, lng], [t])
    P.tt("pool", t[:], t[:], lnb[:], ALU.add, [t, lnb], [t])


def _stage_glu(self, l):
    P = self.P
    ident = self.cst("ident")
    with P.scope():
        ps = [P.ps("gps", [128, 512]) for _ in range(3)]
        gw = P.sb("ggw", [128, 8, UC], BF16); stg = P.sb("gstg", [128, NTOK])
        for kc in range(8):
            P.dma("sp", stg[:, 0:UC], self.glu_w[l][kc * 128:(kc + 1) * 128, :], reads=[self.glu_w], writes=[stg])
            P.act(gw[:, kc, :], stg[:, 0:UC], AF.Copy, [stg], [gw])
        NOC = UC // 128
        gbl = P.sb("ggbl", [NOC, 128]); glub = P.sb("gglub", [128, NOC])
        P.dma("sp", gbl[:], self.glu_b[l:l + 1, :].rearrange("o (k p) -> (o k) p", p=128), reads=[self.glu_b], writes=[gbl])
        P.mm(ps[0][:, 0:NOC], gbl[:], ident[0:NOC, 0:NOC], True, True, [gbl, self.C], [ps[0]])
        P.act(glub[:], ps[0][:, 0:NOC], AF.Copy, [ps[0]], [glub])
        Gb = P.sb("gGb", [128, 8, NTOK], BF16)
        for kc in range(8):
            rr_ = kc // (UC // 128); lb = kc % (UC // 128)
            for j in range(4):
                P.dma("sp" if j % 2 == 0 else "pool", stg[32 * j:32 * (j + 1), :], self.GTF[lb * 4 + j][rr_ * 32:(rr_ + 1) * 32, :],
                      reads=[self.GTF], writes=[stg])
            P.act(Gb[:, kc, :], stg[:], AF.Copy, [stg], [Gb])
        ob = [P.sb("gob", [128, NTOK], BF16) for _ in range(2)]; sg = [P.sb("gsg", [128, 512]) for _ in range(2)]
        for oc in range(NOC):
            o = ob[oc % 2]
            P.dma("sp", stg[:], self.GT[oc * 128:(oc + 1) * 128, :], reads=[self.GT], writes=[stg])
            for nb in range(9):
                c0 = nb * 512; n = min(512, NTOK - c0)
                p_ = ps[nb % 3]; s_ = sg[nb % 2]
                for kc in range(8):
                    P.mm(p_[:, 0:n], gw[:, kc, oc * 128:(oc + 1) * 128], Gb[:, kc, c0:c0 + n], kc == 0, kc == 7, [gw, Gb], [p_])
                P.act(s_[:, 0:n], p_[:, 0:n], AF.Sigmoid, [p_, glub], [s_], bias=glub[:, oc:oc + 1])
                P.tt("dve", o[:, c0:c0 + n], s_[:, 0:n], stg[:, c0:c0 + n], ALU.mult, [s_, stg], [o])
            P.dma("pool", self.S5T[oc * 128:(oc + 1) * 128, :], o[:], reads=[o], writes=[self.S5T])
            for hh in range(2):
                P.coll("AllGather", ALU.bypass, RG, self.S5T[oc * 128 + hh * 64:oc * 128 + (hh + 1) * 64, :], self.S5TF[oc * 2 + hh],
                       [self.S5T], [self.S5TF])


def _stage_outproj(self, l):
    P = self.P
    with P.scope():
        wo = P.sb("owo", [128, 16, DC], BF16); stg = P.sb("ostg", [128, DC])
        for kc in range(16):
            P.dma("sp" if kc % 2 == 0 else "pool", stg[:], self.w_out[l][kc * 128:(kc + 1) * 128, :], reads=[self.w_out], writes=[stg])
            P.act(wo[:, kc, :], stg[:], AF.Copy, [stg], [wo])
        s5t = [P.sb("os5t", [128, 8, 128], BF16) for _ in range(2)]
        dnt = [P.sb("odnt", [128, 1024]) for _ in range(2)]; dnb = P.sb("odnb", [128, 1024], BF16); dnT = P.sb("odnT", [128, 8, 128], BF16)
        pt = P.ps("opt", [128, 1024]); pa = [P.ps("opa", [128, 512]) for _ in range(3)]
        t = [P.sb("ot_", [128, DC]) for _ in range(2)]
        for i in range(NT):
            cols = slice(i * 128, (i + 1) * 128)
            s5 = s5t[i % 2]; dn = dnt[i % 2]; t_ = t[i % 2]
            for kc in range(8):
                rr_ = kc // (UC // 128); lb = kc % (UC // 128)
                for hh in range(2):
                    P.dma("sp" if hh == 0 else "pool", s5[hh * 64:(hh + 1) * 64, kc, :], self.S5TF[lb * 2 + hh][rr_ * 64:(rr_ + 1) * 64, cols],
                          reads=[self.S5TF], writes=[s5])
            c_ = i // 8; ii = i % 8; rc = 1024 if c_ < 4 else 256
            dnf = self.DNOF[R * c_ * 1024:R * c_ * 1024 + R * rc, :].rearrange("(r t) c -> t r c", r=R)
            P.dma("pool", dn[:, :].rearrange("p (r c) -> p r c", r=R), dnf[ii * 128:(ii + 1) * 128], reads=[self.DNOF], writes=[dn])
            P.act(dnb[:], dn[:], AF.Copy, [dn], [dnb])
            for kc in range(8):
                P.mm(pt[:, kc * 128:(kc + 1) * 128], dnb[:, kc * 128:(kc + 1) * 128], self.Cb[:, 0:128], True, True, [dnb, self.Cb], [pt])
            P.act(dnT[:].rearrange("p k n -> p (k n)"), pt[:], AF.Copy, [pt], [dnT])
            for cb in range(DC // 512):
                p_ = pa[(i + cb) % 3]
                cs_ = slice(cb * 512, (cb + 1) * 512)
                for kc in range(8):
                    P.mm(p_[:], s5[:, kc, :], wo[:, kc, cs_], kc == 0, False, [s5, wo], [p_])
                for kc in range(8):
                    P.mm(p_[:], dnT[:, kc, :], wo[:, 8 + kc, cs_], False, kc == 7, [dnT, wo], [p_])
                P.act(t_[:, cs_], p_[:], AF.Copy, [p_], [t_])
            P.dma("sp", self.MIXL[cols, :], t_[:], reads=[t_], writes=[self.MIXL])
            if i % 4 == 3 or i == NT - 1:
                c_ = i // 4; rc = 512 if c_ < 8 else 256
                P.coll("AllGather", ALU.bypass, RG, self.MIXL[c_ * 512:c_ * 512 + rc, :], self.MIXF[R * c_ * 512:R * c_ * 512 + R * rc, :],
                       [self.MIXL], [self.MIXF])
    _stage_resln(self, l, 1)


def _stage_resln(self, l, which):
    P = self.P
    with P.scope():
        gseg = 2 if which == 1 else 5
        g_ = {}
        for who, row in (("lat", 0), ("ctx", 1)):
            g_[who] = P.sb("lg", [128, D])
            self.mod_load("sp", g_[who], l, row, gseg)
        lng = P.sb("llng", [128, D]); lnb = P.sb("llnb", [128, D])
        self.bcast_load("sp", lng, (self.ln1_g if which == 1 else self.ln2_g)[l:l + 1, :], self.ln1_g if which == 1 else self.ln2_g)
        self.bcast_load("pool", lnb, (self.ln1_b if which == 1 else self.ln2_b)[l:l + 1, :], self.ln1_b if which == 1 else self.ln2_b)
        tt_ = [P.sb("lt", [128, D]) for _ in range(2)]; xx = [P.sb("lx", [128, D]) for _ in range(2)]
        tmps = [P.sb("ltmp", [128, D]) for _ in range(2)]; sts = [P.sb("lst", [128, 4]) for _ in range(2)]

        def ar(j):
            P.coll("AllReduce", ALU.add, RG, self.FFP[j * 128:(j + 1) * 128, :], self.FF[j * 128:(j + 1) * 128, :], [self.FFP], [self.FFt[j]])
        if which == 2:
            ar(0); ar(1)

        def ltile(i, par):
            cols = slice(i * 128, (i + 1) * 128)
            t = tt_[par]; x = xx[par]; tmp = tmps[par]; st = sts[par]
            q0, q1 = ("sp", "act") if par == 0 else ("act", "sp")
            if which == 2 and i + 2 < NT:
                ar(i + 2)
            if which == 1:
                c_ = i // 4; ii = i % 4; rc = 512 if c_ < 8 else 256
                mixf = self.MIXF[R * c_ * 512:R * c_ * 512 + R * rc, :].rearrange("(r t) c -> t r c", r=R)
                P.dma(q0, t[:, :].rearrange("p (r c) -> p r c", r=R), mixf[ii * 128:(ii + 1) * 128], reads=[self.MIXF], writes=[t])
            else:
                P.dma(q0, t[:], self.FF[cols, :], reads=[self.FFt[i]], writes=[t])
            P.dma(q1, x[:], self.X[cols, :], reads=[self.Xr], writes=[x])
            yield
            P.tt("dve", t[:], t[:], g_["ctx" if i < 2 else "lat"][:], ALU.mult, [t, g_["ctx"], g_["lat"]], [t])
            yield
            P.stt("dve", t[:], x[:], ALPHA, t[:], ALU.mult, ALU.add, [x, t], [t])
            yield
            P.op("dve", lambda e: e.tensor_reduce(out=st[:, 0:1], in_=t[:], axis=AX.X, op=ALU.add), reads=[t], writes=[st])
            yield
            P.ts("dve", st[:, 1:2], st[:, 0:1], -1.0 / D, None, ALU.mult, None, [st], [st])
            yield
            P.ts("dve", t[:], t[:], st[:, 1:2], None, ALU.add, None, [t, st], [t])
            yield
            P.act(tmp[:], t[:], AF.Square, [t], [tmp])
            yield
            P.op("dve", lambda e: e.tensor_reduce(out=st[:, 2:3], in_=tmp[:], axis=AX.X, op=ALU.add), reads=[tmp], writes=[st])
            yield
            P.act(st[:, 3:4], st[:, 2:3], AF.Sqrt, [st], [st], scale=1.0 / D, bias=1e-5)
            yield
            P.op("dve", lambda e: e.reciprocal(out=st[:, 3:4], in_=st[:, 3:4]), reads=[st], writes=[st])
            yield
            P.stt("dve", t[:], t[:], st[:, 3:4], lng[:], ALU.mult, ALU.mult, [t, st, lng], [t])
            yield
            P.tt("pool", t[:], t[:], lnb[:], ALU.add, [t, lnb], [t])
            yield
            P.dma(q0, self.X[cols, :], t[:], reads=[t], writes=[self.Xw[par]])
            yield

        def lchain(par):
            for i in range(par, NT, 2):
                yield from ltile(i, par)

        _interleave([lchain(0), lchain(1)])
        P.barrier()
    self.dump_dram("X%d" % which, lambda r0, n: self.X[r0:r0 + n, :], NTOK, D, self.X)


Net.stage_glu = _stage_glu
Net.stage_outproj = _stage_outproj


def _stage_moe(self, l):
    P = self.P
    ident = self.cst("ident"); lt = self.cst("lt")
    NE = 16
    with P.scope():
        modt = {}
        for who, row in (("lat", 0), ("ctx", 1)):
            scp = P.sb("mscp", [128, D]); sh = P.sb("msh", [128, D])
            self.mod_load("sp", scp, l, row, 4)
            self.mod_load("pool", sh, l, row, 3)
            modt[who] = (scp, sh)
        rw = P.sb("mrw", [128, 16, 16])
        P.dma("sp", rw[:], self.router_w[l].rearrange("(k p) e -> p k e", p=128), reads=[self.router_w], writes=[rw])
        RB_ = []
        for par in range(2):
            RB_.append({"x": P.sb("mx", [128, D]), "hb": P.sb("mhb", [128, D], BF16), "hT": P.sb("mhT", [128, 16, 128]),
                        "pt": [P.ps("mpt", [128, 512]) for _ in range(2)], "pl": P.ps("mpl", [128, 512]),
                        "af": P.sb("maf", [128, 16]), "st": P.sb("mst", [128, 4]), "aft": P.sb("maft", [16, 128])})

        def rtile(i, b):
            cols = slice(i * 128, (i + 1) * 128)
            x_, h_, hT, pt, pl, af, st, aft = b["x"], b["hb"], b["hT"], b["pt"], b["pl"], b["af"], b["st"], b["aft"]
            scp, sh = modt["ctx" if i < 2 else "lat"]
            q0, q1 = ("sp", "pool") if i % 2 == 0 else ("pool", "sp")
            P.dma(q0, x_[:], self.X[cols, :], reads=[self.X], writes=[x_])
            yield
            P.tt("dve", x_[:], x_[:], scp[:], ALU.mult, [x_, scp], [x_])
            yield
            P.tt("dve", x_[:], x_[:], sh[:], ALU.add, [x_, sh], [x_])
            yield
            P.act(h_[:], x_[:], AF.Copy, [x_], [h_])
            for kg in range(4):
                p_ = pt[kg % 2]
                for kk in range(4):
                    kc = kg * 4 + kk
                    P.mm(p_[:, kk * 128:(kk + 1) * 128], x_[:, kc * 128:(kc + 1) * 128], ident, True, True, [x_, self.C], [p_])
                yield
                P.act(hT[:, kg * 4:(kg + 1) * 4, :].rearrange("p k n -> p (k n)"), p_[:], AF.Copy, [p_], [hT])
            P.dma(q1, self.H2[cols, :], h_[:], reads=[h_], writes=[self.H2t[i]])
            yield
            for kc in range(16):
                P.mm(pl[:, 0:16], hT[:, kc, :], rw[:, kc, :], kc == 0, kc == 15, [hT, rw], [pl])
            yield
            P.op("dve", lambda e: e.tensor_reduce(out=st[:, 0:1], in_=pl[:, 0:16], axis=AX.X, op=ALU.max), reads=[pl], writes=[st])
            yield
            P.ts("dve", st[:, 1:2], st[:, 0:1], -1.0, None, ALU.mult, None, [st], [st])
            yield
            P.act(af[:], pl[:, 0:16], AF.Exp, [pl, st], [af], bias=st[:, 1:2])
            yield
            P.op("dve", lambda e: e.tensor_reduce(out=st[:, 2:3], in_=af[:], axis=AX.X, op=ALU.add), reads=[af], writes=[st])
            yield
            P.op("dve", lambda e: e.reciprocal(out=st[:, 3:4], in_=st[:, 2:3]), reads=[st], writes=[st])
            yield
            P.ts("dve", af[:], af[:], st[:, 3:4], None, ALU.mult, None, [af, st], [af])
            yield
            P.dma(q0, self.AFF[cols, :], af[:], reads=[af], writes=[self.AFFw[i % 2]])
            P.mm(pl[0:16, 128:256], af[:], ident, True, True, [af, self.C], [pl])
            yield
            P.act(aft[:], pl[0:16, 128:256], AF.Copy, [pl], [aft])
            yield
            P.dma(q1, self.AFFT[:, cols], aft[:], reads=[aft], writes=[self.AFFTw[i % 2]])
            yield

        def rchain(par):
            for i in range(par, NT, 2):
                yield from rtile(i, RB_[par])

        _interleave([rchain(0), rchain(1)])
        P.barrier()
    with P.scope():
        affall = P.sb("raff", [128, NT, 16]); rank = P.sb("rrank", [128, NT, 16])
        P.dma("sp", affall[:], self.AFF[:, :].rearrange("(t p) e -> p t e", p=128), reads=[self.AFF], writes=[affall])
        arow = [P.sb("rarow", [128, NLAT]) for _ in range(2)]; junk = P.sb("rjunk", [128, NLAT]); j2 = P.sb("rj2", [128, 128])
        c4 = [P.sb("rc4", [128, 4]) for _ in range(2)]
        pl = P.ps("rpl", [128, 512]); rkt = P.sb("rrkt", [16, 128])
        P.op("pool", lambda g: g.memset(rank[:], 0.0), writes=[rank])
        for tiles, col0 in (([0, 1], 0), (list(range(2, NT)), NCTX)):
            n = len(tiles) * 128
            for e in range(NEL):
                ar = arow[e % 2]
                P.dma("sp" if e % 2 == 0 else "pool", ar[:, 0:n], self.AFFT[e:e + 1, col0:col0 + n].partition_broadcast(128),
                      reads=[self.AFFT], writes=[ar])
                for tl, t in enumerate(tiles):
                    sc = affall[:, t, e:e + 1]
                    c = c4[tl % 2]
                    P.op("pool", lambda g, c=c: g.memset(c[:], 0.0), writes=[c])
                    b0 = tl * 128
                    if tl > 0:
                        P.ts("dve", junk[:, 0:b0], ar[:, 0:b0], sc, 0.0, ALU.is_ge, ALU.add, [ar, affall], [c], accum=c[:, 0:1])
                    if b0 + 128 < n:
                        P.ts("dve", junk[:, b0 + 128:n], ar[:, b0 + 128:n], sc, 0.0, ALU.is_gt, ALU.add, [ar, affall], [c], accum=c[:, 1:2])
                    P.ts("dve", junk[:, b0:b0 + 128], ar[:, b0:b0 + 128], sc, 0.0, ALU.is_gt, ALU.add, [ar, affall], [c], accum=c[:, 2:3])
                    P.stt("dve", j2[:], ar[:, b0:b0 + 128], sc, lt, ALU.is_equal, ALU.mult, [ar, affall, self.C], [j2])
                    P.op("dve", lambda g, c=c: g.tensor_reduce(out=c[:, 3:4], in_=j2[:], axis=AX.X, op=ALU.add), reads=[j2], writes=[c])
                    P.op("dve", lambda g, c=c, t=t, e=e: g.tensor_reduce(out=rank[:, t, e:e + 1], in_=c[:], axis=AX.X, op=ALU.add), reads=[c], writes=[rank])
        P.dma("sp", self.RANKD[:, :].rearrange("(t p) e -> p t e", p=128), rank[:], reads=[rank], writes=[self.RANKD])
        for t in range(NT):
            P.mm(pl[0:16, 0:128], rank[:, t, :], ident, True, True, [rank, self.C], [pl])
            P.act(rkt[:], pl[0:16, 0:128], AF.Copy, [pl], [rkt])
            P.dma("sp", self.RANKT[:, t * 128:(t + 1) * 128], rkt[:], reads=[rkt], writes=[self.RANKT])
    with P.scope():
        rk = P.sb("erk", [128, NT, 16])
        P.dma("sp", rk[:], self.RANKD[:, :].rearrange("(t p) e -> p t e", p=128), reads=[self.RANKD], writes=[rk])
        perm = P.sb("eperm", [128, 4, 512], BF16); permc = P.sb("epermc", [128, 2, 32], BF16)
        w1b = P.sb("ew1", [128, 16, 1024], BF16); w3b = P.sb("ew3", [128, 16, 1024], BF16); w2b = P.sb("ew2", [128, 8, D], BF16)
        stg = [P.sb("estg", [128, D]) for _ in range(2)]
        HsT = P.sb("eHsT", [128, 16, 544], BF16); hidT = P.sb("ehid", [128, 8, 544], BF16)
        h2t = [P.sb("eh2t", [128, 1024], BF16) for _ in range(3)]
        rrow = P.sb("errow", [128, NLAT]); arow = P.sb("earow", [128, NLAT]); g2_ = P.sb("eg2", [128, 2])
        gate = P.sb("egate", [128, 8]); sa = P.sb("esa", [128, 512]); yg = [P.sb("eyg", [128, D], BF16) for _ in range(2)]
        pg = [P.ps("epg", [128, 512]) for _ in range(8)]
        for e in range(NEL):
            ld = 0
            for src, dst, nk, w in ((self.w1, w1b, 16, 1024), (self.w3, w3b, 16, 1024), (self.w2, w2b, 8, D)):
                for kc in range(nk):
                    s_ = stg[ld % 2]
                    P.dma("sp" if ld % 2 == 0 else "pool", s_[:, 0:w], src[l][e][kc * 128:(kc + 1) * 128, :], reads=[src], writes=[s_])
                    if ld % 2 == 0:
                        P.act(dst[:, kc, :], s_[:, 0:w], AF.Copy, [s_], [dst])
                    else:
                        P.op("dve", lambda g, dst=dst, kc=kc, s_=s_, w=w: g.tensor_copy(out=dst[:, kc, :], in_=s_[:, 0:w]), reads=[s_], writes=[dst])
                    ld += 1
            for t in range(2):
                P.ts("dve", permc[:, t, :], self.iota[:, 0:32], rk[:, t, e:e + 1], None, ALU.is_equal, None, [self.iota, rk], [permc])
            P.dma("sp", rrow[:], self.RANKT[e:e + 1, NCTX:NTOK].partition_broadcast(128), reads=[self.RANKT], writes=[rrow])
            P.dma("pool", arow[:], self.AFFT[e:e + 1, NCTX:NTOK].partition_broadcast(128), reads=[self.AFFT], writes=[arow])
            for sbk in range(4):
                for hf in range(2):
                    hs = slice(hf * 2048, (hf + 1) * 2048)
                    P.stt("dve", stg[hf][:], rrow[:, hs], self.iotc[:, sbk:sbk + 1], arow[:, hs], ALU.is_equal, ALU.mult, [rrow, arow, self.iotc], [stg[hf]])
                    P.op("dve", lambda g, hf=hf: g.tensor_reduce(out=g2_[:, hf:hf + 1], in_=stg[hf][:], axis=AX.X, op=ALU.add), reads=[stg[hf]], writes=[g2_])
                P.tt("dve", gate[:, sbk:sbk + 1], g2_[:, 0:1], g2_[:, 1:2], ALU.add, [g2_], [gate])
            P.dma("sp", rrow[:, 0:NCTX], self.RANKT[e:e + 1, 0:NCTX].partition_broadcast(128), reads=[self.RANKT], writes=[rrow])
            P.dma("pool", arow[:, 0:NCTX], self.AFFT[e:e + 1, 0:NCTX].partition_broadcast(128), reads=[self.AFFT], writes=[arow])
            P.stt("dve", stg[0][:, 0:NCTX], rrow[:, 0:NCTX], self.iotc[:, 0:1], arow[:, 0:NCTX], ALU.is_equal, ALU.mult, [rrow, arow, self.iotc], [stg[0]])
            P.op("dve", lambda g: g.tensor_reduce(out=gate[:, 4:5], in_=stg[0][:, 0:NCTX], axis=AX.X, op=ALU.add), reads=[stg[0]], writes=[gate])
            for kg in range(2):
                for t in range(32):
                    h_ = h2t[t % 3]
                    P.dma("sp" if t % 2 == 0 else "pool", h_[:], self.H2[NCTX + t * 128:NCTX + (t + 1) * 128, kg * 1024:(kg + 1) * 1024],
                          reads=[self.H2], writes=[h_])
                    pm_ = perm[:, t % 4, :]
                    P.ts("dve" if t % 2 == 0 else "pool", pm_, self.iota[:, 0:512], rk[:, 2 + t, e:e + 1], None, ALU.is_equal, None,
                         [self.iota, rk], [perm])
                    for kk in range(8):
                        P.mm(pg[kk][:], h_[:, kk * 128:(kk + 1) * 128], pm_, t == 0, t == 31, [h_, perm], [pg[kk]])
                for kk in range(8):
                    P.act(HsT[:, kg * 8 + kk, 0:512], pg[kk][:], AF.Copy, [pg[kk]], [HsT])
                for t in range(2):
                    h_ = h2t[t % 3]
                    P.dma("sp", h_[:], self.H2[t * 128:(t + 1) * 128, kg * 1024:(kg + 1) * 1024], reads=[self.H2], writes=[h_])
                    for kk in range(8):
                        P.mm(pg[kk][:, 0:32], h_[:, kk * 128:(kk + 1) * 128], permc[:, t, :], t == 0, t == 1, [h_, permc], [pg[kk]])
                for kk in range(8):
                    P.act(HsT[:, kg * 8 + kk, 512:544], pg[kk][:, 0:32], AF.Copy, [pg[kk]], [HsT])
            q = 0
            for fc in range(8):
                for s0, n in ((0, 512), (512, 32)):
                    p1 = pg[(2 * q) % 8]; p3 = pg[(2 * q + 1) % 8]; q += 1
                    for kc in range(16):
                        P.mm(p1[:, 0:n], w1b[:, kc, fc * 128:(fc + 1) * 128], HsT[:, kc, s0:s0 + n], kc == 0, kc == 15, [w1b, HsT], [p1])
                    for kc in range(16):
                        P.mm(p3[:, 0:n], w3b[:, kc, fc * 128:(fc + 1) * 128], HsT[:, kc, s0:s0 + n], kc == 0, kc == 15, [w3b, HsT], [p3])
                    P.act(sa[:, 0:n], p1[:, 0:n], AF.Silu, [p1], [sa])
                    P.tt("dve", hidT[:, fc, s0:s0 + n], sa[:, 0:n], p3[:, 0:n], ALU.mult, [sa, p3], [hidT])
            q = 0
            for sbk in range(5):
                nr = 128 if sbk < 4 else 32
                s0 = sbk * 128
                y_ = yg[sbk % 2]
                for cb in range(4):
                    p_ = pg[q % 8]; q += 1
                    for fc in range(8):
                        P.mm(p_[0:nr, :], hidT[:, fc, s0:s0 + nr], w2b[:, fc, cb * 512:(cb + 1) * 512], fc == 0, fc == 7, [hidT, w2b], [p_])
                    P.ts("dve", y_[0:nr, cb * 512:(cb + 1) * 512], p_[0:nr, :], gate[0:nr, sbk:sbk + 1], None, ALU.mult, None, [p_, gate], [y_])
                P.dma("sp", self.YG[e][s0:s0 + nr, :], y_[0:nr, :], reads=[y_], writes=[self.YG])
    with P.scope():
        ygs = P.sb("cygs", [128, 4, NEL, 4, 512], BF16); ygc = P.sb("cygc", [32, 4, NEL, 512], BF16)
        rr = [P.sb("crr", [128, NEL, 128]) for _ in range(2)]; pT = [P.sb("cpT", [128, 4, NEL, 128], BF16) for _ in range(2)]
        pf = [P.ps("cpf", [128, 512]) for _ in range(4)]; fo = [P.sb("cfo", [128, D]) for _ in range(2)]
        for cb in range(4):
            cs_ = slice(cb * 512, (cb + 1) * 512)
            for e in range(NEL):
                P.dma("sp" if e % 2 == 0 else "pool", ygs[:, cb, e], self.YG[e][0:512, cs_].rearrange("(s p) n -> p s n", p=128), reads=[self.YG], writes=[ygs])
                P.dma("sp", ygc[:, cb, e, :], self.YG[e][512:544, cs_], reads=[self.YG], writes=[ygc])
        for t in range(NT):
            cols = slice(t * 128, (t + 1) * 128)
            r_ = rr[t % 2]; p_ = pT[t % 2]; o_ = fo[t % 2]
            P.dma("sp" if t % 2 == 0 else "pool", r_[:], self.RANKT[0:NEL, cols].partition_broadcast(128), reads=[self.RANKT], writes=[r_])
            if t >= 2:
                for sbk in range(4):
                    P.ts("dve" if sbk % 2 == 0 else "pool", p_[:, sbk], r_[:], self.iotc[:, sbk:sbk + 1], None, ALU.is_equal, None, [r_, self.iotc], [p_])
            else:
                P.ts("dve", p_[:, 0], r_[:], self.iotc[:, 0:1], None, ALU.is_equal, None, [r_, self.iotc], [p_])
            for cb in range(4):
                f_ = pf[cb]
                if t >= 2:
                    k = 0
                    for e in range(NEL):
                        for sbk in range(4):
                            P.mm(f_[:], p_[:, sbk, e, :], ygs[:, cb, e, sbk, :], k == 0, k == 4 * NEL - 1, [p_, ygs], [f_])
                            k += 1
                else:
                    for e in range(NEL):
                        P.mm(f_[:], p_[0:32, 0, e, :], ygc[0:32, cb, e, :], e == 0, e == NEL - 1, [p_, ygc], [f_])
                if cb % 2 == 0:
                    P.act(o_[:, cb * 512:(cb + 1) * 512], f_[:], AF.Copy, [f_], [o_])
                else:
                    P.op("dve", lambda g, o_=o_, f_=f_, cb=cb: g.tensor_copy(out=o_[:, cb * 512:(cb + 1) * 512], in_=f_[:]), reads=[f_], writes=[o_])
            P.dma("sp" if t % 2 == 0 else "pool", self.FFP[cols, :], o_[:], reads=[o_], writes=[self.FFP])
    _stage_resln(self, l, 2)


Net.stage_moe = _stage_moe
```
